# Optimizing a Trainium2 kernel written in Bass

```python
import math
import jax, jax.numpy as jnp
from jax import lax
import numpy as np

D_MODEL = 1024
BATCH = 8
SEQ = 4096
DEPTH = 1

N_META = 16
MIX_WIDTH = D_MODEL
S5_GROUP_CH = 16
S5_STATE = 64
S5_WIDTH = MIX_WIDTH // 4
S5_GROUPS = S5_WIDTH // S5_GROUP_CH
RET_HEAD_DIM = 128
RET_WIDTH = MIX_WIDTH - S5_WIDTH
RET_HEADS = RET_WIDTH // RET_HEAD_DIM
CHUNK = 128
ROPE_BASE = 10000.0
D_FF = 4 * D_MODEL
LN_EPS = 1e-5
GN_EPS = 1e-5
IN_PROJ_WIDTH = S5_WIDTH + 4 * RET_WIDTH
DEEPNORM_ALPHA = (2.0 * DEPTH) ** 0.25
DEEPNORM_BETA = (8.0 * DEPTH) ** -0.25

kernel_name = "hymba_s5_retnet_deepnorm_layer"


def layer_norm(x, g, b):
    xf = x.astype(jnp.float32)
    mu = jnp.mean(xf, axis=-1, keepdims=True)
    xc = xf - mu
    var = jnp.mean(xc * xc, axis=-1, keepdims=True)
    y = xc * lax.rsqrt(var + LN_EPS)
    return (y * g.astype(jnp.float32) + b.astype(jnp.float32)).astype(x.dtype)


def _complex_linear_combine(e1, e2):
    a1r, a1i, b1r, b1i = e1
    a2r, a2i, b2r, b2i = e2
    ar = a2r * a1r - a2i * a1i
    ai = a2r * a1i + a2i * a1r
    br = a2r * b1r - a2i * b1i + b2r
    bi = a2r * b1i + a2i * b1r + b2i
    return (ar, ai, br, bi)


def s5_mixer(u, lam_re, lam_im, log_dt, b_re, b_im, c_re, c_im, d, w_glu, b_glu):
    bsz, L, _ = u.shape
    ug = u.reshape(bsz, L, S5_GROUPS, S5_GROUP_CH)
    dt = jnp.exp(log_dt)[:, None]
    mag = jnp.exp(lam_re * dt)
    lbr = mag * jnp.cos(lam_im * dt)
    lbi = mag * jnp.sin(lam_im * dt)
    den = lam_re * lam_re + lam_im * lam_im
    nr = lbr - 1.0
    qr = (nr * lam_re + lbi * lam_im) / den
    qi = (lbi * lam_re - nr * lam_im) / den
    bbr = qr[..., None] * b_re - qi[..., None] * b_im
    bbi = qr[..., None] * b_im + qi[..., None] * b_re
    bur = jnp.einsum('blgh,gph->blgp', ug, bbr)
    bui = jnp.einsum('blgh,gph->blgp', ug, bbi)
    ar = jnp.broadcast_to(lbr[None, None], (1, L, S5_GROUPS, S5_STATE))
    ai = jnp.broadcast_to(lbi[None, None], (1, L, S5_GROUPS, S5_STATE))
    _, _, xr, xi = lax.associative_scan(_complex_linear_combine, (ar, ai, bur, bui), axis=1)
    y = jnp.einsum('blgp,ghp->blgh', xr, c_re) - jnp.einsum('blgp,ghp->blgh', xi, c_im)
    y = y.reshape(bsz, L, S5_WIDTH) + d * u
    y = jax.nn.gelu(y)
    return y * jax.nn.sigmoid(y @ w_glu + b_glu)


def _rotate(x, cos, sin):
    half = x.shape[-1] // 2
    x1, x2 = x[..., :half], x[..., half:]
    return jnp.concatenate([x1 * cos - x2 * sin, x1 * sin + x2 * cos], axis=-1)


def retention_mixer(q, k, v, g, gn_g, gn_b):
    bsz, L, _ = q.shape
    dtype = q.dtype
    q = q.reshape(bsz, L, RET_HEADS, RET_HEAD_DIM)
    k = k.reshape(bsz, L, RET_HEADS, RET_HEAD_DIM)
    v = v.reshape(bsz, L, RET_HEADS, RET_HEAD_DIM)
    pos = jnp.arange(L, dtype=jnp.float32)
    inv_freq = 1.0 / (ROPE_BASE ** (jnp.arange(0, RET_HEAD_DIM, 2, dtype=jnp.float32) / RET_HEAD_DIM))
    ang = pos[:, None] * inv_freq[None, :]
    cos = jnp.cos(ang)[None, :, None, :].astype(dtype)
    sin = jnp.sin(ang)[None, :, None, :].astype(dtype)
    q = _rotate(q, cos, sin)
    k = _rotate(k, cos, sin) * (RET_HEAD_DIM ** -0.5)

    pad = CHUNK - N_META
    n_chunks = (L + pad) // CHUNK

    def to_chunks(t):
        t = jnp.pad(t, ((0, 0), (pad, 0), (0, 0), (0, 0)))
        return t.reshape(bsz, n_chunks, CHUNK, RET_HEADS, RET_HEAD_DIM).transpose(1, 0, 3, 2, 4)

    qc, kc, vc = to_chunks(q), to_chunks(k), to_chunks(v)

    log_gamma = jnp.log1p(-jnp.exp2(-5.0 - jnp.arange(RET_HEADS, dtype=jnp.float32)))
    idx = jnp.arange(CHUNK, dtype=jnp.float32)
    diff = idx[:, None] - idx[None, :]
    dmat = jnp.where(diff[None] >= 0,
                     jnp.exp(jnp.maximum(diff, 0.0)[None] * log_gamma[:, None, None]),
                     0.0).astype(dtype)
    zeta = jnp.exp((CHUNK - 1.0 - idx)[None] * log_gamma[:, None]).astype(dtype)
    xi = jnp.exp((idx + 1.0)[None] * log_gamma[:, None]).astype(dtype)
    gamma_chunk = jnp.exp(CHUNK * log_gamma).astype(dtype)

    def step(state, inp):
        qb, kb, vb = inp
        scores = jnp.einsum('bhid,bhjd->bhij', qb, kb) * dmat[None]
        inner = jnp.einsum('bhij,bhje->bhie', scores, vb)
        cross = jnp.einsum('bhid,bhde->bhie', qb, state) * xi[None, :, :, None]
        new_state = (gamma_chunk[None, :, None, None] * state
                     + jnp.einsum('bhjd,bhje->bhde', kb * zeta[None, :, :, None], vb))
        return new_state, inner + cross

    state0 = jnp.zeros((bsz, RET_HEADS, RET_HEAD_DIM, RET_HEAD_DIM), dtype=dtype)
    _, out = lax.scan(step, state0, (qc, kc, vc))
    out = out.transpose(1, 0, 3, 2, 4).reshape(bsz, n_chunks * CHUNK, RET_HEADS, RET_HEAD_DIM)[:, pad:]

    of = out.astype(jnp.float32)
    mu = jnp.mean(of, axis=-1, keepdims=True)
    oc = of - mu
    var = jnp.mean(oc * oc, axis=-1, keepdims=True)
    on = (oc * lax.rsqrt(var + GN_EPS)).reshape(bsz, L, RET_WIDTH)
    on = (on * gn_g.astype(jnp.float32) + gn_b.astype(jnp.float32)).astype(dtype)
    return jax.nn.silu(g) * on


def setup_inputs(seed: int = 0) -> dict:
    key = jax.random.key(seed)
    ks = jax.random.split(key, 24)
    f32 = jnp.float32
    nrm = lambda k, s, sc: jax.random.normal(k, s, f32) * sc
    P, G, H = S5_STATE, S5_GROUPS, S5_GROUP_CH
    x = jax.random.normal(ks[0], (BATCH, SEQ, D_MODEL), f32)
    meta_tokens = nrm(ks[1], (N_META, D_MODEL), 1.0)
    ln_in_g = 1.0 + nrm(ks[2], (D_MODEL,), 0.02)
    ln_in_b = nrm(ks[3], (D_MODEL,), 0.02)
    w_in = nrm(ks[4], (DEPTH, D_MODEL, IN_PROJ_WIDTH), D_MODEL ** -0.5)
    s5_lambda_re = -0.5 + nrm(ks[5], (DEPTH, G, P), 0.01)
    s5_lambda_im = math.pi * jnp.broadcast_to(jnp.arange(P, dtype=f32), (DEPTH, G, P)) + nrm(ks[6], (DEPTH, G, P), 0.01)
    s5_log_dt = jax.random.uniform(ks[7], (DEPTH, G), f32, math.log(1e-3), math.log(1e-1))
    s5_b_re = nrm(ks[8], (DEPTH, G, P, H), (2.0 * H) ** -0.5)
    s5_b_im = nrm(ks[9], (DEPTH, G, P, H), (2.0 * H) ** -0.5)
    s5_c_re = nrm(ks[10], (DEPTH, G, H, P), (2.0 * P) ** -0.5)
    s5_c_im = nrm(ks[11], (DEPTH, G, H, P), (2.0 * P) ** -0.5)
    s5_d = nrm(ks[12], (DEPTH, S5_WIDTH), 1.0)
    s5_w_glu = nrm(ks[13], (DEPTH, S5_WIDTH, S5_WIDTH), S5_WIDTH ** -0.5)
    s5_b_glu = nrm(ks[14], (DEPTH, S5_WIDTH), 0.01)
    ret_gn_g = 1.0 + nrm(ks[15], (DEPTH, RET_WIDTH), 0.02)
    ret_gn_b = nrm(ks[16], (DEPTH, RET_WIDTH), 0.02)
    w_out = nrm(ks[17], (DEPTH, MIX_WIDTH, D_MODEL), MIX_WIDTH ** -0.5 * DEEPNORM_BETA)
    ln1_g = 1.0 + nrm(ks[18], (DEPTH, D_MODEL), 0.02)
    ln1_b = nrm(ks[19], (DEPTH, D_MODEL), 0.02)
    w_up = nrm(ks[20], (DEPTH, D_MODEL, D_FF), D_MODEL ** -0.5)
    w_down = nrm(ks[21], (DEPTH, D_FF, D_MODEL), D_FF ** -0.5 * DEEPNORM_BETA)
    ln2_g = 1.0 + nrm(ks[22], (DEPTH, D_MODEL), 0.02)
    ln2_b = nrm(ks[23], (DEPTH, D_MODEL), 0.02)
    return {"x": x, "meta_tokens": meta_tokens, "ln_in_g": ln_in_g, "ln_in_b": ln_in_b,
            "w_in": w_in, "s5_lambda_re": s5_lambda_re, "s5_lambda_im": s5_lambda_im,
            "s5_log_dt": s5_log_dt, "s5_b_re": s5_b_re, "s5_b_im": s5_b_im,
            "s5_c_re": s5_c_re, "s5_c_im": s5_c_im, "s5_d": s5_d, "s5_w_glu": s5_w_glu,
            "s5_b_glu": s5_b_glu, "ret_gn_g": ret_gn_g, "ret_gn_b": ret_gn_b, "w_out": w_out,
            "ln1_g": ln1_g, "ln1_b": ln1_b, "w_up": w_up, "w_down": w_down,
            "ln2_g": ln2_g, "ln2_b": ln2_b}


def reference(x, meta_tokens, ln_in_g, ln_in_b, w_in, s5_lambda_re, s5_lambda_im, s5_log_dt,
              s5_b_re, s5_b_im, s5_c_re, s5_c_im, s5_d, s5_w_glu, s5_b_glu, ret_gn_g, ret_gn_b,
              w_out, ln1_g, ln1_b, w_up, w_down, ln2_g, ln2_b):
    bsz = x.shape[0]
    meta = jnp.broadcast_to(meta_tokens[None].astype(x.dtype), (bsz, N_META, D_MODEL))
    h = jnp.concatenate([meta, x], axis=1)
    h = layer_norm(h, ln_in_g, ln_in_b)
    splits = [S5_WIDTH, S5_WIDTH + RET_WIDTH, S5_WIDTH + 2 * RET_WIDTH, S5_WIDTH + 3 * RET_WIDTH]
    for l in range(DEPTH):
        proj = h @ w_in[l]
        u, q, k, v, g = jnp.split(proj, splits, axis=-1)
        y_s5 = s5_mixer(u, s5_lambda_re[l], s5_lambda_im[l], s5_log_dt[l], s5_b_re[l], s5_b_im[l],
                        s5_c_re[l], s5_c_im[l], s5_d[l], s5_w_glu[l], s5_b_glu[l])
        y_ret = retention_mixer(q, k, v, g, ret_gn_g[l], ret_gn_b[l])
        mixed = jnp.concatenate([y_s5, y_ret], axis=-1) @ w_out[l]
        h = layer_norm(DEEPNORM_ALPHA * h + mixed, ln1_g[l], ln1_b[l])
        ff = jnp.square(jax.nn.relu(h @ w_up[l])) @ w_down[l]
        h = layer_norm(DEEPNORM_ALPHA * h + ff, ln2_g[l], ln2_b[l])
    return h[:, N_META:]
```

```python
import math
import numpy as np
import concourse.bass as bass
import concourse.mybir as mybir
from concourse.bass_utils import run_bass_kernel_spmd
from contextlib import ExitStack

F32 = mybir.dt.float32
BF16 = mybir.dt.bfloat16
ALU = mybir.AluOpType
AF = mybir.ActivationFunctionType

D = 1024
SEQ = 4096
NMETA = 16
NH = 6
NT = 4
W = NT * 128
NST = SEQ // W
SB = 128
ALPHA = 2.0 ** 0.25
LN_EPS = 1e-5
NSLOT = 33 * 128
ENG_NAMES = ("pe", "act", "dve", "pool", "sp")
STRICT_SYNC = False
INTERLEAVE = True
A_PER_B = 2.6
LAG = 2
PRE_EVERY = 12


class Prog:
    def __init__(self, nc, n_dma_sems=12):
        self.nc = nc
        self.ops = []
        self.res_w = {}
        self.res_r = {}
        self.cnt = {e: 0 for e in ENG_NAMES}
        self.n_dma_sems = n_dma_sems
        self.dma_cnt = {}
        self.dma_rr = {"sp": 0, "pool": 0, "act": 0}
        self.dma_last = {}

    def _deps(self, reads, writes, excl):
        deps = {}
        for r in list(reads) + list(excl):
            if r in self.res_w:
                deps[self.res_w[r]] = True
        for w in list(writes):
            if w in self.res_w:
                deps.setdefault(self.res_w[w], False)
            for rd in self.res_r.get(w, ()):
                deps.setdefault(rd, False)
        for w in excl:
            for rd in self.res_r.get(w, ()):
                deps.setdefault(rd, False)
        return deps

    def _commit(self, oid, reads, writes, excl):
        for r in list(reads) + list(excl):
            self.res_r.setdefault(r, []).append(oid)
        for w in list(writes):
            self.res_w[w] = oid
            self.res_r[w] = []

    def op(self, eng, fn, reads=(), writes=(), excl=()):
        oid = len(self.ops)
        deps = self._deps(reads, writes, excl)
        self.cnt[eng] += 1
        self.ops.append(dict(eng=eng, fn=fn, deps=deps, tok=("E", eng, self.cnt[eng]), dma=False))
        self._commit(oid, reads, writes, excl)
        return oid

    def dma(self, queue, fn, reads=(), writes=()):
        oid = len(self.ops)
        deps = self._deps(reads, writes, ())
        slot = self.dma_rr[queue]
        self.dma_rr[queue] = (slot + 1) % self.n_dma_sems
        key = (queue, slot)
        if key in self.dma_last:
            deps[self.dma_last[key]] = True
        self.dma_cnt[key] = self.dma_cnt.get(key, 0) + 1
        self.dma_last[key] = oid
        self.ops.append(dict(eng=queue, fn=fn, deps=deps, tok=("D", key, 16 * self.dma_cnt[key]), dma=True))
        self._commit(oid, reads, writes, ())
        return oid

    def emit(self, final_wait_ops=()):
        nc = self.nc
        with ExitStack() as es:
            esem = {e: es.enter_context(nc.semaphore(f"s_{e}")) for e in ENG_NAMES}
            dsem = {}
            for q in ("sp", "pool", "act"):
                for s in range(self.n_dma_sems):
                    if (q, s) in self.dma_cnt:
                        dsem[(q, s)] = es.enter_context(nc.semaphore(f"d_{q}{s}"))
            block = es.enter_context(nc.Block())

            def semval(tok):
                if tok[0] == "E":
                    return esem[tok[1]], tok[2]
                return dsem[tok[1]], tok[2]

            def run_engine(ename, eobj):
                known = {}
                for oid, o in enumerate(self.ops):
                    if o["eng"] != ename:
                        continue
                    for d in sorted(o["deps"]):
                        do = self.ops[d]
                        if (not o["dma"]) and (not do["dma"]) and do["eng"] == ename:
                            if ename == "pe" or not (o["deps"][d] or STRICT_SYNC):
                                continue
                        sem, val = semval(do["tok"])
                        k = id(sem)
                        if known.get(k, 0) >= val:
                            continue
                        eobj.wait_ge(sem, val)
                        known[k] = val
                    ins = o["fn"](eobj)
                    sem, val = semval(o["tok"])
                    ins.then_inc(sem, 16 if o["dma"] else 1)
                if ename == "sp":
                    for oid in final_wait_ops:
                        sem, val = semval(self.ops[oid]["tok"])
                        eobj.wait_ge(sem, val)

            @block.tensor
            def _(e):
                run_engine("pe", e)

            @block.scalar
            def _(e):
                run_engine("act", e)

            @block.vector
            def _(e):
                run_engine("dve", e)

            @block.gpsimd
            def _(e):
                run_engine("pool", e)

            @block.sync
            def _(e):
                run_engine("sp", e)


def _host_consts():
    c = {}
    c["ident_f"] = np.eye(128, dtype=np.float32)
    pm = np.zeros((128, 128), np.float32)
    for dp in range(128):
        pm[(dp + 64) % 128, dp] = 1.0
    c["pswap"] = pm
    pos = (np.arange(NSLOT, dtype=np.float32) - 112.0).astype(np.float32)
    inv_freq = (1.0 / (10000.0 ** (np.arange(0, 128, 2, dtype=np.float32) / 128.0))).astype(np.float32)
    ang = (pos[:, None] * inv_freq[None, :]).astype(np.float32)
    cs, sn = np.cos(ang).astype(np.float32), np.sin(ang).astype(np.float32)
    c["cosT"] = np.ascontiguousarray(np.concatenate([cs, cs], axis=1).T)
    c["sinT"] = np.ascontiguousarray(np.concatenate([-sn, sn], axis=1).T)
    lg = np.log1p(-np.exp2(-5.0 - np.arange(NH, dtype=np.float32))).astype(np.float32)
    idx = np.arange(128, dtype=np.float32)
    scale = 128.0 ** -0.5
    diff = idx[None, :] - idx[:, None]
    dm = np.where(diff[None] >= 0, np.exp(np.maximum(diff, 0.0)[None] * lg[:, None, None]), 0.0) * scale
    c["dmatT"] = np.ascontiguousarray(dm.transpose(1, 0, 2)).astype(np.float32)
    xi = np.exp((idx + 1.0)[None] * lg[:, None]).astype(np.float32)
    c["xi_bc"] = np.ascontiguousarray(np.broadcast_to(xi[None], (128, NH, 128))).astype(np.float32)
    zeta = (np.exp((127.0 - idx)[None] * lg[:, None]) * scale).astype(np.float32)
    c["zeta"] = np.ascontiguousarray(zeta.T)
    c["gammaC"] = [float(np.exp(128.0 * lg[h])) for h in range(NH)]
    mb = np.zeros((128, 4, 8), np.float32)
    for gl in range(2):
        for j4 in range(4):
            mb[gl * 64:(gl + 1) * 64, j4, 2 * j4 + gl] = 1.0
    c["maskB"] = mb
    return c


CONST = _host_consts()


def build():
    nc = bass.Bass("TRN2", target_bir_lowering=False)

    def din(name, shape):
        return nc.dram_tensor(name, list(shape), F32, kind="ExternalInput").ap()

    x_d = din("x", [SEQ, D])
    meta_d = din("meta", [NMETA, D])
    w_in_d = din("w_in", [D, 3328])
    w_out_d = din("w_out", [D, D])
    w_up_d = din("w_up", [D, 4 * D])
    w_down_d = din("w_down", [4 * D, D])
    w_glu_d = din("w_glu", [256, 256])
    lnin_gb_d = din("lnin_gb", [128, 16])
    ln1_gb_d = din("ln1_gb", [128, 16])
    ln2_g_d = din("ln2_g", [1, D])
    ln2_b_d = din("ln2_b", [1, D])
    s5_sc_d = din("s5_sc", [128, 24])
    s5_b_d = din("s5_b", [128, 2, 8, 16])
    s5_c_d = din("s5_c", [128, 2, 8, 16])
    s5_fm_d = din("s5_fm", [128, 4])
    gn_gb_d = din("gn_gb", [128, 12])
    ident_d = din("ident_f", [128, 128])
    pswap_d = din("pswap", [128, 128])
    cosT_d = din("cosT", [128, NSLOT])
    sinT_d = din("sinT", [128, NSLOT])
    dmatT_d = din("dmatT", [128, NH, 128])
    xibc_d = din("xi_bc", [128, NH, 128])
    zeta_d = din("zeta", [128, NH])
    maskB_d = din("maskB", [128, 4, 8])
    out_d = nc.dram_tensor("out", [SEQ, D], F32, kind="ExternalOutput").ap()

    with ExitStack() as es:
        def sb(name, shape, dt=F32):
            return es.enter_context(nc.sbuf_tensor(name, list(shape), dt))

        def psum(name, shape, dt=F32):
            return es.enter_context(nc.psum_tensor(name, list(shape), dt))

        xs = [sb(f"xs{i}", [128, D]) for i in range(2)]
        xnb = [sb(f"xnb{i}", [128, D], BF16) for i in range(2)]
        tmpT = sb("tmpT", [128, 8, 128], BF16)
        _hT = sb("hT", [128, 8, W], BF16)
        hTs = [_hT, _hT]
        h1T = sb("h1T", [128, 8, W], BF16)
        wA = [sb(f"wA{i}", [128, 2048], BF16) for i in range(3)]
        wB = [sb(f"wB{i}", [128, 2048], BF16) for i in range(4)]
        uT = sb("uT", [128, 2, W], BF16)
        q_pre = sb("q_pre", [128, W], BF16)
        k_pre = sb("k_pre", [128, W], BF16)
        qT = sb("qT", [128, W], BF16)
        kT = sb("kT", [128, W], BF16)
        qxT = sb("qxT", [128, W], BF16)
        sgT = sb("sgT", [128, 2, W], BF16)
        v_sb = sb("v_sb", [128, NT, 768], BF16)
        yT = sb("yT", [128, 8, W], BF16)
        aT = sb("aT", [128, 32, W], BF16)
        relu_t = [sb(f"relu_t{i}", [128, W], BF16) for i in range(2)]
        rT = sb("rT", [128, 8, W], BF16)
        cos_sb = sb("cos_sb", [128, W])
        sin_sb = sb("sin_sb", [128, W])
        ln2g = sb("ln2g", [128, D])
        ln2b = sb("ln2b", [128, D])
        ident_f = sb("ident_fs", [128, 128])
        ident_b = sb("ident_b", [128, 128], BF16)
        aI = sb("aI", [128, 128], BF16)
        pswap = sb("pswap_s", [128, 128], BF16)
        dmatT = sb("dmatT_s", [128, NH, 128])
        xibc = sb("xibc_s", [128, NH, 128])
        zeta = sb("zeta_s", [128, NH])
        maskB = sb("maskB_s", [128, 4, 8])
        lnin_gb = sb("lnin_gb_s", [128, 16])
        ln1_gb = sb("ln1_gb_s", [128, 16])
        gn_gb = sb("gn_gb_s", [128, 12])
        s5_fm = sb("s5_fm_s", [128, 4])
        epsb = sb("epsb", [128, 1])
        stt = sb("stt", [128, 8, 12])
        mv = sb("mv", [128, 8, 8])
        scT_sb = sb("scT_sb", [128, NT, 128], BF16)
        kz_sb = sb("kz_sb", [128, NT, 128], BF16)
        on_sb = sb("on_sb", [128, NT, 128], BF16)
        gaff = sb("gaff", [128, W], BF16)
        S32 = sb("S32", [128, NH, 128])
        Sall = sb("Sall", [128, 3, 128])
        Sbf_all = sb("Sbf_all", [128, NT, 128], BF16)
        gst = sb("gst", [128, NT, 6])
        gmv = sb("gmv", [128, NT, 2])
        gsc = sb("gsc", [128, 3, NT])
        hsets = [(qT, kT, qxT, scT_sb, kz_sb, on_sb, Sall, Sbf_all, gst, gmv, gsc),
                 (sb("qT2", [128, W], BF16), sb("kT2", [128, W], BF16), sb("qxT2", [128, W], BF16),
                  sb("scT_sb2", [128, NT, 128], BF16), sb("kz_sb2", [128, NT, 128], BF16), sb("on_sb2", [128, NT, 128], BF16),
                  sb("Sall2", [128, 3, 128]), sb("Sbf_all2", [128, NT, 128], BF16),
                  sb("gst2", [128, NT, 6]), sb("gmv2", [128, NT, 2]), sb("gsc2", [128, 3, NT]))]
        s5_sc = sb("s5_sc_s", [128, 24])
        s5w = sb("s5w", [128, 40, 8])
        WBT = sb("WBT", [128, 16, 128], BF16)
        CT = sb("CT", [128, 16, 128], BF16)
        Etab = sb("Etab", [128, 2, 8, SB])
        s5m = sb("s5m", [128, 4, 4 * SB])
        s5p = sb("s5p", [128, 4, 4 * SB])
        s5x = [sb(f"s5x{i}", [128, 2, 4 * SB], BF16) for i in range(2)]
        Xc = sb("Xc", [128, 2, 8])
        ypre = sb("ypre", [128, W])
        ygb = sb("ygb", [128, 2, W], BF16)
        sgl = sb("sgl", [128, W], BF16)
        wglu = sb("wglu", [128, 2, 256], BF16)
        rope_t1 = s5p[:, 0, :]
        rope_t2 = s5p[:, 1, :]
        PALL = ["s5p0", "s5p1", "s5p2", "s5p3"]
        pfl = s5p[:].rearrange("p a w -> p (a w)")
        s5_b = pfl[:, 0:256].rearrange("p (r j h) -> p r j h", r=2, j=8)
        s5_c = pfl[:, 256:512].rearrange("p (r j h) -> p r j h", r=2, j=8)
        s5bb = pfl[:, 512:768].rearrange("p (r j h) -> p r j h", r=2, j=8)
        s5ex = pfl[:, 768:896].rearrange("p (g h) -> p g h", g=8)
        s5tmp = pfl[:, 1024:1536].rearrange("p (x j h) -> p x j h", x=4, j=8)

        bA = [psum(f"bA{i}", [128, 512]) for i in range(2)]
        bB = [psum(f"bB{i}", [128, 512]) for i in range(2)]
        bT = psum("bT", [128, 1024], BF16)
        X = [psum(f"X{i}", [128, 512]) for i in range(3)]
        bB0h = bB[0][:].bitcast(BF16)
        bB1h = bB[1][:].bitcast(BF16)

        P = Prog(nc)
        Etmp = aT[:, 0:16, :].bitcast(F32).rearrange("p a b -> p (a b)").rearrange("p (x j n) -> p x j n", x=4, j=8)
        DV = lambda fn, r=(), w=(), x=(): P.op("dve", fn, r, w, x)
        AC = lambda fn, r=(), w=(), x=(): P.op("act", fn, r, w, x)
        PE = lambda fn, r=(), w=(): P.op("pe", fn, r, w)

        def ld(dst, src, name, q="sp"):
            P.dma(q, lambda e: e.dma_start(out=dst, in_=src), writes=name if isinstance(name, list) else [name])

        ld(ident_f[:], ident_d, "ident_f")
        ld(dmatT[:], dmatT_d, "dmatT")
        ld(zeta[:], zeta_d, "zeta")
        ld(maskB[:], maskB_d, "maskB")
        ld(lnin_gb[:], lnin_gb_d, "lnin_gb")
        ld(ln1_gb[:], ln1_gb_d, "ln1_gb")
        ld(gn_gb[:], gn_gb_d, "gn_gb")
        ld(s5_fm[:], s5_fm_d, "s5_fm")
        ld(s5_sc[:], s5_sc_d, "s5_sc")
        ld(s5_b, s5_b_d, PALL)
        ld(s5_c, s5_c_d, PALL)
        ld(ln2g[:], ln2_g_d[0].partition_broadcast(128), "ln2g")
        ld(ln2b[:], ln2_b_d[0].partition_broadcast(128), "ln2b")
        ld(pswap[:], pswap_d, "pswap", q="pool")
        ld(xibc[:], xibc_d, "xibc")
        ld(wglu[:], w_glu_d.rearrange("(k p) c -> p k c", p=128), "wglu", q="pool")
        DV(lambda e: e.memset(epsb[:], LN_EPS), w=["epsb"])
        AC(lambda e: e.activation(out=ident_b[:], in_=ident_f[:], func=AF.Copy), r=["ident_f"], w=["ident_b"])
        AC(lambda e: e.activation(out=aI[:], in_=ident_f[:], func=AF.Identity, scale=ALPHA), r=["ident_f"], w=["aI"])
        DV(lambda e: e.memset(S32[:], 0.0), w=[f"S32{h}" for h in range(NH)])
        DV(lambda e: e.memset(Xc[:], 0.0), w=["Xc"])

        SL = lambda i: s5w[:, i, :]
        lam_re, lam_im, log_dt = s5_sc[:, 0:8], s5_sc[:, 8:16], s5_sc[:, 16:24]
        (I_DT, I_AR, I_TH, I_RHO, I_PHI, I_U, I_S, I_C, I_TS, I_TC, I_CC, I_SS, I_CS, I_N, I_RN,
         I_LBR, I_LBI, I_DEN, I_NR, I_QR, I_QI, I_T1, I_T2, I_NS) = range(24)

        def dv2(out, a, b, op):
            DV(lambda e: e.tensor_tensor(out=out, in0=a, in1=b, op=op), r=["s5w", "s5_sc"], w=["s5w"])

        AC(lambda e: e.activation(out=SL(I_DT), in_=log_dt, func=AF.Exp), r=["s5_sc"], w=["s5w"])
        dv2(SL(I_AR), lam_re, SL(I_DT), ALU.mult)
        dv2(SL(I_TH), lam_im, SL(I_DT), ALU.mult)
        AC(lambda e: e.activation(out=SL(I_RHO), in_=SL(I_AR), func=AF.Exp), r=["s5w"], w=["s5w"])
        DV(lambda e: e.tensor_scalar(out=SL(I_PHI), in0=SL(I_TH), scalar1=1.0 / 32.0, scalar2=None, op0=ALU.mult), r=["s5w"], w=["s5w"])
        dv2(SL(I_U), SL(I_PHI), SL(I_PHI), ALU.mult)
        DV(lambda e: e.tensor_copy(out=SL(I_S), in_=SL(I_PHI)), r=["s5w"], w=["s5w"])
        DV(lambda e: e.tensor_copy(out=SL(I_TS), in_=SL(I_PHI)), r=["s5w"], w=["s5w"])
        DV(lambda e: e.memset(SL(I_C), 1.0), r=["s5w"], w=["s5w"])
        DV(lambda e: e.memset(SL(I_TC), 1.0), r=["s5w"], w=["s5w"])
        for kk in range(1, 7):
            cs_ = -1.0 / ((2 * kk) * (2 * kk + 1))
            cc_ = -1.0 / ((2 * kk - 1) * (2 * kk))
            DV(lambda e, c_=cs_: e.scalar_tensor_tensor(out=SL(I_TS), in0=SL(I_TS), scalar=c_, in1=SL(I_U), op0=ALU.mult, op1=ALU.mult), r=["s5w"], w=["s5w"])
            dv2(SL(I_S), SL(I_S), SL(I_TS), ALU.add)
            DV(lambda e, c_=cc_: e.scalar_tensor_tensor(out=SL(I_TC), in0=SL(I_TC), scalar=c_, in1=SL(I_U), op0=ALU.mult, op1=ALU.mult), r=["s5w"], w=["s5w"])
            dv2(SL(I_C), SL(I_C), SL(I_TC), ALU.add)
        for _ in range(5):
            dv2(SL(I_CC), SL(I_C), SL(I_C), ALU.mult)
            dv2(SL(I_SS), SL(I_S), SL(I_S), ALU.mult)
            dv2(SL(I_CS), SL(I_C), SL(I_S), ALU.mult)
            dv2(SL(I_C), SL(I_CC), SL(I_SS), ALU.subtract)
            DV(lambda e: e.tensor_scalar(out=SL(I_S), in0=SL(I_CS), scalar1=2.0, scalar2=None, op0=ALU.mult), r=["s5w"], w=["s5w"])
        dv2(SL(I_CC), SL(I_C), SL(I_C), ALU.mult)
        dv2(SL(I_SS), SL(I_S), SL(I_S), ALU.mult)
        dv2(SL(I_N), SL(I_CC), SL(I_SS), ALU.add)
        AC(lambda e: e.activation(out=SL(I_NS), in_=SL(I_N), func=AF.Sqrt), r=["s5w"], w=["s5w"])
        DV(lambda e: e.reciprocal(out=SL(I_RN), in_=SL(I_NS)), r=["s5w"], w=["s5w"])
        dv2(SL(I_C), SL(I_C), SL(I_RN), ALU.mult)
        dv2(SL(I_S), SL(I_S), SL(I_RN), ALU.mult)
        dv2(SL(I_LBR), SL(I_RHO), SL(I_C), ALU.mult)
        dv2(SL(I_LBI), SL(I_RHO), SL(I_S), ALU.mult)
        dv2(SL(I_T1), lam_re, lam_re, ALU.mult)
        dv2(SL(I_T2), lam_im, lam_im, ALU.mult)
        dv2(SL(I_DEN), SL(I_T1), SL(I_T2), ALU.add)
        DV(lambda e: e.reciprocal(out=SL(I_DEN), in_=SL(I_DEN)), r=["s5w"], w=["s5w"])
        DV(lambda e: e.tensor_scalar(out=SL(I_NR), in0=SL(I_LBR), scalar1=-1.0, scalar2=None, op0=ALU.add), r=["s5w"], w=["s5w"])
        dv2(SL(I_T1), SL(I_NR), lam_re, ALU.mult)
        dv2(SL(I_T2), SL(I_LBI), lam_im, ALU.mult)
        dv2(SL(I_QR), SL(I_T1), SL(I_T2), ALU.add)
        dv2(SL(I_QR), SL(I_QR), SL(I_DEN), ALU.mult)
        dv2(SL(I_T1), SL(I_LBI), lam_re, ALU.mult)
        dv2(SL(I_T2), SL(I_NR), lam_im, ALU.mult)
        dv2(SL(I_QI), SL(I_T1), SL(I_T2), ALU.subtract)
        dv2(SL(I_QI), SL(I_QI), SL(I_DEN), ALU.mult)
        qr_bc = SL(I_QR).unsqueeze(2).broadcast_to([128, 8, 16])
        qi_bc = SL(I_QI).unsqueeze(2).broadcast_to([128, 8, 16])

        def bb(out, a, b_, op):
            DV(lambda e: e.tensor_tensor(out=out, in0=a, in1=b_, op=op), r=["s5w"] + PALL, w=PALL)

        bb(s5tmp[:, 0], s5_b[:, 0], qr_bc, ALU.mult)
        bb(s5tmp[:, 1], s5_b[:, 1], qi_bc, ALU.mult)
        bb(s5tmp[:, 2], s5_b[:, 1], qr_bc, ALU.mult)
        bb(s5tmp[:, 3], s5_b[:, 0], qi_bc, ALU.mult)
        bb(s5bb[:, 0], s5tmp[:, 0], s5tmp[:, 1], ALU.subtract)
        bb(s5bb[:, 1], s5tmp[:, 2], s5tmp[:, 3], ALU.add)
        for j in range(8):
            mk = maskB[:, j % 4, :].unsqueeze(2).broadcast_to([128, 8, 16])
            for ri in range(2):
                src = s5bb[:, ri, j, :].unsqueeze(1).broadcast_to([128, 8, 16])
                DV(lambda e, src=src, mk=mk: e.tensor_tensor(out=s5ex, in0=src, in1=mk, op=ALU.mult), r=PALL + ["maskB"], w=PALL)
                PE(lambda e: e.transpose(X[0][:, 0:128], s5ex.rearrange("p g h -> p (g h)"), ident_f[:]), r=PALL + ["ident_f"], w=["X0"])
                AC(lambda e, j=j, ri=ri: e.activation(out=WBT[:, 2 * j + ri, :], in_=X[0][:, 0:128], func=AF.Copy), x=["X0"], w=["WBT"])
                csrc = s5_c[:, ri, j, :].unsqueeze(1).broadcast_to([128, 8, 16])
                sgn = 1.0 if ri == 0 else -1.0
                DV(lambda e, csrc=csrc, mk=mk, j=j, ri=ri, sgn=sgn: e.scalar_tensor_tensor(
                    out=CT[:, 2 * j + ri, :].rearrange("p (g h) -> p g h", g=8), in0=csrc, scalar=sgn, in1=mk,
                    op0=ALU.mult, op1=ALU.mult), r=PALL + ["maskB"], w=["CT"])
        DV(lambda e: e.tensor_copy(out=Etab[:, 0, :, 0], in_=SL(I_C)), r=["s5w"], w=["Etab"])
        DV(lambda e: e.tensor_scalar(out=Etab[:, 1, :, 0], in0=SL(I_S), scalar1=-1.0, scalar2=None, op0=ALU.mult), r=["s5w"], w=["Etab"])
        n = 1
        while n < SB:
            ar = Etab[:, 0, :, 0:n]
            ai = Etab[:, 1, :, 0:n]
            br = Etab[:, 0, :, n - 1:n].broadcast_to([128, 8, n])
            bi = Etab[:, 1, :, n - 1:n].broadcast_to([128, 8, n])
            t_ = [Etmp[:, i, :, 0:n] for i in range(4)]
            for (o_, a_, b_) in ((t_[0], ar, br), (t_[1], ai, bi), (t_[2], ar, bi), (t_[3], ai, br)):
                DV(lambda e, o_=o_, a_=a_, b_=b_: e.tensor_tensor(out=o_, in0=a_, in1=b_, op=ALU.mult), r=["Etab", "aT"], w=["aT"])
            DV(lambda e, n=n, t_=t_: e.tensor_tensor(out=Etab[:, 0, :, n:2 * n], in0=t_[0], in1=t_[1], op=ALU.subtract), r=["aT"], w=["Etab"])
            DV(lambda e, n=n, t_=t_: e.tensor_tensor(out=Etab[:, 1, :, n:2 * n], in0=t_[2], in1=t_[3], op=ALU.add), r=["aT"], w=["Etab"])
            n *= 2

        class WStream:
            def __init__(self, bufs, names, plan):
                self.bufs, self.names, self.plan = bufs, names, plan
                self.issued, self.n, self.i = 0, len(bufs), 0

            def take(self, span=1):
                i = self.i
                while self.issued < min(len(self.plan), i + self.n):
                    b = self.issued % self.n
                    for (view_fn, src) in self.plan[self.issued]:
                        P.dma("pool", lambda e, d=view_fn(self.bufs[b]), s=src: e.dma_start(out=d, in_=s), writes=[self.names[b]])
                    self.issued += 1
                self.i += span
                return [(self.bufs[(i + s) % self.n], self.names[(i + s) % self.n]) for s in range(span)]

        v8 = lambda buf: buf[:].rearrange("p (k c) -> p k c", k=8)
        v16 = lambda buf: buf[:].rearrange("p (k c) -> p k c", k=16)

        def blk256(w_d, c0):
            return [(lambda buf: v8(buf), w_d[:, c0:c0 + 256].rearrange("(k p) c -> p k c", p=128))]

        planA = []
        for st_i in range(-1, NST):
            m_ = st_i < 0
            planA.append(blk256(w_in_d, 0))
            for hp in range(3):
                planA.append(blk256(w_in_d, 256 + 1536 + 256 * hp))
                if not m_:
                    planA.append(blk256(w_in_d, 256 + 2304 + 256 * hp))
                for hl in range(2):
                    h = 2 * hp + hl
                    planA.append([
                        (lambda buf: v8(buf)[:, :, 0:128], w_in_d[:, 256 + 128 * h:256 + 128 * h + 128].rearrange("(k p) c -> p k c", p=128)),
                        (lambda buf: v8(buf)[:, :, 128:256], w_in_d[:, 1024 + 128 * h:1024 + 128 * h + 128].rearrange("(k p) c -> p k c", p=128)),
                    ])
        planB = []
        for st_i in range(NST):
            for cb in range(4):
                planB.append(blk256(w_out_d, 256 * cb))
            for ub in range(16):
                planB.append(blk256(w_up_d, 256 * ub))
            for dm in range(8):
                for half in range(2):
                    planB.append([(lambda buf: v16(buf),
                                   w_down_d[2048 * half:2048 * half + 2048, 128 * dm:128 * dm + 128].rearrange("(k p) c -> p k c", p=128))])
        WA = WStream(wA, [f"wA{i}" for i in range(3)], planA)
        WB = WStream(wB, [f"wB{i}" for i in range(4)], planB)

        bA_i = [0]

        def next_bA():
            i = bA_i[0] % 2
            bA_i[0] += 1
            return i

        X_i = [0]

        def next_X():
            i = X_i[0] % 3
            X_i[0] += 1
            return i

        def mm8(bank, bname, wv, wn, c0, actT, aname, wt, resid=None, rname=None):
            def fn(e):
                ins = None
                for k in range(8):
                    ins = e.matmul(bank[:, 0:wt], lhsT=wv[:, k, c0:c0 + 128], rhs=actT[:, k, 0:wt],
                                   start=(k == 0), stop=(k == 7 and resid is None))
                if resid is not None:
                    ins = e.matmul(bank[:, 0:wt], lhsT=aI[:], rhs=resid[:, 0:wt], start=False, stop=True)
                return ins
            PE(fn, r=[wn, aname] + (["aI", rname] if resid is not None else []), w=[bname])

        def ln_tile(src_halves, src_reads, src_excl, slot, out_kind, gb=None, gbn=None, destT=None, dname=None, t=0,
                    out_rows=None, tb=None, tbn=None, xbh=None):
            st = stt[:, slot, :]
            m = mv[:, slot, :]
            for hh in range(2):
                DV(lambda e, hh=hh: e.bn_stats(out=st[:, 6 * hh:6 * hh + 6], in_=src_halves[hh]),
                   r=src_reads, w=[f"stt{slot}"], x=[src_excl[hh]] if src_excl else [])
            DV(lambda e: e.bn_aggr(out=m[:, 0:2], in_=st), r=[f"stt{slot}"], w=[f"mv{slot}"])
            AC(lambda e: e.activation(out=m[:, 2:3], in_=m[:, 1:2], func=AF.Sqrt, bias=epsb[:], scale=1.0), r=[f"mv{slot}", "epsb"], w=[f"mv{slot}"])
            DV(lambda e: e.reciprocal(out=m[:, 3:4], in_=m[:, 2:3]), r=[f"mv{slot}"], w=[f"mv{slot}"])
            DV(lambda e: e.scalar_tensor_tensor(out=m[:, 4:5], in0=m[:, 0:1], scalar=-1.0, in1=m[:, 3:4], op0=ALU.mult, op1=ALU.mult), r=[f"mv{slot}"], w=[f"mv{slot}"])
            if out_kind == "T":
                if xbh is None:
                    xbh = [(xnb[slot % 2][:, 0:512], f"xnb{slot % 2}"), (xnb[slot % 2][:, 512:1024], f"xnb{slot % 2}")]
                for hh in range(2):
                    AC(lambda e, hh=hh: e.activation(out=xbh[hh][0], in_=src_halves[hh], func=AF.Identity,
                                                     scale=m[:, 3:4], bias=m[:, 4:5]),
                       r=src_reads + [f"mv{slot}"], w=[xbh[hh][1]], x=[src_excl[hh]] if src_excl else [])
                for _ in range(LAG):
                    yield

                def trf(e):
                    ins = None
                    for k in range(8):
                        ins = e.transpose(tb[:, 128 * k:128 * k + 128], xbh[k // 4][0][:, 128 * (k % 4):128 * (k % 4) + 128], ident_b[:])
                    return ins
                PE(trf, r=[xbh[0][1], xbh[1][1], "ident_b"], w=[tbn])
                g_bc = gb[:, 0:8].unsqueeze(2).broadcast_to([128, 8, 128])
                b_bc = gb[:, 8:16].unsqueeze(2).broadcast_to([128, 8, 128])
                DV(lambda e: e.tensor_tensor(out=tmpT[:], in0=tb.rearrange("p (k c) -> p k c", k=8), in1=g_bc, op=ALU.mult),
                   r=[gbn], w=["tmpT"], x=[tbn])
                DV(lambda e: e.tensor_tensor(out=destT[:, :, 128 * t:128 * t + 128], in0=tmpT[:], in1=b_bc, op=ALU.add),
                   r=["tmpT", gbn], w=[dname])
            else:
                xo = xs[slot % 2]
                xon = f"xs{slot % 2}"
                for hh in range(2):
                    AC(lambda e, hh=hh: e.activation(out=xo[:, 512 * hh:512 * hh + 512], in_=src_halves[hh], func=AF.Identity,
                                                     scale=m[:, 3:4], bias=m[:, 4:5]),
                       r=src_reads + [f"mv{slot}"], w=[xon], x=[src_excl[hh]] if src_excl else [])
                DV(lambda e: e.tensor_tensor(out=xo[:], in0=xo[:], in1=ln2g[:], op=ALU.mult), r=[xon, "ln2g"], w=[xon])
                DV(lambda e: e.tensor_tensor(out=xo[:], in0=xo[:], in1=ln2b[:], op=ALU.add), r=[xon, "ln2b"], w=[xon])
                out_dma_ops.append(P.dma("sp", lambda e: e.dma_start(out=out_d[out_rows[0]:out_rows[1], :], in_=xo[:]), reads=[xon]))

        def rT_to_tokmajor(t):
            def fn(e):
                ins = None
                for dm in range(8):
                    ins = e.transpose(bB0h[:, 128 * dm:128 * dm + 128], rT[:, dm, 128 * t:128 * t + 128], ident_b[:])
                return ins
            PE(fn, r=["rT", "ident_b"], w=["bB0"])

        out_dma_ops = []
        bT_all = bT[:]

        def s5_block(U, c0, meta, xi_):
            Er = Etab[:, 0, 4 * U:4 * U + 4, :].rearrange("p j t -> p (j t)")
            Ei = Etab[:, 1, 4 * U:4 * U + 4, :].rearrange("p j t -> p (j t)")
            Xre, Xim, XY = X[0], X[1], X[2]

            def mmbu(e):
                ins = None
                for j4 in range(4):
                    j = 4 * U + j4
                    e.matmul(Xre[:, 128 * j4:128 * j4 + 128], lhsT=WBT[:, 2 * j, :], rhs=uT[:, U, c0:c0 + 128], start=True, stop=True)
                    ins = e.matmul(Xim[:, 128 * j4:128 * j4 + 128], lhsT=WBT[:, 2 * j + 1, :], rhs=uT[:, U, c0:c0 + 128], start=True, stop=True)
                return ins
            PE(mmbu, r=["WBT", "uT"], w=["X0", "X1"])
            for (o_, a_, b_, xb_) in ((0, Xre, Er, "X0"), (1, Xim, Ei, "X1"), (2, Xim, Er, "X1"), (3, Xre, Ei, "X0")):
                DV(lambda e, o_=o_, a_=a_, b_=b_: e.tensor_tensor(out=s5m[:, o_, :], in0=a_[:], in1=b_, op=ALU.mult),
                   r=["Etab"], w=[f"s5m{o_}"], x=[xb_])
            DV(lambda e: e.tensor_tensor(out=s5m[:, 0, :], in0=s5m[:, 0, :], in1=s5m[:, 1, :], op=ALU.subtract), r=["s5m0", "s5m1"], w=["s5m0"])
            DV(lambda e: e.tensor_tensor(out=s5m[:, 2, :], in0=s5m[:, 2, :], in1=s5m[:, 3, :], op=ALU.add), r=["s5m2", "s5m3"], w=["s5m2"])
            yield
            for j4 in range(4):
                j = 4 * U + j4
                bs = slice(128 * j4, 128 * j4 + 128)
                for (ri, si, so) in ((0, 0, 1), (1, 2, 3)):
                    DV(lambda e, ri=ri, si=si, so=so, j=j, bs=bs: e.tensor_tensor_scan(
                        out=s5m[:, so, bs], data0=s5w[:, I_RHO, j:j + 1].broadcast_to([128, 128]), data1=s5m[:, si, bs],
                        initial=Xc[:, ri, j:j + 1], op0=ALU.mult, op1=ALU.add),
                       r=["s5w", f"s5m{si}", "Xc"], w=[f"s5m{so}"])
            for (o_, zi_, e_) in ((0, 1, Er), (1, 3, Ei), (2, 3, Er), (3, 1, Ei)):
                DV(lambda e, o_=o_, zi_=zi_, e_=e_: e.tensor_tensor(out=s5p[:, o_, :], in0=s5m[:, zi_, :], in1=e_, op=ALU.mult),
                   r=["Etab", f"s5m{zi_}"], w=[f"s5p{o_}"])
            pv = lambda o_: s5p[:, o_, :].rearrange("p (j t) -> p j t", j=4)[:, :, 127]
            DV(lambda e: e.tensor_tensor(out=Xc[:, 0, 4 * U:4 * U + 4], in0=pv(0), in1=pv(1), op=ALU.add), r=["s5p0", "s5p1"], w=["Xc"])
            DV(lambda e: e.tensor_tensor(out=Xc[:, 1, 4 * U:4 * U + 4], in0=pv(2), in1=pv(3), op=ALU.subtract), r=["s5p2", "s5p3"], w=["Xc"])
            if meta:
                return
            xx = s5x[xi_]
            xn = f"s5x{xi_}"
            DV(lambda e: e.tensor_tensor(out=xx[:, 0, :], in0=s5p[:, 0, :], in1=s5p[:, 1, :], op=ALU.add), r=["s5p0", "s5p1"], w=[xn])
            DV(lambda e: e.tensor_tensor(out=xx[:, 1, :], in0=s5p[:, 2, :], in1=s5p[:, 3, :], op=ALU.subtract), r=["s5p2", "s5p3"], w=[xn])
            for _ in range(LAG):
                yield

            def mmy(e):
                ins = None
                for j4 in range(4):
                    j = 4 * U + j4
                    e.matmul(XY[:, 0:128], lhsT=CT[:, 2 * j, :], rhs=xx[:, 0, 128 * j4:128 * j4 + 128], start=(j4 == 0), stop=False)
                    ins = e.matmul(XY[:, 0:128], lhsT=CT[:, 2 * j + 1, :], rhs=xx[:, 1, 128 * j4:128 * j4 + 128], start=False, stop=(j4 == 3))
                return ins
            PE(mmy, r=["CT", xn], w=["X2"])
            DV(lambda e: e.scalar_tensor_tensor(out=ypre[:, c0:c0 + 128], in0=uT[:, U, c0:c0 + 128], scalar=s5_fm[:, U:U + 1],
                                                in1=XY[:, 0:128], op0=ALU.mult, op1=ALU.add),
               r=["uT", "s5_fm"], w=["ypre"], x=["X2"])
            yield

        def head(hp, hl, meta, nt, wt, hT, hTn):
            h = 2 * hp + hl
            qT, kT, qxT, scT_sb, kz_sb, on_sb, Sall, Sbf_all, gst, gmv, gsc = hsets[hl]
            sfx = f"_{hl}"
            (wb_, wn_), = WA.take()
            wv_ = v8(wb_)
            todo = [("k", 128, k_pre, kT)] + ([] if meta else [("q", 0, q_pre, qT)])
            for (nm, coff, pre, dst) in todo:
                xi = next_X()
                mm8(X[xi], f"X{xi}", wv_, wn_, coff, hT, hTn, wt)
                AC(lambda e, pre=pre, xi=xi: e.activation(out=pre[:, 0:wt], in_=X[xi][:, 0:wt], func=AF.Copy), x=[f"X{xi}"], w=[nm + "_pre"])
                xj = next_X()
                PE(lambda e, pre=pre, xj=xj: e.matmul(X[xj][:, 0:wt], lhsT=pswap[:], rhs=pre[:, 0:wt], start=True, stop=True),
                   r=[nm + "_pre", "pswap"], w=[f"X{xj}"])
                DV(lambda e, pre=pre: e.tensor_tensor(out=rope_t1[:, 0:wt], in0=pre[:, 0:wt], in1=cos_sb[:, 0:wt], op=ALU.mult),
                   r=[nm + "_pre", "cos_sb"], w=["s5p0"])
                DV(lambda e, xj=xj: e.tensor_tensor(out=rope_t2[:, 0:wt], in0=X[xj][:, 0:wt], in1=sin_sb[:, 0:wt], op=ALU.mult),
                   r=["sin_sb"], w=["s5p1"], x=[f"X{xj}"])
                DV(lambda e, dst=dst: e.tensor_tensor(out=dst[:, 0:wt], in0=rope_t1[:, 0:wt], in1=rope_t2[:, 0:wt], op=ALU.add),
                   r=["s5p0", "s5p1"], w=[nm + "T" + sfx])
            if not meta:
                DV(lambda e: e.tensor_tensor(out=qxT[:, 0:wt].rearrange("p (t i) -> p t i", t=nt), in0=qT[:, 0:wt].rearrange("p (t i) -> p t i", t=nt),
                                             in1=xibc[:, h, :].unsqueeze(1).broadcast_to([128, nt, 128]), op=ALU.mult),
                   r=["qT" + sfx, "xibc"], w=["qxT" + sfx])

                def p1(e):
                    ins = None
                    for t in range(nt):
                        ins = e.matmul(X[0][:, 128 * t:128 * t + 128], lhsT=kT[:, 128 * t:128 * t + 128], rhs=qT[:, 128 * t:128 * t + 128], start=True, stop=True)
                    return ins
                PE(p1, r=["kT" + sfx, "qT" + sfx], w=["X0"])
                DV(lambda e: e.tensor_tensor(out=scT_sb[:, 0:nt, :], in0=X[0][:, 0:wt].rearrange("p (t i) -> p t i", t=nt),
                                             in1=dmatT[:, h, :].unsqueeze(1).broadcast_to([128, nt, 128]), op=ALU.mult),
                   r=["dmatT"], w=["scT_sb" + sfx], x=["X0"])

            def p1b(e):
                ins = None
                for t in range(nt):
                    ins = e.transpose(bT_all[:, 128 * t:128 * t + 128], kT[:, 128 * t:128 * t + 128], ident_b[:])
                return ins
            PE(p1b, r=["kT" + sfx, "ident_b"], w=["bT"])
            AC(lambda e: e.activation(out=kz_sb[:, 0:nt, :].rearrange("p t d -> p (t d)"), in_=bT_all[:, 0:wt], func=AF.Identity, scale=zeta[:, h:h + 1]),
               r=["zeta"], w=["kz_sb" + sfx], x=["bT"])
            for _ in range(LAG):
                yield

            def p2(e):
                ins = None
                for t in range(nt):
                    ins = e.matmul(X[1][:, 128 * t:128 * t + 128], lhsT=kz_sb[:, t, :], rhs=v_sb[:, t, 128 * h:128 * h + 128], start=True, stop=True)
                return ins
            PE(p2, r=["kz_sb" + sfx, "v_sb"], w=["X1"])
            if not meta:
                AC(lambda e: e.activation(out=Sbf_all[:, 0, :], in_=S32[:, h, :], func=AF.Copy), r=[f"S32{h}"], w=["Sbf_all" + sfx])
            prev, prevn = S32[:, h, :], f"S32{h}"
            for t in range(nt):
                last = (t == nt - 1)
                dst_, dstn = (S32[:, h, :], f"S32{h}") if last else (Sall[:, t, :], "Sall" + sfx)
                DV(lambda e, prev=prev, dst_=dst_, t=t: e.scalar_tensor_tensor(out=dst_, in0=prev, scalar=CONST["gammaC"][h],
                                                                                in1=X[1][:, 128 * t:128 * t + 128], op0=ALU.mult, op1=ALU.add),
                   r=[prevn], w=[dstn], x=["X1"])
                prev, prevn = dst_, dstn
            if meta:
                return
            AC(lambda e: e.activation(out=Sbf_all[:, 1:nt, :], in_=Sall[:, 0:nt - 1, :], func=AF.Copy), r=["Sall" + sfx], w=["Sbf_all" + sfx])
            for _ in range(LAG):
                yield

            def p3(e):
                ins = None
                for t in range(nt):
                    e.matmul(X[2][:, 128 * t:128 * t + 128], lhsT=scT_sb[:, t, :], rhs=v_sb[:, t, 128 * h:128 * h + 128], start=True, stop=False)
                    ins = e.matmul(X[2][:, 128 * t:128 * t + 128], lhsT=qxT[:, 128 * t:128 * t + 128], rhs=Sbf_all[:, t, :], start=False, stop=True)
                return ins
            PE(p3, r=["scT_sb" + sfx, "v_sb", "qxT" + sfx, "Sbf_all" + sfx], w=["X2"])
            for t in range(nt):
                DV(lambda e, t=t: e.bn_stats(out=gst[:, t, :], in_=X[2][:, 128 * t:128 * t + 128]), w=["gst" + sfx], x=["X2"])
            for t in range(nt):
                DV(lambda e, t=t: e.bn_aggr(out=gmv[:, t, :], in_=gst[:, t, :]), r=["gst" + sfx], w=["gmv" + sfx])
            AC(lambda e: e.activation(out=gsc[:, 0, 0:nt], in_=gmv[:, 0:nt, 1], func=AF.Sqrt, bias=epsb[:], scale=1.0), r=["gmv" + sfx, "epsb"], w=["gsc" + sfx])
            DV(lambda e: e.reciprocal(out=gsc[:, 1, 0:nt], in_=gsc[:, 0, 0:nt]), r=["gsc" + sfx], w=["gsc" + sfx])
            DV(lambda e: e.scalar_tensor_tensor(out=gsc[:, 2, 0:nt], in0=gmv[:, 0:nt, 0], scalar=-1.0, in1=gsc[:, 1, 0:nt], op0=ALU.mult, op1=ALU.mult),
               r=["gmv" + sfx, "gsc" + sfx], w=["gsc" + sfx])
            for t in range(nt):
                AC(lambda e, t=t: e.activation(out=on_sb[:, t, :], in_=X[2][:, 128 * t:128 * t + 128], func=AF.Identity,
                                               scale=gsc[:, 1, t:t + 1], bias=gsc[:, 2, t:t + 1]),
                   r=["gsc" + sfx], w=["on_sb" + sfx], x=["X2"])
            for _ in range(LAG):
                yield

            def p4(e):
                ins = None
                for t in range(nt):
                    ins = e.transpose(bT_all[:, 512 + 128 * t:512 + 128 * t + 128], on_sb[:, t, :], ident_b[:])
                return ins
            PE(p4, r=["on_sb" + sfx, "ident_b"], w=["bT"])
            AC(lambda e: e.activation(out=gaff[:, 0:wt], in_=bT_all[:, 512:512 + wt], func=AF.Identity, scale=gn_gb[:, h:h + 1], bias=gn_gb[:, 6 + h:7 + h]),
               r=["gn_gb"], w=["gaff"], x=["bT"])
            DV(lambda e: e.tensor_tensor(out=yT[:, 2 + h, 0:wt], in0=gaff[:, 0:wt], in1=sgT[:, hl, 0:wt], op=ALU.mult), r=["gaff", "sgT"], w=["yT"])
            yield

        def gen_Apre(st_idx):
            meta = st_idx < 0
            nt = 1 if meta else NT
            hT, hTn = hTs[(st_idx + 1) % 2], "hT"
            for t in range(nt):
                xb_ = xs[t % 2]
                xn_ = f"xs{t % 2}"
                if meta:
                    DV(lambda e, xb_=xb_: e.memset(xb_[:], 0.0), w=[xn_])
                    P.dma("sp", lambda e, xb_=xb_: e.dma_start(out=xb_[112:128, :], in_=meta_d), writes=[xn_])
                else:
                    r0 = st_idx * W + t * 128
                    P.dma("sp", lambda e, r0=r0, xb_=xb_: e.dma_start(out=xb_[:], in_=x_d[r0:r0 + 128, :]), writes=[xn_])
                yield from ln_tile([xb_[:, 0:512], xb_[:, 512:1024]], [xn_], None, t, "T", gb=lnin_gb, gbn="lnin_gb", destT=hT, dname=hTn, t=t,
                                   tb=bT_all, tbn="bT")
                yield
            if meta:
                DV(lambda e: e.memset(hT[:, :, 0:112], 0.0), r=[hTn], w=[hTn])

        def gen_Amain(st_idx):
            meta = st_idx < 0
            nt = 1 if meta else NT
            wt = nt * 128
            slot0 = 0 if meta else 128 + st_idx * W
            hT, hTn = hTs[(st_idx + 1) % 2], "hT"
            P.dma("sp", lambda e: e.dma_start(out=cos_sb[:, 0:wt], in_=cosT_d[:, slot0:slot0 + wt]), writes=["cos_sb"])
            P.dma("sp", lambda e: e.dma_start(out=sin_sb[:, 0:wt], in_=sinT_d[:, slot0:slot0 + wt]), writes=["sin_sb"])
            (wb_, wn_), = WA.take()
            for U in range(2):
                xi = next_X()
                mm8(X[xi], f"X{xi}", v8(wb_), wn_, 128 * U, hT, hTn, wt)
                AC(lambda e, U=U, xi=xi: e.activation(out=uT[:, U, 0:wt], in_=X[xi][:, 0:wt], func=AF.Copy), x=[f"X{xi}"], w=["uT"])
            yield
            def gen_s5():
                cnt = 0
                for U in range(2):
                    for sbi in range(wt // SB):
                        yield from s5_block(U, sbi * SB, meta, cnt % 2)
                        cnt += 1
                    if not meta:
                        AC(lambda e, U=U: e.activation(out=ygb[:, U, 0:wt], in_=ypre[:, 0:wt], func=AF.Gelu_apprx_tanh), r=["ypre"], w=["ygb"])
                    yield
                if not meta:
                    for U2 in range(2):
                        xi = next_X()

                        def mmg(e, U2=U2, xi=xi):
                            e.matmul(X[xi][:, 0:wt], lhsT=wglu[:, 0, 128 * U2:128 * U2 + 128], rhs=ygb[:, 0, 0:wt], start=True, stop=False)
                            return e.matmul(X[xi][:, 0:wt], lhsT=wglu[:, 1, 128 * U2:128 * U2 + 128], rhs=ygb[:, 1, 0:wt], start=False, stop=True)
                        PE(mmg, r=["wglu", "ygb"], w=[f"X{xi}"])
                        AC(lambda e, U2=U2, xi=xi: e.activation(out=sgl[:, 0:wt], in_=X[xi][:, 0:wt], func=AF.Sigmoid, bias=s5_fm[:, 2 + U2:3 + U2], scale=1.0),
                           r=["s5_fm"], w=["sgl"], x=[f"X{xi}"])
                        DV(lambda e, U2=U2: e.tensor_tensor(out=yT[:, U2, 0:wt], in0=ygb[:, U2, 0:wt], in1=sgl[:, 0:wt], op=ALU.mult), r=["ygb", "sgl"], w=["yT"])
                    yield

            def gen_ret():
                for hp in range(3):
                    (wb_, wn_), = WA.take()
                    for tp in range((nt + 1) // 2):
                        xi = next_X()
                        tl = [t for t in (2 * tp, 2 * tp + 1) if t < nt]

                        def mmv(e, tl=tl, xi=xi, wb_=wb_):
                            ins = None
                            for ii, t in enumerate(tl):
                                for k in range(8):
                                    ins = e.matmul(X[xi][:, 256 * ii:256 * ii + 256], lhsT=hT[:, k, 128 * t:128 * t + 128], rhs=v8(wb_)[:, k, 0:256],
                                                   start=(k == 0), stop=(k == 7))
                            return ins
                        PE(mmv, r=[wn_, hTn], w=[f"X{xi}"])
                        AC(lambda e, tl=tl, xi=xi, hp=hp: e.activation(
                            out=v_sb[:, tl[0]:tl[0] + len(tl), 256 * hp:256 * hp + 256],
                            in_=X[xi][:, 0:256 * len(tl)].rearrange("p (t c) -> p t c", t=len(tl)), func=AF.Copy),
                           x=[f"X{xi}"], w=["v_sb"])
                    yield
                    if not meta:
                        (wb_, wn_), = WA.take()
                        for hl in range(2):
                            xi = next_X()
                            mm8(X[xi], f"X{xi}", v8(wb_), wn_, 128 * hl, hT, hTn, wt)
                            AC(lambda e, hl=hl, xi=xi: e.activation(out=sgT[:, hl, 0:wt], in_=X[xi][:, 0:wt], func=AF.Silu), x=[f"X{xi}"], w=["sgT"])
                        yield
                    hl_live = [head(hp, 0, meta, nt, wt, hT, hTn), head(hp, 1, meta, nt, wt, hT, hTn)]
                    while hl_live:
                        for g in list(hl_live):
                            try:
                                next(g)
                                yield
                            except StopIteration:
                                hl_live.remove(g)

            g1, g2 = gen_ret(), gen_s5()
            live = [g1, g2]
            while live:
                for g in list(live):
                    try:
                        next(g)
                        yield
                    except StopIteration:
                        live.remove(g)

        def gen_E(st_idx):
            hT, hTn = hTs[(st_idx + 1) % 2], "hT"
            for cb in range(4):
                (wb_, wn_), = WB.take()
                for hl in range(2):
                    dm = 2 * cb + hl
                    bi_ = next_bA()
                    mm8(bA[bi_], f"bA{bi_}", v8(wb_), wn_, 128 * hl, yT, "yT", W, resid=hT[:, dm, :], rname=hTn)
                    AC(lambda e, dm=dm, bi_=bi_: e.activation(out=rT[:, dm, :], in_=bA[bi_][:, 0:W], func=AF.Copy), x=[f"bA{bi_}"], w=["rT"])
                yield

        def gen_B(st_idx):
            for t in range(NT):
                rT_to_tokmajor(t)
                yield from ln_tile([bB0h[:, 0:512], bB0h[:, 512:1024]], [], ["bB0", "bB0"], 4 + t, "T", gb=ln1_gb, gbn="ln1_gb", destT=h1T, dname="h1T", t=t,
                                   tb=bB1h, tbn="bB1", xbh=[(relu_t[0][:], "relu_t0"), (relu_t[1][:], "relu_t1")])
                yield
            for ub in range(16):
                (wb_, wn_), = WB.take()
                for hl in range(2):
                    ff = 2 * ub + hl
                    bi_ = next_bA()
                    mm8(bA[bi_], f"bA{bi_}", v8(wb_), wn_, 128 * hl, h1T, "h1T", W)
                    rt = relu_t[ff % 2]
                    AC(lambda e, rt=rt, bi_=bi_: e.activation(out=rt[:], in_=bA[bi_][:, 0:W], func=AF.Relu), x=[f"bA{bi_}"], w=[f"relu_t{ff % 2}"])
                    AC(lambda e, rt=rt, ff=ff: e.activation(out=aT[:, ff, :], in_=rt[:], func=AF.Square), r=[f"relu_t{ff % 2}"], w=["aT"])
                    yield
            for dm in range(8):
                (w0, n0), (w1, n1) = WB.take(2)
                bi_ = next_bA()

                def fnd(e, w0=w0, w1=w1, dm=dm, bi_=bi_):
                    for k in range(32):
                        wv_ = v16(w0) if k < 16 else v16(w1)
                        e.matmul(bA[bi_][:, 0:W], lhsT=wv_[:, k % 16, :], rhs=aT[:, k, :], start=(k == 0), stop=False)
                    return e.matmul(bA[bi_][:, 0:W], lhsT=aI[:], rhs=h1T[:, dm, :], start=False, stop=True)
                PE(fnd, r=[n0, n1, "aT", "h1T", "aI"], w=[f"bA{bi_}"])
                AC(lambda e, dm=dm, bi_=bi_: e.activation(out=rT[:, dm, :], in_=bA[bi_][:, 0:W], func=AF.Copy), x=[f"bA{bi_}"], w=["rT"])
                yield
            for t in range(NT):
                rT_to_tokmajor(t)
                r0 = st_idx * W + t * 128
                yield from ln_tile([bB0h[:, 0:512], bB0h[:, 512:1024]], [], ["bB0", "bB0"], 4 + t, "O", out_rows=(r0, r0 + 128))
                yield

        def run(g):
            for _ in g:
                pass

        def interleave(gb, ga, a_per_b):
            done_a = done_b = False
            acc = 0.0
            na = nb = 0
            while not (done_a and done_b):
                if not done_b:
                    try:
                        next(gb)
                        nb += 1
                    except StopIteration:
                        done_b = True
                acc += a_per_b
                while (acc >= 1.0 or done_b) and not done_a:
                    acc -= 1.0
                    try:
                        next(ga)
                        na += 1
                    except StopIteration:
                        done_a = True
                if done_a:
                    acc = 0.0
            return na, nb

        def chain(*gs):
            for g in gs:
                if g is not None:
                    yield from g

        def spread(gmain, gextra, every):
            n = 0
            extra_live = gextra is not None
            for _ in gmain:
                yield
                n += 1
                if extra_live and n % every == 0:
                    try:
                        next(gextra)
                        yield
                    except StopIteration:
                        extra_live = False
            if extra_live:
                for _ in gextra:
                    yield

        ratio = [A_PER_B]
        run(gen_Apre(-1))
        run(gen_Amain(-1))
        run(gen_Apre(0))
        run(gen_Amain(0))
        for s in range(NST):
            run(gen_E(s))
            if s + 1 < NST:
                a_stream = chain(gen_Apre(s + 1), gen_Amain(s + 1))
                if INTERLEAVE:
                    na_, nb_ = interleave(gen_B(s), a_stream, ratio[0])
                    ratio[0] = na_ / max(nb_, 1)
                else:
                    run(gen_B(s))
                    run(a_stream)
            else:
                run(gen_B(s))
        P.emit(final_wait_ops=out_dma_ops)
    return nc


_NC_CACHE = {}


def _prep_inputs(inp):
    f = lambda a: np.ascontiguousarray(np.asarray(a, dtype=np.float32))
    pk = lambda v: np.ascontiguousarray(f(v).reshape(-1, 128).T)
    shared = {}
    shared["meta"] = f(inp["meta_tokens"])
    shared["w_in"] = f(inp["w_in"][0])
    shared["w_out"] = f(inp["w_out"][0])
    shared["w_up"] = f(inp["w_up"][0])
    shared["w_down"] = f(inp["w_down"][0])
    shared["w_glu"] = f(inp["s5_w_glu"][0])
    shared["lnin_gb"] = np.ascontiguousarray(np.concatenate([pk(inp["ln_in_g"]), pk(inp["ln_in_b"])], axis=1))
    shared["ln1_gb"] = np.ascontiguousarray(np.concatenate([pk(inp["ln1_g"][0]), pk(inp["ln1_b"][0])], axis=1))
    shared["ln2_g"] = f(inp["ln2_g"][0]).reshape(1, D)
    shared["ln2_b"] = f(inp["ln2_b"][0]).reshape(1, D)
    st = lambda a: np.ascontiguousarray(f(a).reshape(8, 128).T)
    ldt = np.repeat(f(inp["s5_log_dt"][0])[:, None], 64, axis=1)
    shared["s5_sc"] = np.ascontiguousarray(np.concatenate([st(inp["s5_lambda_re"][0]), st(inp["s5_lambda_im"][0]), st(ldt)], axis=1))

    def bl(a):
        return f(a).reshape(8, 2, 64, 16).transpose(1, 2, 0, 3).reshape(128, 8, 16)
    shared["s5_b"] = np.ascontiguousarray(np.stack([bl(inp["s5_b_re"][0]), bl(inp["s5_b_im"][0])], axis=1))
    cl = lambda a: bl(f(a).transpose(0, 2, 1))
    shared["s5_c"] = np.ascontiguousarray(np.stack([cl(inp["s5_c_re"][0]), cl(inp["s5_c_im"][0])], axis=1))
    shared["s5_fm"] = np.ascontiguousarray(np.concatenate([pk(inp["s5_d"][0]), pk(inp["s5_b_glu"][0])], axis=1))
    shared["gn_gb"] = np.ascontiguousarray(np.concatenate([pk(inp["ret_gn_g"][0]), pk(inp["ret_gn_b"][0])], axis=1))
    for k in ("ident_f", "pswap", "cosT", "sinT", "dmatT", "xi_bc", "zeta", "maskB"):
        shared[k] = CONST[k]
    x = f(inp["x"])
    maps = []
    for c in range(8):
        m = dict(shared)
        m["x"] = x[c]
        maps.append(m)
    return maps


def kernel(**inputs):
    if "nc" not in _NC_CACHE:
        _NC_CACHE["nc"] = build()
    nc = _NC_CACHE["nc"]
    maps = _prep_inputs(inputs)
    res = run_bass_kernel_spmd(nc, maps, core_ids=list(range(8)))
    out = np.stack([np.asarray(r["out"], dtype=np.float32) for r in res.results], axis=0)
    return out
```

```python
import math
import numpy as np
import concourse.bass as bass
import concourse.mybir as mybir
from concourse.bass_utils import run_bass_kernel_spmd
from contextlib import ExitStack

F32 = mybir.dt.float32
BF16 = mybir.dt.bfloat16
ALU = mybir.AluOpType
AF = mybir.ActivationFunctionType

D = 1024
SEQ = 4096
NMETA = 16
NH = 6
NT = 4
W = NT * 128
NST = SEQ // W
SB = 128
ALPHA = 2.0 ** 0.25
LN_EPS = 1e-5
NSLOT = 33 * 128
ENG_NAMES = ("pe", "act", "dve", "pool", "sp")
STRICT_SYNC = False
INTERLEAVE = True
A_PER_B = 2.6
LAG = 3
PRE_EVERY = 12


class Prog:
    def __init__(self, nc, n_dma_sems=12):
        self.nc = nc
        self.ops = []
        self.res_w = {}
        self.res_r = {}
        self.cnt = {e: 0 for e in ENG_NAMES}
        self.n_dma_sems = n_dma_sems
        self.dma_cnt = {}
        self.dma_rr = {"sp": 0, "pool": 0, "act": 0}
        self.dma_last = {}

    def _deps(self, reads, writes, excl):
        deps = {}
        for r in list(reads) + list(excl):
            if r in self.res_w:
                deps[self.res_w[r]] = True
        for w in list(writes):
            if w in self.res_w:
                deps.setdefault(self.res_w[w], False)
            for rd in self.res_r.get(w, ()):
                deps.setdefault(rd, False)
        for w in excl:
            for rd in self.res_r.get(w, ()):
                deps.setdefault(rd, False)
        return deps

    def _commit(self, oid, reads, writes, excl):
        for r in list(reads) + list(excl):
            self.res_r.setdefault(r, []).append(oid)
        for w in list(writes):
            self.res_w[w] = oid
            self.res_r[w] = []

    def op(self, eng, fn, reads=(), writes=(), excl=()):
        oid = len(self.ops)
        deps = self._deps(reads, writes, excl)
        self.cnt[eng] += 1
        self.ops.append(dict(eng=eng, fn=fn, deps=deps, tok=("E", eng, self.cnt[eng]), dma=False))
        self._commit(oid, reads, writes, excl)
        return oid

    def dma(self, queue, fn, reads=(), writes=()):
        oid = len(self.ops)
        deps = self._deps(reads, writes, ())
        slot = self.dma_rr[queue]
        self.dma_rr[queue] = (slot + 1) % self.n_dma_sems
        key = (queue, slot)
        if key in self.dma_last:
            deps[self.dma_last[key]] = True
        self.dma_cnt[key] = self.dma_cnt.get(key, 0) + 1
        self.dma_last[key] = oid
        self.ops.append(dict(eng=queue, fn=fn, deps=deps, tok=("D", key, 16 * self.dma_cnt[key]), dma=True))
        self._commit(oid, reads, writes, ())
        return oid

    def emit(self, final_wait_ops=()):
        nc = self.nc
        with ExitStack() as es:
            esem = {e: es.enter_context(nc.semaphore(f"s_{e}")) for e in ENG_NAMES}
            dsem = {}
            for q in ("sp", "pool", "act"):
                for s in range(self.n_dma_sems):
                    if (q, s) in self.dma_cnt:
                        dsem[(q, s)] = es.enter_context(nc.semaphore(f"d_{q}{s}"))
            block = es.enter_context(nc.Block())

            def semval(tok):
                if tok[0] == "E":
                    return esem[tok[1]], tok[2]
                return dsem[tok[1]], tok[2]

            def run_engine(ename, eobj):
                known = {}
                for oid, o in enumerate(self.ops):
                    if o["eng"] != ename:
                        continue
                    for d in sorted(o["deps"]):
                        do = self.ops[d]
                        if (not o["dma"]) and (not do["dma"]) and do["eng"] == ename:
                            if ename == "pe" or not (o["deps"][d] or STRICT_SYNC):
                                continue
                        sem, val = semval(do["tok"])
                        k = id(sem)
                        if known.get(k, 0) >= val:
                            continue
                        eobj.wait_ge(sem, val)
                        known[k] = val
                    ins = o["fn"](eobj)
                    sem, val = semval(o["tok"])
                    ins.then_inc(sem, 16 if o["dma"] else 1)
                if ename == "sp":
                    for oid in final_wait_ops:
                        sem, val = semval(self.ops[oid]["tok"])
                        eobj.wait_ge(sem, val)

            @block.tensor
            def _(e):
                run_engine("pe", e)

            @block.scalar
            def _(e):
                run_engine("act", e)

            @block.vector
            def _(e):
                run_engine("dve", e)

            @block.gpsimd
            def _(e):
                run_engine("pool", e)

            @block.sync
            def _(e):
                run_engine("sp", e)


def _host_consts():
    c = {}
    c["ident_f"] = np.eye(128, dtype=np.float32)
    pm = np.zeros((128, 128), np.float32)
    for dp in range(128):
        pm[(dp + 64) % 128, dp] = 1.0
    c["pswap"] = pm
    pos = (np.arange(NSLOT, dtype=np.float32) - 112.0).astype(np.float32)
    inv_freq = (1.0 / (10000.0 ** (np.arange(0, 128, 2, dtype=np.float32) / 128.0))).astype(np.float32)
    ang = (pos[:, None] * inv_freq[None, :]).astype(np.float32)
    cs, sn = np.cos(ang).astype(np.float32), np.sin(ang).astype(np.float32)
    c["cosT"] = np.ascontiguousarray(np.concatenate([cs, cs], axis=1).T)
    c["sinT"] = np.ascontiguousarray(np.concatenate([-sn, sn], axis=1).T)
    lg = np.log1p(-np.exp2(-5.0 - np.arange(NH, dtype=np.float32))).astype(np.float32)
    idx = np.arange(128, dtype=np.float32)
    scale = 128.0 ** -0.5
    diff = idx[None, :] - idx[:, None]
    dm = np.where(diff[None] >= 0, np.exp(np.maximum(diff, 0.0)[None] * lg[:, None, None]), 0.0) * scale
    c["dmatT"] = np.ascontiguousarray(dm.transpose(1, 0, 2)).astype(np.float32)
    xi = np.exp((idx + 1.0)[None] * lg[:, None]).astype(np.float32)
    c["xi_bc"] = np.ascontiguousarray(np.broadcast_to(xi[None], (128, NH, 128))).astype(np.float32)
    zeta = (np.exp((127.0 - idx)[None] * lg[:, None]) * scale).astype(np.float32)
    c["zeta"] = np.ascontiguousarray(zeta.T)
    c["gammaC"] = [float(np.exp(128.0 * lg[h])) for h in range(NH)]
    mb = np.zeros((128, 4, 8), np.float32)
    for gl in range(2):
        for j4 in range(4):
            mb[gl * 64:(gl + 1) * 64, j4, 2 * j4 + gl] = 1.0
    c["maskB"] = mb
    return c


CONST = _host_consts()


def build():
    nc = bass.Bass("TRN2", target_bir_lowering=False)

    def din(name, shape):
        return nc.dram_tensor(name, list(shape), F32, kind="ExternalInput").ap()

    x_d = din("x", [SEQ, D])
    meta_d = din("meta", [NMETA, D])
    w_in_d = din("w_in", [D, 3328])
    w_out_d = din("w_out", [D, D])
    w_up_d = din("w_up", [D, 4 * D])
    w_down_d = din("w_down", [4 * D, D])
    w_glu_d = din("w_glu", [256, 256])
    lnin_gb_d = din("lnin_gb", [128, 16])
    ln1_gb_d = din("ln1_gb", [128, 16])
    ln2_g_d = din("ln2_g", [1, D])
    ln2_b_d = din("ln2_b", [1, D])
    s5_sc_d = din("s5_sc", [128, 24])
    s5_b_d = din("s5_b", [128, 2, 8, 16])
    s5_c_d = din("s5_c", [128, 2, 8, 16])
    s5_fm_d = din("s5_fm", [128, 4])
    gn_gb_d = din("gn_gb", [128, 12])
    ident_d = din("ident_f", [128, 128])
    pswap_d = din("pswap", [128, 128])
    cosT_d = din("cosT", [128, NSLOT])
    sinT_d = din("sinT", [128, NSLOT])
    dmatT_d = din("dmatT", [128, NH, 128])
    xibc_d = din("xi_bc", [128, NH, 128])
    zeta_d = din("zeta", [128, NH])
    maskB_d = din("maskB", [128, 4, 8])
    out_d = nc.dram_tensor("out", [SEQ, D], F32, kind="ExternalOutput").ap()

    with ExitStack() as es:
        def sb(name, shape, dt=F32):
            return es.enter_context(nc.sbuf_tensor(name, list(shape), dt))

        def psum(name, shape, dt=F32):
            return es.enter_context(nc.psum_tensor(name, list(shape), dt))

        xs = [sb(f"xs{i}", [128, D]) for i in range(2)]
        xnb = [sb(f"xnb{i}", [128, D], BF16) for i in range(2)]
        tmpT = sb("tmpT", [128, 8, 128], BF16)
        hTs = [sb(f"hT{i}", [128, 8, W], BF16) for i in range(2)]
        h1T = sb("h1T", [128, 8, W], BF16)
        wA = [sb(f"wA{i}", [128, 2048], BF16) for i in range(3)]
        wB = [sb(f"wB{i}", [128, 2048], BF16) for i in range(4)]
        uT = sb("uT", [128, 2, W], BF16)
        q_pre = sb("q_pre", [128, W], BF16)
        k_pre = sb("k_pre", [128, W], BF16)
        qT = sb("qT", [128, W], BF16)
        kT = sb("kT", [128, W], BF16)
        qxT = sb("qxT", [128, W], BF16)
        sgT = sb("sgT", [128, 2, W], BF16)
        v_sb = sb("v_sb", [128, NT, 768], BF16)
        yT = sb("yT", [128, 8, W], BF16)
        aT = sb("aT", [128, 32, W], BF16)
        relu_t = [sb(f"relu_t{i}", [128, W], BF16) for i in range(2)]
        rT = sb("rT", [128, 8, W], BF16)
        cos_sb = sb("cos_sb", [128, W])
        sin_sb = sb("sin_sb", [128, W])
        ln2g = sb("ln2g", [128, D])
        ln2b = sb("ln2b", [128, D])
        ident_f = sb("ident_fs", [128, 128])
        ident_b = sb("ident_b", [128, 128], BF16)
        aI = sb("aI", [128, 128], BF16)
        pswap = sb("pswap_s", [128, 128], BF16)
        dmatT = sb("dmatT_s", [128, NH, 128])
        xibc = sb("xibc_s", [128, NH, 128])
        zeta = sb("zeta_s", [128, NH])
        maskB = sb("maskB_s", [128, 4, 8])
        lnin_gb = sb("lnin_gb_s", [128, 16])
        ln1_gb = sb("ln1_gb_s", [128, 16])
        gn_gb = sb("gn_gb_s", [128, 12])
        s5_fm = sb("s5_fm_s", [128, 4])
        epsb = sb("epsb", [128, 1])
        stt = sb("stt", [128, 8, 12])
        mv = sb("mv", [128, 8, 8])
        scT_sb = sb("scT_sb", [128, NT, 128], BF16)
        kz_sb = sb("kz_sb", [128, NT, 128], BF16)
        on_sb = sb("on_sb", [128, NT, 128], BF16)
        gaff = sb("gaff", [128, W], BF16)
        S32 = sb("S32", [128, NH, 128])
        Sall = sb("Sall", [128, 3, 128])
        Sbf_all = sb("Sbf_all", [128, NT, 128], BF16)
        gst = sb("gst", [128, NT, 6])
        gmv = sb("gmv", [128, NT, 2])
        gsc = sb("gsc", [128, 3, NT])
        s5_sc = sb("s5_sc_s", [128, 24])
        s5w = sb("s5w", [128, 40, 8])
        WBT = sb("WBT", [128, 16, 128], BF16)
        CT = sb("CT", [128, 16, 128], BF16)
        Etab = sb("Etab", [128, 2, 8, SB])
        s5m = sb("s5m", [128, 4, 4 * SB])
        s5p = sb("s5p", [128, 4, 4 * SB])
        s5x = [sb(f"s5x{i}", [128, 2, 4 * SB], BF16) for i in range(2)]
        Xc = sb("Xc", [128, 2, 8])
        ypre = sb("ypre", [128, W])
        ygb = sb("ygb", [128, 2, W], BF16)
        sgl = sb("sgl", [128, W], BF16)
        wglu = sb("wglu", [128, 2, 256], BF16)
        rope_t1 = s5p[:, 0, :]
        rope_t2 = s5p[:, 1, :]
        PALL = ["s5p0", "s5p1", "s5p2", "s5p3"]
        pfl = s5p[:].rearrange("p a w -> p (a w)")
        s5_b = pfl[:, 0:256].rearrange("p (r j h) -> p r j h", r=2, j=8)
        s5_c = pfl[:, 256:512].rearrange("p (r j h) -> p r j h", r=2, j=8)
        s5bb = pfl[:, 512:768].rearrange("p (r j h) -> p r j h", r=2, j=8)
        s5ex = pfl[:, 768:896].rearrange("p (g h) -> p g h", g=8)
        s5tmp = pfl[:, 1024:1536].rearrange("p (x j h) -> p x j h", x=4, j=8)

        bA = [psum(f"bA{i}", [128, 512]) for i in range(2)]
        bB = [psum(f"bB{i}", [128, 512]) for i in range(2)]
        bT = psum("bT", [128, 1024], BF16)
        X = [psum(f"X{i}", [128, 512]) for i in range(3)]
        bB0h = bB[0][:].bitcast(BF16)
        bB1h = bB[1][:].bitcast(BF16)

        P = Prog(nc)
        Etmp = aT[:, 0:16, :].bitcast(F32).rearrange("p a b -> p (a b)").rearrange("p (x j n) -> p x j n", x=4, j=8)
        DV = lambda fn, r=(), w=(), x=(): P.op("dve", fn, r, w, x)
        AC = lambda fn, r=(), w=(), x=(): P.op("act", fn, r, w, x)
        PE = lambda fn, r=(), w=(): P.op("pe", fn, r, w)

        def ld(dst, src, name, q="sp"):
            P.dma(q, lambda e: e.dma_start(out=dst, in_=src), writes=name if isinstance(name, list) else [name])

        ld(ident_f[:], ident_d, "ident_f")
        ld(dmatT[:], dmatT_d, "dmatT")
        ld(zeta[:], zeta_d, "zeta")
        ld(maskB[:], maskB_d, "maskB")
        ld(lnin_gb[:], lnin_gb_d, "lnin_gb")
        ld(ln1_gb[:], ln1_gb_d, "ln1_gb")
        ld(gn_gb[:], gn_gb_d, "gn_gb")
        ld(s5_fm[:], s5_fm_d, "s5_fm")
        ld(s5_sc[:], s5_sc_d, "s5_sc")
        ld(s5_b, s5_b_d, PALL)
        ld(s5_c, s5_c_d, PALL)
        ld(ln2g[:], ln2_g_d[0].partition_broadcast(128), "ln2g")
        ld(ln2b[:], ln2_b_d[0].partition_broadcast(128), "ln2b")
        ld(pswap[:], pswap_d, "pswap", q="pool")
        ld(xibc[:], xibc_d, "xibc")
        ld(wglu[:], w_glu_d.rearrange("(k p) c -> p k c", p=128), "wglu", q="pool")
        DV(lambda e: e.memset(epsb[:], LN_EPS), w=["epsb"])
        AC(lambda e: e.activation(out=ident_b[:], in_=ident_f[:], func=AF.Copy), r=["ident_f"], w=["ident_b"])
        AC(lambda e: e.activation(out=aI[:], in_=ident_f[:], func=AF.Identity, scale=ALPHA), r=["ident_f"], w=["aI"])
        DV(lambda e: e.memset(S32[:], 0.0), w=[f"S32{h}" for h in range(NH)])
        DV(lambda e: e.memset(Xc[:], 0.0), w=["Xc"])

        SL = lambda i: s5w[:, i, :]
        lam_re, lam_im, log_dt = s5_sc[:, 0:8], s5_sc[:, 8:16], s5_sc[:, 16:24]
        (I_DT, I_AR, I_TH, I_RHO, I_PHI, I_U, I_S, I_C, I_TS, I_TC, I_CC, I_SS, I_CS, I_N, I_RN,
         I_LBR, I_LBI, I_DEN, I_NR, I_QR, I_QI, I_T1, I_T2, I_NS) = range(24)

        def dv2(out, a, b, op):
            DV(lambda e: e.tensor_tensor(out=out, in0=a, in1=b, op=op), r=["s5w", "s5_sc"], w=["s5w"])

        AC(lambda e: e.activation(out=SL(I_DT), in_=log_dt, func=AF.Exp), r=["s5_sc"], w=["s5w"])
        dv2(SL(I_AR), lam_re, SL(I_DT), ALU.mult)
        dv2(SL(I_TH), lam_im, SL(I_DT), ALU.mult)
        AC(lambda e: e.activation(out=SL(I_RHO), in_=SL(I_AR), func=AF.Exp), r=["s5w"], w=["s5w"])
        DV(lambda e: e.tensor_scalar(out=SL(I_PHI), in0=SL(I_TH), scalar1=1.0 / 32.0, scalar2=None, op0=ALU.mult), r=["s5w"], w=["s5w"])
        dv2(SL(I_U), SL(I_PHI), SL(I_PHI), ALU.mult)
        DV(lambda e: e.tensor_copy(out=SL(I_S), in_=SL(I_PHI)), r=["s5w"], w=["s5w"])
        DV(lambda e: e.tensor_copy(out=SL(I_TS), in_=SL(I_PHI)), r=["s5w"], w=["s5w"])
        DV(lambda e: e.memset(SL(I_C), 1.0), r=["s5w"], w=["s5w"])
        DV(lambda e: e.memset(SL(I_TC), 1.0), r=["s5w"], w=["s5w"])
        for kk in range(1, 7):
            cs_ = -1.0 / ((2 * kk) * (2 * kk + 1))
            cc_ = -1.0 / ((2 * kk - 1) * (2 * kk))
            DV(lambda e, c_=cs_: e.scalar_tensor_tensor(out=SL(I_TS), in0=SL(I_TS), scalar=c_, in1=SL(I_U), op0=ALU.mult, op1=ALU.mult), r=["s5w"], w=["s5w"])
            dv2(SL(I_S), SL(I_S), SL(I_TS), ALU.add)
            DV(lambda e, c_=cc_: e.scalar_tensor_tensor(out=SL(I_TC), in0=SL(I_TC), scalar=c_, in1=SL(I_U), op0=ALU.mult, op1=ALU.mult), r=["s5w"], w=["s5w"])
            dv2(SL(I_C), SL(I_C), SL(I_TC), ALU.add)
        for _ in range(5):
            dv2(SL(I_CC), SL(I_C), SL(I_C), ALU.mult)
            dv2(SL(I_SS), SL(I_S), SL(I_S), ALU.mult)
            dv2(SL(I_CS), SL(I_C), SL(I_S), ALU.mult)
            dv2(SL(I_C), SL(I_CC), SL(I_SS), ALU.subtract)
            DV(lambda e: e.tensor_scalar(out=SL(I_S), in0=SL(I_CS), scalar1=2.0, scalar2=None, op0=ALU.mult), r=["s5w"], w=["s5w"])
        dv2(SL(I_CC), SL(I_C), SL(I_C), ALU.mult)
        dv2(SL(I_SS), SL(I_S), SL(I_S), ALU.mult)
        dv2(SL(I_N), SL(I_CC), SL(I_SS), ALU.add)
        AC(lambda e: e.activation(out=SL(I_NS), in_=SL(I_N), func=AF.Sqrt), r=["s5w"], w=["s5w"])
        DV(lambda e: e.reciprocal(out=SL(I_RN), in_=SL(I_NS)), r=["s5w"], w=["s5w"])
        dv2(SL(I_C), SL(I_C), SL(I_RN), ALU.mult)
        dv2(SL(I_S), SL(I_S), SL(I_RN), ALU.mult)
        dv2(SL(I_LBR), SL(I_RHO), SL(I_C), ALU.mult)
        dv2(SL(I_LBI), SL(I_RHO), SL(I_S), ALU.mult)
        dv2(SL(I_T1), lam_re, lam_re, ALU.mult)
        dv2(SL(I_T2), lam_im, lam_im, ALU.mult)
        dv2(SL(I_DEN), SL(I_T1), SL(I_T2), ALU.add)
        DV(lambda e: e.reciprocal(out=SL(I_DEN), in_=SL(I_DEN)), r=["s5w"], w=["s5w"])
        DV(lambda e: e.tensor_scalar(out=SL(I_NR), in0=SL(I_LBR), scalar1=-1.0, scalar2=None, op0=ALU.add), r=["s5w"], w=["s5w"])
        dv2(SL(I_T1), SL(I_NR), lam_re, ALU.mult)
        dv2(SL(I_T2), SL(I_LBI), lam_im, ALU.mult)
        dv2(SL(I_QR), SL(I_T1), SL(I_T2), ALU.add)
        dv2(SL(I_QR), SL(I_QR), SL(I_DEN), ALU.mult)
        dv2(SL(I_T1), SL(I_LBI), lam_re, ALU.mult)
        dv2(SL(I_T2), SL(I_NR), lam_im, ALU.mult)
        dv2(SL(I_QI), SL(I_T1), SL(I_T2), ALU.subtract)
        dv2(SL(I_QI), SL(I_QI), SL(I_DEN), ALU.mult)
        qr_bc = SL(I_QR).unsqueeze(2).broadcast_to([128, 8, 16])
        qi_bc = SL(I_QI).unsqueeze(2).broadcast_to([128, 8, 16])

        def bb(out, a, b_, op):
            DV(lambda e: e.tensor_tensor(out=out, in0=a, in1=b_, op=op), r=["s5w"] + PALL, w=PALL)

        bb(s5tmp[:, 0], s5_b[:, 0], qr_bc, ALU.mult)
        bb(s5tmp[:, 1], s5_b[:, 1], qi_bc, ALU.mult)
        bb(s5tmp[:, 2], s5_b[:, 1], qr_bc, ALU.mult)
        bb(s5tmp[:, 3], s5_b[:, 0], qi_bc, ALU.mult)
        bb(s5bb[:, 0], s5tmp[:, 0], s5tmp[:, 1], ALU.subtract)
        bb(s5bb[:, 1], s5tmp[:, 2], s5tmp[:, 3], ALU.add)
        for j in range(8):
            mk = maskB[:, j % 4, :].unsqueeze(2).broadcast_to([128, 8, 16])
            for ri in range(2):
                src = s5bb[:, ri, j, :].unsqueeze(1).broadcast_to([128, 8, 16])
                DV(lambda e, src=src, mk=mk: e.tensor_tensor(out=s5ex, in0=src, in1=mk, op=ALU.mult), r=PALL + ["maskB"], w=PALL)
                PE(lambda e: e.transpose(X[0][:, 0:128], s5ex.rearrange("p g h -> p (g h)"), ident_f[:]), r=PALL + ["ident_f"], w=["X0"])
                AC(lambda e, j=j, ri=ri: e.activation(out=WBT[:, 2 * j + ri, :], in_=X[0][:, 0:128], func=AF.Copy), x=["X0"], w=["WBT"])
                csrc = s5_c[:, ri, j, :].unsqueeze(1).broadcast_to([128, 8, 16])
                sgn = 1.0 if ri == 0 else -1.0
                DV(lambda e, csrc=csrc, mk=mk, j=j, ri=ri, sgn=sgn: e.scalar_tensor_tensor(
                    out=CT[:, 2 * j + ri, :].rearrange("p (g h) -> p g h", g=8), in0=csrc, scalar=sgn, in1=mk,
                    op0=ALU.mult, op1=ALU.mult), r=PALL + ["maskB"], w=["CT"])
        DV(lambda e: e.tensor_copy(out=Etab[:, 0, :, 0], in_=SL(I_C)), r=["s5w"], w=["Etab"])
        DV(lambda e: e.tensor_scalar(out=Etab[:, 1, :, 0], in0=SL(I_S), scalar1=-1.0, scalar2=None, op0=ALU.mult), r=["s5w"], w=["Etab"])
        n = 1
        while n < SB:
            ar = Etab[:, 0, :, 0:n]
            ai = Etab[:, 1, :, 0:n]
            br = Etab[:, 0, :, n - 1:n].broadcast_to([128, 8, n])
            bi = Etab[:, 1, :, n - 1:n].broadcast_to([128, 8, n])
            t_ = [Etmp[:, i, :, 0:n] for i in range(4)]
            for (o_, a_, b_) in ((t_[0], ar, br), (t_[1], ai, bi), (t_[2], ar, bi), (t_[3], ai, br)):
                DV(lambda e, o_=o_, a_=a_, b_=b_: e.tensor_tensor(out=o_, in0=a_, in1=b_, op=ALU.mult), r=["Etab", "aT"], w=["aT"])
            DV(lambda e, n=n, t_=t_: e.tensor_tensor(out=Etab[:, 0, :, n:2 * n], in0=t_[0], in1=t_[1], op=ALU.subtract), r=["aT"], w=["Etab"])
            DV(lambda e, n=n, t_=t_: e.tensor_tensor(out=Etab[:, 1, :, n:2 * n], in0=t_[2], in1=t_[3], op=ALU.add), r=["aT"], w=["Etab"])
            n *= 2

        class WStream:
            def __init__(self, bufs, names, plan):
                self.bufs, self.names, self.plan = bufs, names, plan
                self.issued, self.n, self.i = 0, len(bufs), 0

            def take(self, span=1):
                i = self.i
                while self.issued < min(len(self.plan), i + self.n):
                    b = self.issued % self.n
                    for (view_fn, src) in self.plan[self.issued]:
                        P.dma("pool", lambda e, d=view_fn(self.bufs[b]), s=src: e.dma_start(out=d, in_=s), writes=[self.names[b]])
                    self.issued += 1
                self.i += span
                return [(self.bufs[(i + s) % self.n], self.names[(i + s) % self.n]) for s in range(span)]

        v8 = lambda buf: buf[:].rearrange("p (k c) -> p k c", k=8)
        v16 = lambda buf: buf[:].rearrange("p (k c) -> p k c", k=16)

        def blk256(w_d, c0):
            return [(lambda buf: v8(buf), w_d[:, c0:c0 + 256].rearrange("(k p) c -> p k c", p=128))]

        planA = []
        for st_i in range(-1, NST):
            m_ = st_i < 0
            planA.append(blk256(w_in_d, 0))
            for hp in range(3):
                planA.append(blk256(w_in_d, 256 + 1536 + 256 * hp))
                if not m_:
                    planA.append(blk256(w_in_d, 256 + 2304 + 256 * hp))
                for hl in range(2):
                    h = 2 * hp + hl
                    planA.append([
                        (lambda buf: v8(buf)[:, :, 0:128], w_in_d[:, 256 + 128 * h:256 + 128 * h + 128].rearrange("(k p) c -> p k c", p=128)),
                        (lambda buf: v8(buf)[:, :, 128:256], w_in_d[:, 1024 + 128 * h:1024 + 128 * h + 128].rearrange("(k p) c -> p k c", p=128)),
                    ])
        planB = []
        for st_i in range(NST):
            for cb in range(4):
                planB.append(blk256(w_out_d, 256 * cb))
            for ub in range(16):
                planB.append(blk256(w_up_d, 256 * ub))
            for dm in range(8):
                for half in range(2):
                    planB.append([(lambda buf: v16(buf),
                                   w_down_d[2048 * half:2048 * half + 2048, 128 * dm:128 * dm + 128].rearrange("(k p) c -> p k c", p=128))])
        WA = WStream(wA, [f"wA{i}" for i in range(3)], planA)
        WB = WStream(wB, [f"wB{i}" for i in range(4)], planB)

        bA_i = [0]

        def next_bA():
            i = bA_i[0] % 2
            bA_i[0] += 1
            return i

        X_i = [0]

        def next_X():
            i = X_i[0] % 3
            X_i[0] += 1
            return i

        def mm8(bank, bname, wv, wn, c0, actT, aname, wt, resid=None, rname=None):
            def fn(e):
                ins = None
                for k in range(8):
                    ins = e.matmul(bank[:, 0:wt], lhsT=wv[:, k, c0:c0 + 128], rhs=actT[:, k, 0:wt],
                                   start=(k == 0), stop=(k == 7 and resid is None))
                if resid is not None:
                    ins = e.matmul(bank[:, 0:wt], lhsT=aI[:], rhs=resid[:, 0:wt], start=False, stop=True)
                return ins
            PE(fn, r=[wn, aname] + (["aI", rname] if resid is not None else []), w=[bname])

        def ln_tile(src_halves, src_reads, src_excl, slot, out_kind, gb=None, gbn=None, destT=None, dname=None, t=0,
                    out_rows=None, tb=None, tbn=None, xbh=None):
            st = stt[:, slot, :]
            m = mv[:, slot, :]
            for hh in range(2):
                DV(lambda e, hh=hh: e.bn_stats(out=st[:, 6 * hh:6 * hh + 6], in_=src_halves[hh]),
                   r=src_reads, w=[f"stt{slot}"], x=[src_excl[hh]] if src_excl else [])
            DV(lambda e: e.bn_aggr(out=m[:, 0:2], in_=st), r=[f"stt{slot}"], w=[f"mv{slot}"])
            AC(lambda e: e.activation(out=m[:, 2:3], in_=m[:, 1:2], func=AF.Sqrt, bias=epsb[:], scale=1.0), r=[f"mv{slot}", "epsb"], w=[f"mv{slot}"])
            DV(lambda e: e.reciprocal(out=m[:, 3:4], in_=m[:, 2:3]), r=[f"mv{slot}"], w=[f"mv{slot}"])
            DV(lambda e: e.scalar_tensor_tensor(out=m[:, 4:5], in0=m[:, 0:1], scalar=-1.0, in1=m[:, 3:4], op0=ALU.mult, op1=ALU.mult), r=[f"mv{slot}"], w=[f"mv{slot}"])
            if out_kind == "T":
                if xbh is None:
                    xbh = [(xnb[slot % 2][:, 0:512], f"xnb{slot % 2}"), (xnb[slot % 2][:, 512:1024], f"xnb{slot % 2}")]
                for hh in range(2):
                    AC(lambda e, hh=hh: e.activation(out=xbh[hh][0], in_=src_halves[hh], func=AF.Identity,
                                                     scale=m[:, 3:4], bias=m[:, 4:5]),
                       r=src_reads + [f"mv{slot}"], w=[xbh[hh][1]], x=[src_excl[hh]] if src_excl else [])
                for _ in range(LAG):
                    yield

                def trf(e):
                    ins = None
                    for k in range(8):
                        ins = e.transpose(tb[:, 128 * k:128 * k + 128], xbh[k // 4][0][:, 128 * (k % 4):128 * (k % 4) + 128], ident_b[:])
                    return ins
                PE(trf, r=[xbh[0][1], xbh[1][1], "ident_b"], w=[tbn])
                g_bc = gb[:, 0:8].unsqueeze(2).broadcast_to([128, 8, 128])
                b_bc = gb[:, 8:16].unsqueeze(2).broadcast_to([128, 8, 128])
                DV(lambda e: e.tensor_tensor(out=tmpT[:], in0=tb.rearrange("p (k c) -> p k c", k=8), in1=g_bc, op=ALU.mult),
                   r=[gbn], w=["tmpT"], x=[tbn])
                DV(lambda e: e.tensor_tensor(out=destT[:, :, 128 * t:128 * t + 128], in0=tmpT[:], in1=b_bc, op=ALU.add),
                   r=["tmpT", gbn], w=[dname])
            else:
                xo = xs[slot % 2]
                xon = f"xs{slot % 2}"
                for hh in range(2):
                    AC(lambda e, hh=hh: e.activation(out=xo[:, 512 * hh:512 * hh + 512], in_=src_halves[hh], func=AF.Identity,
                                                     scale=m[:, 3:4], bias=m[:, 4:5]),
                       r=src_reads + [f"mv{slot}"], w=[xon], x=[src_excl[hh]] if src_excl else [])
                DV(lambda e: e.tensor_tensor(out=xo[:], in0=xo[:], in1=ln2g[:], op=ALU.mult), r=[xon, "ln2g"], w=[xon])
                DV(lambda e: e.tensor_tensor(out=xo[:], in0=xo[:], in1=ln2b[:], op=ALU.add), r=[xon, "ln2b"], w=[xon])
                out_dma_ops.append(P.dma("sp", lambda e: e.dma_start(out=out_d[out_rows[0]:out_rows[1], :], in_=xo[:]), reads=[xon]))

        def rT_to_tokmajor(t):
            def fn(e):
                ins = None
                for dm in range(8):
                    ins = e.transpose(bB0h[:, 128 * dm:128 * dm + 128], rT[:, dm, 128 * t:128 * t + 128], ident_b[:])
                return ins
            PE(fn, r=["rT", "ident_b"], w=["bB0"])

        out_dma_ops = []
        bT_all = bT[:]

        def s5_block(U, c0, meta, xi_):
            Er = Etab[:, 0, 4 * U:4 * U + 4, :].rearrange("p j t -> p (j t)")
            Ei = Etab[:, 1, 4 * U:4 * U + 4, :].rearrange("p j t -> p (j t)")
            Xre, Xim, XY = X[0], X[1], X[2]

            def mmbu(e):
                ins = None
                for j4 in range(4):
                    j = 4 * U + j4
                    e.matmul(Xre[:, 128 * j4:128 * j4 + 128], lhsT=WBT[:, 2 * j, :], rhs=uT[:, U, c0:c0 + 128], start=True, stop=True)
                    ins = e.matmul(Xim[:, 128 * j4:128 * j4 + 128], lhsT=WBT[:, 2 * j + 1, :], rhs=uT[:, U, c0:c0 + 128], start=True, stop=True)
                return ins
            PE(mmbu, r=["WBT", "uT"], w=["X0", "X1"])
            for (o_, a_, b_, xb_) in ((0, Xre, Er, "X0"), (1, Xim, Ei, "X1"), (2, Xim, Er, "X1"), (3, Xre, Ei, "X0")):
                DV(lambda e, o_=o_, a_=a_, b_=b_: e.tensor_tensor(out=s5m[:, o_, :], in0=a_[:], in1=b_, op=ALU.mult),
                   r=["Etab"], w=[f"s5m{o_}"], x=[xb_])
            DV(lambda e: e.tensor_tensor(out=s5m[:, 0, :], in0=s5m[:, 0, :], in1=s5m[:, 1, :], op=ALU.subtract), r=["s5m0", "s5m1"], w=["s5m0"])
            DV(lambda e: e.tensor_tensor(out=s5m[:, 2, :], in0=s5m[:, 2, :], in1=s5m[:, 3, :], op=ALU.add), r=["s5m2", "s5m3"], w=["s5m2"])
            yield
            for j4 in range(4):
                j = 4 * U + j4
                bs = slice(128 * j4, 128 * j4 + 128)
                for (ri, si, so) in ((0, 0, 1), (1, 2, 3)):
                    DV(lambda e, ri=ri, si=si, so=so, j=j, bs=bs: e.tensor_tensor_scan(
                        out=s5m[:, so, bs], data0=s5w[:, I_RHO, j:j + 1].broadcast_to([128, 128]), data1=s5m[:, si, bs],
                        initial=Xc[:, ri, j:j + 1], op0=ALU.mult, op1=ALU.add),
                       r=["s5w", f"s5m{si}", "Xc"], w=[f"s5m{so}"])
            for (o_, zi_, e_) in ((0, 1, Er), (1, 3, Ei), (2, 3, Er), (3, 1, Ei)):
                DV(lambda e, o_=o_, zi_=zi_, e_=e_: e.tensor_tensor(out=s5p[:, o_, :], in0=s5m[:, zi_, :], in1=e_, op=ALU.mult),
                   r=["Etab", f"s5m{zi_}"], w=[f"s5p{o_}"])
            pv = lambda o_: s5p[:, o_, :].rearrange("p (j t) -> p j t", j=4)[:, :, 127]
            DV(lambda e: e.tensor_tensor(out=Xc[:, 0, 4 * U:4 * U + 4], in0=pv(0), in1=pv(1), op=ALU.add), r=["s5p0", "s5p1"], w=["Xc"])
            DV(lambda e: e.tensor_tensor(out=Xc[:, 1, 4 * U:4 * U + 4], in0=pv(2), in1=pv(3), op=ALU.subtract), r=["s5p2", "s5p3"], w=["Xc"])
            if meta:
                return
            xx = s5x[xi_]
            xn = f"s5x{xi_}"
            DV(lambda e: e.tensor_tensor(out=xx[:, 0, :], in0=s5p[:, 0, :], in1=s5p[:, 1, :], op=ALU.add), r=["s5p0", "s5p1"], w=[xn])
            DV(lambda e: e.tensor_tensor(out=xx[:, 1, :], in0=s5p[:, 2, :], in1=s5p[:, 3, :], op=ALU.subtract), r=["s5p2", "s5p3"], w=[xn])
            for _ in range(LAG):
                yield

            def mmy(e):
                ins = None
                for j4 in range(4):
                    j = 4 * U + j4
                    e.matmul(XY[:, 0:128], lhsT=CT[:, 2 * j, :], rhs=xx[:, 0, 128 * j4:128 * j4 + 128], start=(j4 == 0), stop=False)
                    ins = e.matmul(XY[:, 0:128], lhsT=CT[:, 2 * j + 1, :], rhs=xx[:, 1, 128 * j4:128 * j4 + 128], start=False, stop=(j4 == 3))
                return ins
            PE(mmy, r=["CT", xn], w=["X2"])
            DV(lambda e: e.scalar_tensor_tensor(out=ypre[:, c0:c0 + 128], in0=uT[:, U, c0:c0 + 128], scalar=s5_fm[:, U:U + 1],
                                                in1=XY[:, 0:128], op0=ALU.mult, op1=ALU.add),
               r=["uT", "s5_fm"], w=["ypre"], x=["X2"])
            yield

        def head(hp, hl, meta, nt, wt, hT, hTn):
            h = 2 * hp + hl
            (wb_, wn_), = WA.take()
            wv_ = v8(wb_)
            todo = [("k", 128, k_pre, kT)] + ([] if meta else [("q", 0, q_pre, qT)])
            for (nm, coff, pre, dst) in todo:
                xi = next_X()
                mm8(X[xi], f"X{xi}", wv_, wn_, coff, hT, hTn, wt)
                AC(lambda e, pre=pre, xi=xi: e.activation(out=pre[:, 0:wt], in_=X[xi][:, 0:wt], func=AF.Copy), x=[f"X{xi}"], w=[nm + "_pre"])
                xj = next_X()
                PE(lambda e, pre=pre, xj=xj: e.matmul(X[xj][:, 0:wt], lhsT=pswap[:], rhs=pre[:, 0:wt], start=True, stop=True),
                   r=[nm + "_pre", "pswap"], w=[f"X{xj}"])
                DV(lambda e, pre=pre: e.tensor_tensor(out=rope_t1[:, 0:wt], in0=pre[:, 0:wt], in1=cos_sb[:, 0:wt], op=ALU.mult),
                   r=[nm + "_pre", "cos_sb"], w=["s5p0"])
                DV(lambda e, xj=xj: e.tensor_tensor(out=rope_t2[:, 0:wt], in0=X[xj][:, 0:wt], in1=sin_sb[:, 0:wt], op=ALU.mult),
                   r=["sin_sb"], w=["s5p1"], x=[f"X{xj}"])
                DV(lambda e, dst=dst: e.tensor_tensor(out=dst[:, 0:wt], in0=rope_t1[:, 0:wt], in1=rope_t2[:, 0:wt], op=ALU.add),
                   r=["s5p0", "s5p1"], w=[nm + "T"])
                yield
            for _ in range(LAG - 1):
                yield
            if not meta:
                DV(lambda e: e.tensor_tensor(out=qxT[:, 0:wt].rearrange("p (t i) -> p t i", t=nt), in0=qT[:, 0:wt].rearrange("p (t i) -> p t i", t=nt),
                                             in1=xibc[:, h, :].unsqueeze(1).broadcast_to([128, nt, 128]), op=ALU.mult),
                   r=["qT", "xibc"], w=["qxT"])

                def p1(e):
                    ins = None
                    for t in range(nt):
                        ins = e.matmul(X[0][:, 128 * t:128 * t + 128], lhsT=kT[:, 128 * t:128 * t + 128], rhs=qT[:, 128 * t:128 * t + 128], start=True, stop=True)
                    return ins
                PE(p1, r=["kT", "qT"], w=["X0"])
                DV(lambda e: e.tensor_tensor(out=scT_sb[:, 0:nt, :], in0=X[0][:, 0:wt].rearrange("p (t i) -> p t i", t=nt),
                                             in1=dmatT[:, h, :].unsqueeze(1).broadcast_to([128, nt, 128]), op=ALU.mult),
                   r=["dmatT"], w=["scT_sb"], x=["X0"])

            def p1b(e):
                ins = None
                for t in range(nt):
                    ins = e.transpose(bT_all[:, 128 * t:128 * t + 128], kT[:, 128 * t:128 * t + 128], ident_b[:])
                return ins
            PE(p1b, r=["kT", "ident_b"], w=["bT"])
            AC(lambda e: e.activation(out=kz_sb[:, 0:nt, :].rearrange("p t d -> p (t d)"), in_=bT_all[:, 0:wt], func=AF.Identity, scale=zeta[:, h:h + 1]),
               r=["zeta"], w=["kz_sb"], x=["bT"])
            for _ in range(LAG):
                yield

            def p2(e):
                ins = None
                for t in range(nt):
                    ins = e.matmul(X[1][:, 128 * t:128 * t + 128], lhsT=kz_sb[:, t, :], rhs=v_sb[:, t, 128 * h:128 * h + 128], start=True, stop=True)
                return ins
            PE(p2, r=["kz_sb", "v_sb"], w=["X1"])
            if not meta:
                AC(lambda e: e.activation(out=Sbf_all[:, 0, :], in_=S32[:, h, :], func=AF.Copy), r=[f"S32{h}"], w=["Sbf_all"])
            prev, prevn = S32[:, h, :], f"S32{h}"
            for t in range(nt):
                last = (t == nt - 1)
                dst_, dstn = (S32[:, h, :], f"S32{h}") if last else (Sall[:, t, :], "Sall")
                DV(lambda e, prev=prev, dst_=dst_, t=t: e.scalar_tensor_tensor(out=dst_, in0=prev, scalar=CONST["gammaC"][h],
                                                                                in1=X[1][:, 128 * t:128 * t + 128], op0=ALU.mult, op1=ALU.add),
                   r=[prevn], w=[dstn], x=["X1"])
                prev, prevn = dst_, dstn
            if meta:
                return
            AC(lambda e: e.activation(out=Sbf_all[:, 1:nt, :], in_=Sall[:, 0:nt - 1, :], func=AF.Copy), r=["Sall"], w=["Sbf_all"])
            for _ in range(LAG):
                yield

            def p3(e):
                ins = None
                for t in range(nt):
                    e.matmul(X[2][:, 128 * t:128 * t + 128], lhsT=scT_sb[:, t, :], rhs=v_sb[:, t, 128 * h:128 * h + 128], start=True, stop=False)
                    ins = e.matmul(X[2][:, 128 * t:128 * t + 128], lhsT=qxT[:, 128 * t:128 * t + 128], rhs=Sbf_all[:, t, :], start=False, stop=True)
                return ins
            PE(p3, r=["scT_sb", "v_sb", "qxT", "Sbf_all"], w=["X2"])
            for t in range(nt):
                DV(lambda e, t=t: e.bn_stats(out=gst[:, t, :], in_=X[2][:, 128 * t:128 * t + 128]), w=["gst"], x=["X2"])
            for t in range(nt):
                DV(lambda e, t=t: e.bn_aggr(out=gmv[:, t, :], in_=gst[:, t, :]), r=["gst"], w=["gmv"])
            AC(lambda e: e.activation(out=gsc[:, 0, 0:nt], in_=gmv[:, 0:nt, 1], func=AF.Sqrt, bias=epsb[:], scale=1.0), r=["gmv", "epsb"], w=["gsc"])
            DV(lambda e: e.reciprocal(out=gsc[:, 1, 0:nt], in_=gsc[:, 0, 0:nt]), r=["gsc"], w=["gsc"])
            DV(lambda e: e.scalar_tensor_tensor(out=gsc[:, 2, 0:nt], in0=gmv[:, 0:nt, 0], scalar=-1.0, in1=gsc[:, 1, 0:nt], op0=ALU.mult, op1=ALU.mult),
               r=["gmv", "gsc"], w=["gsc"])
            for t in range(nt):
                AC(lambda e, t=t: e.activation(out=on_sb[:, t, :], in_=X[2][:, 128 * t:128 * t + 128], func=AF.Identity,
                                               scale=gsc[:, 1, t:t + 1], bias=gsc[:, 2, t:t + 1]),
                   r=["gsc"], w=["on_sb"], x=["X2"])
            for _ in range(LAG):
                yield

            def p4(e):
                ins = None
                for t in range(nt):
                    ins = e.transpose(bT_all[:, 512 + 128 * t:512 + 128 * t + 128], on_sb[:, t, :], ident_b[:])
                return ins
            PE(p4, r=["on_sb", "ident_b"], w=["bT"])
            AC(lambda e: e.activation(out=gaff[:, 0:wt], in_=bT_all[:, 512:512 + wt], func=AF.Identity, scale=gn_gb[:, h:h + 1], bias=gn_gb[:, 6 + h:7 + h]),
               r=["gn_gb"], w=["gaff"], x=["bT"])
            DV(lambda e: e.tensor_tensor(out=yT[:, 2 + h, 0:wt], in0=gaff[:, 0:wt], in1=sgT[:, hl, 0:wt], op=ALU.mult), r=["gaff", "sgT"], w=["yT"])
            yield

        def gen_Apre(st_idx):
            meta = st_idx < 0
            nt = 1 if meta else NT
            hT, hTn = hTs[(st_idx + 1) % 2], f"hT{(st_idx + 1) % 2}"
            for t in range(nt):
                xb_ = xs[t % 2]
                xn_ = f"xs{t % 2}"
                if meta:
                    DV(lambda e, xb_=xb_: e.memset(xb_[:], 0.0), w=[xn_])
                    P.dma("sp", lambda e, xb_=xb_: e.dma_start(out=xb_[112:128, :], in_=meta_d), writes=[xn_])
                else:
                    r0 = st_idx * W + t * 128
                    P.dma("sp", lambda e, r0=r0, xb_=xb_: e.dma_start(out=xb_[:], in_=x_d[r0:r0 + 128, :]), writes=[xn_])
                yield from ln_tile([xb_[:, 0:512], xb_[:, 512:1024]], [xn_], None, t, "T", gb=lnin_gb, gbn="lnin_gb", destT=hT, dname=hTn, t=t,
                                   tb=bT_all, tbn="bT")
                yield
            if meta:
                DV(lambda e: e.memset(hT[:, :, 0:112], 0.0), r=[hTn], w=[hTn])

        def gen_Amain(st_idx):
            meta = st_idx < 0
            nt = 1 if meta else NT
            wt = nt * 128
            slot0 = 0 if meta else 128 + st_idx * W
            hT, hTn = hTs[(st_idx + 1) % 2], f"hT{(st_idx + 1) % 2}"
            P.dma("sp", lambda e: e.dma_start(out=cos_sb[:, 0:wt], in_=cosT_d[:, slot0:slot0 + wt]), writes=["cos_sb"])
            P.dma("sp", lambda e: e.dma_start(out=sin_sb[:, 0:wt], in_=sinT_d[:, slot0:slot0 + wt]), writes=["sin_sb"])
            (wb_, wn_), = WA.take()
            for U in range(2):
                xi = next_X()
                mm8(X[xi], f"X{xi}", v8(wb_), wn_, 128 * U, hT, hTn, wt)
                AC(lambda e, U=U, xi=xi: e.activation(out=uT[:, U, 0:wt], in_=X[xi][:, 0:wt], func=AF.Copy), x=[f"X{xi}"], w=["uT"])
            yield
            def gen_s5():
                cnt = 0
                for U in range(2):
                    for sbi in range(wt // SB):
                        yield from s5_block(U, sbi * SB, meta, cnt % 2)
                        cnt += 1
                    if not meta:
                        AC(lambda e, U=U: e.activation(out=ygb[:, U, 0:wt], in_=ypre[:, 0:wt], func=AF.Gelu_apprx_tanh), r=["ypre"], w=["ygb"])
                    yield
                if not meta:
                    for U2 in range(2):
                        xi = next_X()

                        def mmg(e, U2=U2, xi=xi):
                            e.matmul(X[xi][:, 0:wt], lhsT=wglu[:, 0, 128 * U2:128 * U2 + 128], rhs=ygb[:, 0, 0:wt], start=True, stop=False)
                            return e.matmul(X[xi][:, 0:wt], lhsT=wglu[:, 1, 128 * U2:128 * U2 + 128], rhs=ygb[:, 1, 0:wt], start=False, stop=True)
                        PE(mmg, r=["wglu", "ygb"], w=[f"X{xi}"])
                        AC(lambda e, U2=U2, xi=xi: e.activation(out=sgl[:, 0:wt], in_=X[xi][:, 0:wt], func=AF.Sigmoid, bias=s5_fm[:, 2 + U2:3 + U2], scale=1.0),
                           r=["s5_fm"], w=["sgl"], x=[f"X{xi}"])
                        DV(lambda e, U2=U2: e.tensor_tensor(out=yT[:, U2, 0:wt], in0=ygb[:, U2, 0:wt], in1=sgl[:, 0:wt], op=ALU.mult), r=["ygb", "sgl"], w=["yT"])
                    yield

            def gen_ret():
                for hp in range(3):
                    (wb_, wn_), = WA.take()
                    for tp in range((nt + 1) // 2):
                        xi = next_X()
                        tl = [t for t in (2 * tp, 2 * tp + 1) if t < nt]

                        def mmv(e, tl=tl, xi=xi, wb_=wb_):
                            ins = None
                            for ii, t in enumerate(tl):
                                for k in range(8):
                                    ins = e.matmul(X[xi][:, 256 * ii:256 * ii + 256], lhsT=hT[:, k, 128 * t:128 * t + 128], rhs=v8(wb_)[:, k, 0:256],
                                                   start=(k == 0), stop=(k == 7))
                            return ins
                        PE(mmv, r=[wn_, hTn], w=[f"X{xi}"])
                        AC(lambda e, tl=tl, xi=xi, hp=hp: e.activation(
                            out=v_sb[:, tl[0]:tl[0] + len(tl), 256 * hp:256 * hp + 256],
                            in_=X[xi][:, 0:256 * len(tl)].rearrange("p (t c) -> p t c", t=len(tl)), func=AF.Copy),
                           x=[f"X{xi}"], w=["v_sb"])
                    yield
                    if not meta:
                        (wb_, wn_), = WA.take()
                        for hl in range(2):
                            xi = next_X()
                            mm8(X[xi], f"X{xi}", v8(wb_), wn_, 128 * hl, hT, hTn, wt)
                            AC(lambda e, hl=hl, xi=xi: e.activation(out=sgT[:, hl, 0:wt], in_=X[xi][:, 0:wt], func=AF.Silu), x=[f"X{xi}"], w=["sgT"])
                        yield
                    for hl in range(2):
                        yield from head(hp, hl, meta, nt, wt, hT, hTn)

            g1, g2 = gen_ret(), gen_s5()
            live = [g1, g2]
            while live:
                for g in list(live):
                    try:
                        next(g)
                        yield
                    except StopIteration:
                        live.remove(g)

        def gen_E(st_idx):
            hT, hTn = hTs[(st_idx + 1) % 2], f"hT{(st_idx + 1) % 2}"
            for cb in range(4):
                (wb_, wn_), = WB.take()
                for hl in range(2):
                    dm = 2 * cb + hl
                    bi_ = next_bA()
                    mm8(bA[bi_], f"bA{bi_}", v8(wb_), wn_, 128 * hl, yT, "yT", W, resid=hT[:, dm, :], rname=hTn)
                    AC(lambda e, dm=dm, bi_=bi_: e.activation(out=rT[:, dm, :], in_=bA[bi_][:, 0:W], func=AF.Copy), x=[f"bA{bi_}"], w=["rT"])
                yield

        def gen_B(st_idx):
            for t in range(NT):
                rT_to_tokmajor(t)
                yield from ln_tile([bB0h[:, 0:512], bB0h[:, 512:1024]], [], ["bB0", "bB0"], 4 + t, "T", gb=ln1_gb, gbn="ln1_gb", destT=h1T, dname="h1T", t=t,
                                   tb=bB1h, tbn="bB1", xbh=[(relu_t[0][:], "relu_t0"), (relu_t[1][:], "relu_t1")])
                yield
            for ub in range(16):
                (wb_, wn_), = WB.take()
                for hl in range(2):
                    ff = 2 * ub + hl
                    bi_ = next_bA()
                    mm8(bA[bi_], f"bA{bi_}", v8(wb_), wn_, 128 * hl, h1T, "h1T", W)
                    rt = relu_t[ff % 2]
                    AC(lambda e, rt=rt, bi_=bi_: e.activation(out=rt[:], in_=bA[bi_][:, 0:W], func=AF.Relu), x=[f"bA{bi_}"], w=[f"relu_t{ff % 2}"])
                    AC(lambda e, rt=rt, ff=ff: e.activation(out=aT[:, ff, :], in_=rt[:], func=AF.Square), r=[f"relu_t{ff % 2}"], w=["aT"])
                    yield
            for dm in range(8):
                (w0, n0), (w1, n1) = WB.take(2)
                bi_ = next_bA()

                def fnd(e, w0=w0, w1=w1, dm=dm, bi_=bi_):
                    for k in range(32):
                        wv_ = v16(w0) if k < 16 else v16(w1)
                        e.matmul(bA[bi_][:, 0:W], lhsT=wv_[:, k % 16, :], rhs=aT[:, k, :], start=(k == 0), stop=False)
                    return e.matmul(bA[bi_][:, 0:W], lhsT=aI[:], rhs=h1T[:, dm, :], start=False, stop=True)
                PE(fnd, r=[n0, n1, "aT", "h1T", "aI"], w=[f"bA{bi_}"])
                AC(lambda e, dm=dm, bi_=bi_: e.activation(out=rT[:, dm, :], in_=bA[bi_][:, 0:W], func=AF.Copy), x=[f"bA{bi_}"], w=["rT"])
                yield
            for t in range(NT):
                rT_to_tokmajor(t)
                r0 = st_idx * W + t * 128
                yield from ln_tile([bB0h[:, 0:512], bB0h[:, 512:1024]], [], ["bB0", "bB0"], 4 + t, "O", out_rows=(r0, r0 + 128))
                yield

        def run(g):
            for _ in g:
                pass

        def interleave(gb, ga, a_per_b):
            done_a = done_b = False
            acc = 0.0
            na = nb = 0
            while not (done_a and done_b):
                if not done_b:
                    try:
                        next(gb)
                        nb += 1
                    except StopIteration:
                        done_b = True
                acc += a_per_b
                while (acc >= 1.0 or done_b) and not done_a:
                    acc -= 1.0
                    try:
                        next(ga)
                        na += 1
                    except StopIteration:
                        done_a = True
                if done_a:
                    acc = 0.0
            return na, nb

        def chain(*gs):
            for g in gs:
                if g is not None:
                    yield from g

        def spread(gmain, gextra, every):
            n = 0
            extra_live = gextra is not None
            for _ in gmain:
                yield
                n += 1
                if extra_live and n % every == 0:
                    try:
                        next(gextra)
                        yield
                    except StopIteration:
                        extra_live = False
            if extra_live:
                for _ in gextra:
                    yield

        ratio = [A_PER_B]
        run(gen_Apre(-1))
        run(spread(gen_Amain(-1), gen_Apre(0), 3))
        run(spread(gen_Amain(0), gen_Apre(1) if NST > 1 else None, PRE_EVERY))
        for s in range(NST):
            run(gen_E(s))
            if s + 1 < NST:
                a_stream = spread(gen_Amain(s + 1), gen_Apre(s + 2) if s + 2 < NST else None, PRE_EVERY)
                if INTERLEAVE:
                    na_, nb_ = interleave(gen_B(s), a_stream, ratio[0])
                    ratio[0] = na_ / max(nb_, 1)
                else:
                    run(gen_B(s))
                    run(a_stream)
            else:
                run(gen_B(s))
        P.emit(final_wait_ops=out_dma_ops)
    return nc


_NC_CACHE = {}


def _prep_inputs(inp):
    f = lambda a: np.ascontiguousarray(np.asarray(a, dtype=np.float32))
    pk = lambda v: np.ascontiguousarray(f(v).reshape(-1, 128).T)
    shared = {}
    shared["meta"] = f(inp["meta_tokens"])
    shared["w_in"] = f(inp["w_in"][0])
    shared["w_out"] = f(inp["w_out"][0])
    shared["w_up"] = f(inp["w_up"][0])
    shared["w_down"] = f(inp["w_down"][0])
    shared["w_glu"] = f(inp["s5_w_glu"][0])
    shared["lnin_gb"] = np.ascontiguousarray(np.concatenate([pk(inp["ln_in_g"]), pk(inp["ln_in_b"])], axis=1))
    shared["ln1_gb"] = np.ascontiguousarray(np.concatenate([pk(inp["ln1_g"][0]), pk(inp["ln1_b"][0])], axis=1))
    shared["ln2_g"] = f(inp["ln2_g"][0]).reshape(1, D)
    shared["ln2_b"] = f(inp["ln2_b"][0]).reshape(1, D)
    st = lambda a: np.ascontiguousarray(f(a).reshape(8, 128).T)
    ldt = np.repeat(f(inp["s5_log_dt"][0])[:, None], 64, axis=1)
    shared["s5_sc"] = np.ascontiguousarray(np.concatenate([st(inp["s5_lambda_re"][0]), st(inp["s5_lambda_im"][0]), st(ldt)], axis=1))

    def bl(a):
        return f(a).reshape(8, 2, 64, 16).transpose(1, 2, 0, 3).reshape(128, 8, 16)
    shared["s5_b"] = np.ascontiguousarray(np.stack([bl(inp["s5_b_re"][0]), bl(inp["s5_b_im"][0])], axis=1))
    cl = lambda a: bl(f(a).transpose(0, 2, 1))
    shared["s5_c"] = np.ascontiguousarray(np.stack([cl(inp["s5_c_re"][0]), cl(inp["s5_c_im"][0])], axis=1))
    shared["s5_fm"] = np.ascontiguousarray(np.concatenate([pk(inp["s5_d"][0]), pk(inp["s5_b_glu"][0])], axis=1))
    shared["gn_gb"] = np.ascontiguousarray(np.concatenate([pk(inp["ret_gn_g"][0]), pk(inp["ret_gn_b"][0])], axis=1))
    for k in ("ident_f", "pswap", "cosT", "sinT", "dmatT", "xi_bc", "zeta", "maskB"):
        shared[k] = CONST[k]
    x = f(inp["x"])
    maps = []
    for c in range(8):
        m = dict(shared)
        m["x"] = x[c]
        maps.append(m)
    return maps


def kernel(**inputs):
    if "nc" not in _NC_CACHE:
        _NC_CACHE["nc"] = build()
    nc = _NC_CACHE["nc"]
    maps = _prep_inputs(inputs)
    res = run_bass_kernel_spmd(nc, maps, core_ids=list(range(8)))
    out = np.stack([np.asarray(r["out"], dtype=np.float32) for r in res.results], axis=0)
    return out
```

```python
import math
import numpy as np
import concourse.bass as bass
import concourse.mybir as mybir
from concourse.bass_utils import run_bass_kernel_spmd
from contextlib import ExitStack

F32 = mybir.dt.float32
BF16 = mybir.dt.bfloat16
ALU = mybir.AluOpType
AF = mybir.ActivationFunctionType

D = 1024
SEQ = 4096
NMETA = 16
NH = 6
NT = 4
W = NT * 128
NST = SEQ // W
SB = 128
ALPHA = 2.0 ** 0.25
LN_EPS = 1e-5
NSLOT = 33 * 128
ENG_NAMES = ("pe", "act", "dve", "pool", "sp")
STRICT_SYNC = False
INTERLEAVE = True
A_PER_B = 2.6
LAG = 2
PRE_EVERY = 12


class Prog:
    def __init__(self, nc, n_dma_sems=12):
        self.nc = nc
        self.ops = []
        self.res_w = {}
        self.res_r = {}
        self.cnt = {e: 0 for e in ENG_NAMES}
        self.n_dma_sems = n_dma_sems
        self.dma_cnt = {}
        self.dma_rr = {"sp": 0, "pool": 0, "act": 0}
        self.dma_last = {}

    def _deps(self, reads, writes, excl):
        deps = {}
        for r in list(reads) + list(excl):
            if r in self.res_w:
                deps[self.res_w[r]] = True
        for w in list(writes):
            if w in self.res_w:
                deps.setdefault(self.res_w[w], False)
            for rd in self.res_r.get(w, ()):
                deps.setdefault(rd, False)
        for w in excl:
            for rd in self.res_r.get(w, ()):
                deps.setdefault(rd, False)
        return deps

    def _commit(self, oid, reads, writes, excl):
        for r in list(reads) + list(excl):
            self.res_r.setdefault(r, []).append(oid)
        for w in list(writes):
            self.res_w[w] = oid
            self.res_r[w] = []

    def op(self, eng, fn, reads=(), writes=(), excl=()):
        oid = len(self.ops)
        deps = self._deps(reads, writes, excl)
        self.cnt[eng] += 1
        self.ops.append(dict(eng=eng, fn=fn, deps=deps, tok=("E", eng, self.cnt[eng]), dma=False))
        self._commit(oid, reads, writes, excl)
        return oid

    def dma(self, queue, fn, reads=(), writes=()):
        oid = len(self.ops)
        deps = self._deps(reads, writes, ())
        slot = self.dma_rr[queue]
        self.dma_rr[queue] = (slot + 1) % self.n_dma_sems
        key = (queue, slot)
        if key in self.dma_last:
            deps[self.dma_last[key]] = True
        self.dma_cnt[key] = self.dma_cnt.get(key, 0) + 1
        self.dma_last[key] = oid
        self.ops.append(dict(eng=queue, fn=fn, deps=deps, tok=("D", key, 16 * self.dma_cnt[key]), dma=True))
        self._commit(oid, reads, writes, ())
        return oid

    def emit(self, final_wait_ops=()):
        nc = self.nc
        with ExitStack() as es:
            esem = {e: es.enter_context(nc.semaphore(f"s_{e}")) for e in ENG_NAMES}
            dsem = {}
            for q in ("sp", "pool", "act"):
                for s in range(self.n_dma_sems):
                    if (q, s) in self.dma_cnt:
                        dsem[(q, s)] = es.enter_context(nc.semaphore(f"d_{q}{s}"))
            block = es.enter_context(nc.Block())

            def semval(tok):
                if tok[0] == "E":
                    return esem[tok[1]], tok[2]
                return dsem[tok[1]], tok[2]

            def run_engine(ename, eobj):
                known = {}
                for oid, o in enumerate(self.ops):
                    if o["eng"] != ename:
                        continue
                    for d in sorted(o["deps"]):
                        do = self.ops[d]
                        if (not o["dma"]) and (not do["dma"]) and do["eng"] == ename:
                            if ename == "pe" or not (o["deps"][d] or STRICT_SYNC):
                                continue
                        sem, val = semval(do["tok"])
                        k = id(sem)
                        if known.get(k, 0) >= val:
                            continue
                        eobj.wait_ge(sem, val)
                        known[k] = val
                    ins = o["fn"](eobj)
                    sem, val = semval(o["tok"])
                    ins.then_inc(sem, 16 if o["dma"] else 1)
                if ename == "sp":
                    for oid in final_wait_ops:
                        sem, val = semval(self.ops[oid]["tok"])
                        eobj.wait_ge(sem, val)

            @block.tensor
            def _(e):
                run_engine("pe", e)

            @block.scalar
            def _(e):
                run_engine("act", e)

            @block.vector
            def _(e):
                run_engine("dve", e)

            @block.gpsimd
            def _(e):
                run_engine("pool", e)

            @block.sync
            def _(e):
                run_engine("sp", e)


def _host_consts():
    c = {}
    c["ident_f"] = np.eye(128, dtype=np.float32)
    pm = np.zeros((128, 128), np.float32)
    for dp in range(128):
        pm[(dp + 64) % 128, dp] = 1.0
    c["pswap"] = pm
    pos = (np.arange(NSLOT, dtype=np.float32) - 112.0).astype(np.float32)
    inv_freq = (1.0 / (10000.0 ** (np.arange(0, 128, 2, dtype=np.float32) / 128.0))).astype(np.float32)
    ang = (pos[:, None] * inv_freq[None, :]).astype(np.float32)
    cs, sn = np.cos(ang).astype(np.float32), np.sin(ang).astype(np.float32)
    c["cosT"] = np.ascontiguousarray(np.concatenate([cs, cs], axis=1).T)
    c["sinT"] = np.ascontiguousarray(np.concatenate([-sn, sn], axis=1).T)
    lg = np.log1p(-np.exp2(-5.0 - np.arange(NH, dtype=np.float32))).astype(np.float32)
    idx = np.arange(128, dtype=np.float32)
    scale = 128.0 ** -0.5
    diff = idx[None, :] - idx[:, None]
    dm = np.where(diff[None] >= 0, np.exp(np.maximum(diff, 0.0)[None] * lg[:, None, None]), 0.0) * scale
    c["dmatT"] = np.ascontiguousarray(dm.transpose(1, 0, 2)).astype(np.float32)
    xi = np.exp((idx + 1.0)[None] * lg[:, None]).astype(np.float32)
    c["xi_bc"] = np.ascontiguousarray(np.broadcast_to(xi[None], (128, NH, 128))).astype(np.float32)
    zeta = (np.exp((127.0 - idx)[None] * lg[:, None]) * scale).astype(np.float32)
    c["zeta"] = np.ascontiguousarray(zeta.T)
    c["gammaC"] = [float(np.exp(128.0 * lg[h])) for h in range(NH)]
    mb = np.zeros((128, 4, 8), np.float32)
    for gl in range(2):
        for j4 in range(4):
            mb[gl * 64:(gl + 1) * 64, j4, 2 * j4 + gl] = 1.0
    c["maskB"] = mb
    return c


CONST = _host_consts()


def build():
    nc = bass.Bass("TRN2", target_bir_lowering=False)

    def din(name, shape):
        return nc.dram_tensor(name, list(shape), F32, kind="ExternalInput").ap()

    x_d = din("x", [SEQ, D])
    meta_d = din("meta", [NMETA, D])
    w_in_d = din("w_in", [D, 3328])
    w_out_d = din("w_out", [D, D])
    w_up_d = din("w_up", [D, 4 * D])
    w_down_d = din("w_down", [4 * D, D])
    w_glu_d = din("w_glu", [256, 256])
    lnin_gb_d = din("lnin_gb", [128, 16])
    ln1_gb_d = din("ln1_gb", [128, 16])
    ln2_g_d = din("ln2_g", [1, D])
    ln2_b_d = din("ln2_b", [1, D])
    s5_sc_d = din("s5_sc", [128, 24])
    s5_b_d = din("s5_b", [128, 2, 8, 16])
    s5_c_d = din("s5_c", [128, 2, 8, 16])
    s5_fm_d = din("s5_fm", [128, 4])
    gn_gb_d = din("gn_gb", [128, 12])
    ident_d = din("ident_f", [128, 128])
    pswap_d = din("pswap", [128, 128])
    cosT_d = din("cosT", [128, NSLOT])
    sinT_d = din("sinT", [128, NSLOT])
    dmatT_d = din("dmatT", [128, NH, 128])
    xibc_d = din("xi_bc", [128, NH, 128])
    zeta_d = din("zeta", [128, NH])
    maskB_d = din("maskB", [128, 4, 8])
    out_d = nc.dram_tensor("out", [SEQ, D], F32, kind="ExternalOutput").ap()

    with ExitStack() as es:
        def sb(name, shape, dt=F32):
            return es.enter_context(nc.sbuf_tensor(name, list(shape), dt))

        def psum(name, shape, dt=F32):
            return es.enter_context(nc.psum_tensor(name, list(shape), dt))

        xs = [sb(f"xs{i}", [128, D]) for i in range(2)]
        xnb = [sb(f"xnb{i}", [128, D], BF16) for i in range(2)]
        tmpT = sb("tmpT", [128, 8, 128], BF16)
        hTs = [sb(f"hT{i}", [128, 8, W], BF16) for i in range(2)]
        h1T = sb("h1T", [128, 8, W], BF16)
        wA = [sb(f"wA{i}", [128, 2048], BF16) for i in range(3)]
        wB = [sb(f"wB{i}", [128, 2048], BF16) for i in range(4)]
        uT = sb("uT", [128, 2, W], BF16)
        q_pre = sb("q_pre", [128, W], BF16)
        k_pre = sb("k_pre", [128, W], BF16)
        qT = sb("qT", [128, W], BF16)
        kT = sb("kT", [128, W], BF16)
        qxT = sb("qxT", [128, W], BF16)
        sgT = sb("sgT", [128, 2, W], BF16)
        v_sb = sb("v_sb", [128, NT, 768], BF16)
        yT = sb("yT", [128, 8, W], BF16)
        aT = sb("aT", [128, 32, W], BF16)
        relu_t = [sb(f"relu_t{i}", [128, W], BF16) for i in range(2)]
        rT = sb("rT", [128, 8, W], BF16)
        cos_sb = sb("cos_sb", [128, W])
        sin_sb = sb("sin_sb", [128, W])
        ln2g = sb("ln2g", [128, D])
        ln2b = sb("ln2b", [128, D])
        ident_f = sb("ident_fs", [128, 128])
        ident_b = sb("ident_b", [128, 128], BF16)
        aI = sb("aI", [128, 128], BF16)
        pswap = sb("pswap_s", [128, 128], BF16)
        dmatT = sb("dmatT_s", [128, NH, 128])
        xibc = sb("xibc_s", [128, NH, 128])
        zeta = sb("zeta_s", [128, NH])
        maskB = sb("maskB_s", [128, 4, 8])
        lnin_gb = sb("lnin_gb_s", [128, 16])
        ln1_gb = sb("ln1_gb_s", [128, 16])
        gn_gb = sb("gn_gb_s", [128, 12])
        s5_fm = sb("s5_fm_s", [128, 4])
        epsb = sb("epsb", [128, 1])
        stt = sb("stt", [128, 8, 12])
        mv = sb("mv", [128, 8, 8])
        scT_sb = sb("scT_sb", [128, NT, 128], BF16)
        kz_sb = sb("kz_sb", [128, NT, 128], BF16)
        on_sb = sb("on_sb", [128, NT, 128], BF16)
        gaff = sb("gaff", [128, W], BF16)
        S32 = sb("S32", [128, NH, 128])
        Sall = sb("Sall", [128, 3, 128])
        Sbf_all = sb("Sbf_all", [128, NT, 128], BF16)
        gst = sb("gst", [128, NT, 6])
        gmv = sb("gmv", [128, NT, 2])
        gsc = sb("gsc", [128, 3, NT])
        s5_sc = sb("s5_sc_s", [128, 24])
        s5w = sb("s5w", [128, 40, 8])
        WBT = sb("WBT", [128, 16, 128], BF16)
        CT = sb("CT", [128, 16, 128], BF16)
        Etab = sb("Etab", [128, 2, 8, SB])
        s5m = sb("s5m", [128, 4, 4 * SB])
        s5p = sb("s5p", [128, 4, 4 * SB])
        s5x = [sb(f"s5x{i}", [128, 2, 4 * SB], BF16) for i in range(2)]
        Xc = sb("Xc", [128, 2, 8])
        ypre = sb("ypre", [128, W])
        ygb = sb("ygb", [128, 2, W], BF16)
        sgl = sb("sgl", [128, W], BF16)
        wglu = sb("wglu", [128, 2, 256], BF16)
        rope_t1 = s5p[:, 0, :]
        rope_t2 = s5p[:, 1, :]
        PALL = ["s5p0", "s5p1", "s5p2", "s5p3"]
        pfl = s5p[:].rearrange("p a w -> p (a w)")
        s5_b = pfl[:, 0:256].rearrange("p (r j h) -> p r j h", r=2, j=8)
        s5_c = pfl[:, 256:512].rearrange("p (r j h) -> p r j h", r=2, j=8)
        s5bb = pfl[:, 512:768].rearrange("p (r j h) -> p r j h", r=2, j=8)
        s5ex = pfl[:, 768:896].rearrange("p (g h) -> p g h", g=8)
        s5tmp = pfl[:, 1024:1536].rearrange("p (x j h) -> p x j h", x=4, j=8)

        bA = [psum(f"bA{i}", [128, 512]) for i in range(2)]
        bB = [psum(f"bB{i}", [128, 512]) for i in range(2)]
        bT = psum("bT", [128, 1024], BF16)
        X = [psum(f"X{i}", [128, 512]) for i in range(3)]
        bB0h = bB[0][:].bitcast(BF16)
        bB1h = bB[1][:].bitcast(BF16)

        P = Prog(nc)
        Etmp = aT[:, 0:16, :].bitcast(F32).rearrange("p a b -> p (a b)").rearrange("p (x j n) -> p x j n", x=4, j=8)
        DV = lambda fn, r=(), w=(), x=(): P.op("dve", fn, r, w, x)
        AC = lambda fn, r=(), w=(), x=(): P.op("act", fn, r, w, x)
        PE = lambda fn, r=(), w=(): P.op("pe", fn, r, w)

        def ld(dst, src, name, q="sp"):
            P.dma(q, lambda e: e.dma_start(out=dst, in_=src), writes=name if isinstance(name, list) else [name])

        ld(ident_f[:], ident_d, "ident_f")
        ld(dmatT[:], dmatT_d, "dmatT")
        ld(zeta[:], zeta_d, "zeta")
        ld(maskB[:], maskB_d, "maskB")
        ld(lnin_gb[:], lnin_gb_d, "lnin_gb")
        ld(ln1_gb[:], ln1_gb_d, "ln1_gb")
        ld(gn_gb[:], gn_gb_d, "gn_gb")
        ld(s5_fm[:], s5_fm_d, "s5_fm")
        ld(s5_sc[:], s5_sc_d, "s5_sc")
        ld(s5_b, s5_b_d, PALL)
        ld(s5_c, s5_c_d, PALL)
        ld(ln2g[:], ln2_g_d[0].partition_broadcast(128), "ln2g")
        ld(ln2b[:], ln2_b_d[0].partition_broadcast(128), "ln2b")
        ld(pswap[:], pswap_d, "pswap", q="pool")
        ld(xibc[:], xibc_d, "xibc")
        ld(wglu[:], w_glu_d.rearrange("(k p) c -> p k c", p=128), "wglu", q="pool")
        DV(lambda e: e.memset(epsb[:], LN_EPS), w=["epsb"])
        AC(lambda e: e.activation(out=ident_b[:], in_=ident_f[:], func=AF.Copy), r=["ident_f"], w=["ident_b"])
        AC(lambda e: e.activation(out=aI[:], in_=ident_f[:], func=AF.Identity, scale=ALPHA), r=["ident_f"], w=["aI"])
        DV(lambda e: e.memset(S32[:], 0.0), w=[f"S32{h}" for h in range(NH)])
        DV(lambda e: e.memset(Xc[:], 0.0), w=["Xc"])

        SL = lambda i: s5w[:, i, :]
        lam_re, lam_im, log_dt = s5_sc[:, 0:8], s5_sc[:, 8:16], s5_sc[:, 16:24]
        (I_DT, I_AR, I_TH, I_RHO, I_PHI, I_U, I_S, I_C, I_TS, I_TC, I_CC, I_SS, I_CS, I_N, I_RN,
         I_LBR, I_LBI, I_DEN, I_NR, I_QR, I_QI, I_T1, I_T2, I_NS) = range(24)

        def dv2(out, a, b, op):
            DV(lambda e: e.tensor_tensor(out=out, in0=a, in1=b, op=op), r=["s5w", "s5_sc"], w=["s5w"])

        AC(lambda e: e.activation(out=SL(I_DT), in_=log_dt, func=AF.Exp), r=["s5_sc"], w=["s5w"])
        dv2(SL(I_AR), lam_re, SL(I_DT), ALU.mult)
        dv2(SL(I_TH), lam_im, SL(I_DT), ALU.mult)
        AC(lambda e: e.activation(out=SL(I_RHO), in_=SL(I_AR), func=AF.Exp), r=["s5w"], w=["s5w"])
        DV(lambda e: e.tensor_scalar(out=SL(I_PHI), in0=SL(I_TH), scalar1=1.0 / 32.0, scalar2=None, op0=ALU.mult), r=["s5w"], w=["s5w"])
        dv2(SL(I_U), SL(I_PHI), SL(I_PHI), ALU.mult)
        DV(lambda e: e.tensor_copy(out=SL(I_S), in_=SL(I_PHI)), r=["s5w"], w=["s5w"])
        DV(lambda e: e.tensor_copy(out=SL(I_TS), in_=SL(I_PHI)), r=["s5w"], w=["s5w"])
        DV(lambda e: e.memset(SL(I_C), 1.0), r=["s5w"], w=["s5w"])
        DV(lambda e: e.memset(SL(I_TC), 1.0), r=["s5w"], w=["s5w"])
        for kk in range(1, 7):
            cs_ = -1.0 / ((2 * kk) * (2 * kk + 1))
            cc_ = -1.0 / ((2 * kk - 1) * (2 * kk))
            DV(lambda e, c_=cs_: e.scalar_tensor_tensor(out=SL(I_TS), in0=SL(I_TS), scalar=c_, in1=SL(I_U), op0=ALU.mult, op1=ALU.mult), r=["s5w"], w=["s5w"])
            dv2(SL(I_S), SL(I_S), SL(I_TS), ALU.add)
            DV(lambda e, c_=cc_: e.scalar_tensor_tensor(out=SL(I_TC), in0=SL(I_TC), scalar=c_, in1=SL(I_U), op0=ALU.mult, op1=ALU.mult), r=["s5w"], w=["s5w"])
            dv2(SL(I_C), SL(I_C), SL(I_TC), ALU.add)
        for _ in range(5):
            dv2(SL(I_CC), SL(I_C), SL(I_C), ALU.mult)
            dv2(SL(I_SS), SL(I_S), SL(I_S), ALU.mult)
            dv2(SL(I_CS), SL(I_C), SL(I_S), ALU.mult)
            dv2(SL(I_C), SL(I_CC), SL(I_SS), ALU.subtract)
            DV(lambda e: e.tensor_scalar(out=SL(I_S), in0=SL(I_CS), scalar1=2.0, scalar2=None, op0=ALU.mult), r=["s5w"], w=["s5w"])
        dv2(SL(I_CC), SL(I_C), SL(I_C), ALU.mult)
        dv2(SL(I_SS), SL(I_S), SL(I_S), ALU.mult)
        dv2(SL(I_N), SL(I_CC), SL(I_SS), ALU.add)
        AC(lambda e: e.activation(out=SL(I_NS), in_=SL(I_N), func=AF.Sqrt), r=["s5w"], w=["s5w"])
        DV(lambda e: e.reciprocal(out=SL(I_RN), in_=SL(I_NS)), r=["s5w"], w=["s5w"])
        dv2(SL(I_C), SL(I_C), SL(I_RN), ALU.mult)
        dv2(SL(I_S), SL(I_S), SL(I_RN), ALU.mult)
        dv2(SL(I_LBR), SL(I_RHO), SL(I_C), ALU.mult)
        dv2(SL(I_LBI), SL(I_RHO), SL(I_S), ALU.mult)
        dv2(SL(I_T1), lam_re, lam_re, ALU.mult)
        dv2(SL(I_T2), lam_im, lam_im, ALU.mult)
        dv2(SL(I_DEN), SL(I_T1), SL(I_T2), ALU.add)
        DV(lambda e: e.reciprocal(out=SL(I_DEN), in_=SL(I_DEN)), r=["s5w"], w=["s5w"])
        DV(lambda e: e.tensor_scalar(out=SL(I_NR), in0=SL(I_LBR), scalar1=-1.0, scalar2=None, op0=ALU.add), r=["s5w"], w=["s5w"])
        dv2(SL(I_T1), SL(I_NR), lam_re, ALU.mult)
        dv2(SL(I_T2), SL(I_LBI), lam_im, ALU.mult)
        dv2(SL(I_QR), SL(I_T1), SL(I_T2), ALU.add)
        dv2(SL(I_QR), SL(I_QR), SL(I_DEN), ALU.mult)
        dv2(SL(I_T1), SL(I_LBI), lam_re, ALU.mult)
        dv2(SL(I_T2), SL(I_NR), lam_im, ALU.mult)
        dv2(SL(I_QI), SL(I_T1), SL(I_T2), ALU.subtract)
        dv2(SL(I_QI), SL(I_QI), SL(I_DEN), ALU.mult)
        qr_bc = SL(I_QR).unsqueeze(2).broadcast_to([128, 8, 16])
        qi_bc = SL(I_QI).unsqueeze(2).broadcast_to([128, 8, 16])

        def bb(out, a, b_, op):
            DV(lambda e: e.tensor_tensor(out=out, in0=a, in1=b_, op=op), r=["s5w"] + PALL, w=PALL)

        bb(s5tmp[:, 0], s5_b[:, 0], qr_bc, ALU.mult)
        bb(s5tmp[:, 1], s5_b[:, 1], qi_bc, ALU.mult)
        bb(s5tmp[:, 2], s5_b[:, 1], qr_bc, ALU.mult)
        bb(s5tmp[:, 3], s5_b[:, 0], qi_bc, ALU.mult)
        bb(s5bb[:, 0], s5tmp[:, 0], s5tmp[:, 1], ALU.subtract)
        bb(s5bb[:, 1], s5tmp[:, 2], s5tmp[:, 3], ALU.add)
        for j in range(8):
            mk = maskB[:, j % 4, :].unsqueeze(2).broadcast_to([128, 8, 16])
            for ri in range(2):
                src = s5bb[:, ri, j, :].unsqueeze(1).broadcast_to([128, 8, 16])
                DV(lambda e, src=src, mk=mk: e.tensor_tensor(out=s5ex, in0=src, in1=mk, op=ALU.mult), r=PALL + ["maskB"], w=PALL)
                PE(lambda e: e.transpose(X[0][:, 0:128], s5ex.rearrange("p g h -> p (g h)"), ident_f[:]), r=PALL + ["ident_f"], w=["X0"])
                AC(lambda e, j=j, ri=ri: e.activation(out=WBT[:, 2 * j + ri, :], in_=X[0][:, 0:128], func=AF.Copy), x=["X0"], w=["WBT"])
                csrc = s5_c[:, ri, j, :].unsqueeze(1).broadcast_to([128, 8, 16])
                sgn = 1.0 if ri == 0 else -1.0
                DV(lambda e, csrc=csrc, mk=mk, j=j, ri=ri, sgn=sgn: e.scalar_tensor_tensor(
                    out=CT[:, 2 * j + ri, :].rearrange("p (g h) -> p g h", g=8), in0=csrc, scalar=sgn, in1=mk,
                    op0=ALU.mult, op1=ALU.mult), r=PALL + ["maskB"], w=["CT"])
        DV(lambda e: e.tensor_copy(out=Etab[:, 0, :, 0], in_=SL(I_C)), r=["s5w"], w=["Etab"])
        DV(lambda e: e.tensor_scalar(out=Etab[:, 1, :, 0], in0=SL(I_S), scalar1=-1.0, scalar2=None, op0=ALU.mult), r=["s5w"], w=["Etab"])
        n = 1
        while n < SB:
            ar = Etab[:, 0, :, 0:n]
            ai = Etab[:, 1, :, 0:n]
            br = Etab[:, 0, :, n - 1:n].broadcast_to([128, 8, n])
            bi = Etab[:, 1, :, n - 1:n].broadcast_to([128, 8, n])
            t_ = [Etmp[:, i, :, 0:n] for i in range(4)]
            for (o_, a_, b_) in ((t_[0], ar, br), (t_[1], ai, bi), (t_[2], ar, bi), (t_[3], ai, br)):
                DV(lambda e, o_=o_, a_=a_, b_=b_: e.tensor_tensor(out=o_, in0=a_, in1=b_, op=ALU.mult), r=["Etab", "aT"], w=["aT"])
            DV(lambda e, n=n, t_=t_: e.tensor_tensor(out=Etab[:, 0, :, n:2 * n], in0=t_[0], in1=t_[1], op=ALU.subtract), r=["aT"], w=["Etab"])
            DV(lambda e, n=n, t_=t_: e.tensor_tensor(out=Etab[:, 1, :, n:2 * n], in0=t_[2], in1=t_[3], op=ALU.add), r=["aT"], w=["Etab"])
            n *= 2

        class WStream:
            def __init__(self, bufs, names, plan):
                self.bufs, self.names, self.plan = bufs, names, plan
                self.issued, self.n, self.i = 0, len(bufs), 0

            def take(self, span=1):
                i = self.i
                while self.issued < min(len(self.plan), i + self.n):
                    b = self.issued % self.n
                    for (view_fn, src) in self.plan[self.issued]:
                        P.dma("pool", lambda e, d=view_fn(self.bufs[b]), s=src: e.dma_start(out=d, in_=s), writes=[self.names[b]])
                    self.issued += 1
                self.i += span
                return [(self.bufs[(i + s) % self.n], self.names[(i + s) % self.n]) for s in range(span)]

        v8 = lambda buf: buf[:].rearrange("p (k c) -> p k c", k=8)
        v16 = lambda buf: buf[:].rearrange("p (k c) -> p k c", k=16)

        def blk256(w_d, c0):
            return [(lambda buf: v8(buf), w_d[:, c0:c0 + 256].rearrange("(k p) c -> p k c", p=128))]

        planA = []
        for st_i in range(-1, NST):
            m_ = st_i < 0
            planA.append(blk256(w_in_d, 0))
            for hp in range(3):
                planA.append(blk256(w_in_d, 256 + 1536 + 256 * hp))
                if not m_:
                    planA.append(blk256(w_in_d, 256 + 2304 + 256 * hp))
                for hl in range(2):
                    h = 2 * hp + hl
                    planA.append([
                        (lambda buf: v8(buf)[:, :, 0:128], w_in_d[:, 256 + 128 * h:256 + 128 * h + 128].rearrange("(k p) c -> p k c", p=128)),
                        (lambda buf: v8(buf)[:, :, 128:256], w_in_d[:, 1024 + 128 * h:1024 + 128 * h + 128].rearrange("(k p) c -> p k c", p=128)),
                    ])
        planB = []
        for st_i in range(NST):
            for cb in range(4):
                planB.append(blk256(w_out_d, 256 * cb))
            for ub in range(16):
                planB.append(blk256(w_up_d, 256 * ub))
            for dm in range(8):
                for half in range(2):
                    planB.append([(lambda buf: v16(buf),
                                   w_down_d[2048 * half:2048 * half + 2048, 128 * dm:128 * dm + 128].rearrange("(k p) c -> p k c", p=128))])
        WA = WStream(wA, [f"wA{i}" for i in range(3)], planA)
        WB = WStream(wB, [f"wB{i}" for i in range(4)], planB)

        bA_i = [0]

        def next_bA():
            i = bA_i[0] % 2
            bA_i[0] += 1
            return i

        X_i = [0]

        def next_X():
            i = X_i[0] % 3
            X_i[0] += 1
            return i

        def mm8(bank, bname, wv, wn, c0, actT, aname, wt, resid=None, rname=None):
            def fn(e):
                ins = None
                for k in range(8):
                    ins = e.matmul(bank[:, 0:wt], lhsT=wv[:, k, c0:c0 + 128], rhs=actT[:, k, 0:wt],
                                   start=(k == 0), stop=(k == 7 and resid is None))
                if resid is not None:
                    ins = e.matmul(bank[:, 0:wt], lhsT=aI[:], rhs=resid[:, 0:wt], start=False, stop=True)
                return ins
            PE(fn, r=[wn, aname] + (["aI", rname] if resid is not None else []), w=[bname])

        def ln_tile(src_halves, src_reads, src_excl, slot, out_kind, gb=None, gbn=None, destT=None, dname=None, t=0,
                    out_rows=None, tb=None, tbn=None, xbh=None):
            st = stt[:, slot, :]
            m = mv[:, slot, :]
            for hh in range(2):
                DV(lambda e, hh=hh: e.bn_stats(out=st[:, 6 * hh:6 * hh + 6], in_=src_halves[hh]),
                   r=src_reads, w=[f"stt{slot}"], x=[src_excl[hh]] if src_excl else [])
            DV(lambda e: e.bn_aggr(out=m[:, 0:2], in_=st), r=[f"stt{slot}"], w=[f"mv{slot}"])
            AC(lambda e: e.activation(out=m[:, 2:3], in_=m[:, 1:2], func=AF.Sqrt, bias=epsb[:], scale=1.0), r=[f"mv{slot}", "epsb"], w=[f"mv{slot}"])
            DV(lambda e: e.reciprocal(out=m[:, 3:4], in_=m[:, 2:3]), r=[f"mv{slot}"], w=[f"mv{slot}"])
            DV(lambda e: e.scalar_tensor_tensor(out=m[:, 4:5], in0=m[:, 0:1], scalar=-1.0, in1=m[:, 3:4], op0=ALU.mult, op1=ALU.mult), r=[f"mv{slot}"], w=[f"mv{slot}"])
            if out_kind == "T":
                if xbh is None:
                    xbh = [(xnb[slot % 2][:, 0:512], f"xnb{slot % 2}"), (xnb[slot % 2][:, 512:1024], f"xnb{slot % 2}")]
                for hh in range(2):
                    AC(lambda e, hh=hh: e.activation(out=xbh[hh][0], in_=src_halves[hh], func=AF.Identity,
                                                     scale=m[:, 3:4], bias=m[:, 4:5]),
                       r=src_reads + [f"mv{slot}"], w=[xbh[hh][1]], x=[src_excl[hh]] if src_excl else [])
                for _ in range(LAG):
                    yield

                def trf(e):
                    ins = None
                    for k in range(8):
                        ins = e.transpose(tb[:, 128 * k:128 * k + 128], xbh[k // 4][0][:, 128 * (k % 4):128 * (k % 4) + 128], ident_b[:])
                    return ins
                PE(trf, r=[xbh[0][1], xbh[1][1], "ident_b"], w=[tbn])
                g_bc = gb[:, 0:8].unsqueeze(2).broadcast_to([128, 8, 128])
                b_bc = gb[:, 8:16].unsqueeze(2).broadcast_to([128, 8, 128])
                DV(lambda e: e.tensor_tensor(out=tmpT[:], in0=tb.rearrange("p (k c) -> p k c", k=8), in1=g_bc, op=ALU.mult),
                   r=[gbn], w=["tmpT"], x=[tbn])
                DV(lambda e: e.tensor_tensor(out=destT[:, :, 128 * t:128 * t + 128], in0=tmpT[:], in1=b_bc, op=ALU.add),
                   r=["tmpT", gbn], w=[dname])
            else:
                xo = xs[slot % 2]
                xon = f"xs{slot % 2}"
                for hh in range(2):
                    AC(lambda e, hh=hh: e.activation(out=xo[:, 512 * hh:512 * hh + 512], in_=src_halves[hh], func=AF.Identity,
                                                     scale=m[:, 3:4], bias=m[:, 4:5]),
                       r=src_reads + [f"mv{slot}"], w=[xon], x=[src_excl[hh]] if src_excl else [])
                DV(lambda e: e.tensor_tensor(out=xo[:], in0=xo[:], in1=ln2g[:], op=ALU.mult), r=[xon, "ln2g"], w=[xon])
                DV(lambda e: e.tensor_tensor(out=xo[:], in0=xo[:], in1=ln2b[:], op=ALU.add), r=[xon, "ln2b"], w=[xon])
                out_dma_ops.append(P.dma("sp", lambda e: e.dma_start(out=out_d[out_rows[0]:out_rows[1], :], in_=xo[:]), reads=[xon]))

        def rT_to_tokmajor(t):
            def fn(e):
                ins = None
                for dm in range(8):
                    ins = e.transpose(bB0h[:, 128 * dm:128 * dm + 128], rT[:, dm, 128 * t:128 * t + 128], ident_b[:])
                return ins
            PE(fn, r=["rT", "ident_b"], w=["bB0"])

        out_dma_ops = []
        bT_all = bT[:]

        def s5_block(U, c0, meta, xi_):
            Er = Etab[:, 0, 4 * U:4 * U + 4, :].rearrange("p j t -> p (j t)")
            Ei = Etab[:, 1, 4 * U:4 * U + 4, :].rearrange("p j t -> p (j t)")
            Xre, Xim, XY = X[0], X[1], X[2]

            def mmbu(e):
                ins = None
                for j4 in range(4):
                    j = 4 * U + j4
                    e.matmul(Xre[:, 128 * j4:128 * j4 + 128], lhsT=WBT[:, 2 * j, :], rhs=uT[:, U, c0:c0 + 128], start=True, stop=True)
                    ins = e.matmul(Xim[:, 128 * j4:128 * j4 + 128], lhsT=WBT[:, 2 * j + 1, :], rhs=uT[:, U, c0:c0 + 128], start=True, stop=True)
                return ins
            PE(mmbu, r=["WBT", "uT"], w=["X0", "X1"])
            for (o_, a_, b_, xb_) in ((0, Xre, Er, "X0"), (1, Xim, Ei, "X1"), (2, Xim, Er, "X1"), (3, Xre, Ei, "X0")):
                DV(lambda e, o_=o_, a_=a_, b_=b_: e.tensor_tensor(out=s5m[:, o_, :], in0=a_[:], in1=b_, op=ALU.mult),
                   r=["Etab"], w=[f"s5m{o_}"], x=[xb_])
            DV(lambda e: e.tensor_tensor(out=s5m[:, 0, :], in0=s5m[:, 0, :], in1=s5m[:, 1, :], op=ALU.subtract), r=["s5m0", "s5m1"], w=["s5m0"])
            DV(lambda e: e.tensor_tensor(out=s5m[:, 2, :], in0=s5m[:, 2, :], in1=s5m[:, 3, :], op=ALU.add), r=["s5m2", "s5m3"], w=["s5m2"])
            yield
            for j4 in range(4):
                j = 4 * U + j4
                bs = slice(128 * j4, 128 * j4 + 128)
                for (ri, si, so) in ((0, 0, 1), (1, 2, 3)):
                    DV(lambda e, ri=ri, si=si, so=so, j=j, bs=bs: e.tensor_tensor_scan(
                        out=s5m[:, so, bs], data0=s5w[:, I_RHO, j:j + 1].broadcast_to([128, 128]), data1=s5m[:, si, bs],
                        initial=Xc[:, ri, j:j + 1], op0=ALU.mult, op1=ALU.add),
                       r=["s5w", f"s5m{si}", "Xc"], w=[f"s5m{so}"])
            for (o_, zi_, e_) in ((0, 1, Er), (1, 3, Ei), (2, 3, Er), (3, 1, Ei)):
                DV(lambda e, o_=o_, zi_=zi_, e_=e_: e.tensor_tensor(out=s5p[:, o_, :], in0=s5m[:, zi_, :], in1=e_, op=ALU.mult),
                   r=["Etab", f"s5m{zi_}"], w=[f"s5p{o_}"])
            pv = lambda o_: s5p[:, o_, :].rearrange("p (j t) -> p j t", j=4)[:, :, 127]
            DV(lambda e: e.tensor_tensor(out=Xc[:, 0, 4 * U:4 * U + 4], in0=pv(0), in1=pv(1), op=ALU.add), r=["s5p0", "s5p1"], w=["Xc"])
            DV(lambda e: e.tensor_tensor(out=Xc[:, 1, 4 * U:4 * U + 4], in0=pv(2), in1=pv(3), op=ALU.subtract), r=["s5p2", "s5p3"], w=["Xc"])
            if meta:
                return
            xx = s5x[xi_]
            xn = f"s5x{xi_}"
            DV(lambda e: e.tensor_tensor(out=xx[:, 0, :], in0=s5p[:, 0, :], in1=s5p[:, 1, :], op=ALU.add), r=["s5p0", "s5p1"], w=[xn])
            DV(lambda e: e.tensor_tensor(out=xx[:, 1, :], in0=s5p[:, 2, :], in1=s5p[:, 3, :], op=ALU.subtract), r=["s5p2", "s5p3"], w=[xn])
            for _ in range(LAG):
                yield

            def mmy(e):
                ins = None
                for j4 in range(4):
                    j = 4 * U + j4
                    e.matmul(XY[:, 0:128], lhsT=CT[:, 2 * j, :], rhs=xx[:, 0, 128 * j4:128 * j4 + 128], start=(j4 == 0), stop=False)
                    ins = e.matmul(XY[:, 0:128], lhsT=CT[:, 2 * j + 1, :], rhs=xx[:, 1, 128 * j4:128 * j4 + 128], start=False, stop=(j4 == 3))
                return ins
            PE(mmy, r=["CT", xn], w=["X2"])
            DV(lambda e: e.scalar_tensor_tensor(out=ypre[:, c0:c0 + 128], in0=uT[:, U, c0:c0 + 128], scalar=s5_fm[:, U:U + 1],
                                                in1=XY[:, 0:128], op0=ALU.mult, op1=ALU.add),
               r=["uT", "s5_fm"], w=["ypre"], x=["X2"])
            yield

        def head(hp, hl, meta, nt, wt, hT, hTn):
            h = 2 * hp + hl
            (wb_, wn_), = WA.take()
            wv_ = v8(wb_)
            todo = [("k", 128, k_pre, kT)] + ([] if meta else [("q", 0, q_pre, qT)])
            for (nm, coff, pre, dst) in todo:
                xi = next_X()
                mm8(X[xi], f"X{xi}", wv_, wn_, coff, hT, hTn, wt)
                AC(lambda e, pre=pre, xi=xi: e.activation(out=pre[:, 0:wt], in_=X[xi][:, 0:wt], func=AF.Copy), x=[f"X{xi}"], w=[nm + "_pre"])
                xj = next_X()
                PE(lambda e, pre=pre, xj=xj: e.matmul(X[xj][:, 0:wt], lhsT=pswap[:], rhs=pre[:, 0:wt], start=True, stop=True),
                   r=[nm + "_pre", "pswap"], w=[f"X{xj}"])
                DV(lambda e, pre=pre: e.tensor_tensor(out=rope_t1[:, 0:wt], in0=pre[:, 0:wt], in1=cos_sb[:, 0:wt], op=ALU.mult),
                   r=[nm + "_pre", "cos_sb"], w=["s5p0"])
                DV(lambda e, xj=xj: e.tensor_tensor(out=rope_t2[:, 0:wt], in0=X[xj][:, 0:wt], in1=sin_sb[:, 0:wt], op=ALU.mult),
                   r=["sin_sb"], w=["s5p1"], x=[f"X{xj}"])
                DV(lambda e, dst=dst: e.tensor_tensor(out=dst[:, 0:wt], in0=rope_t1[:, 0:wt], in1=rope_t2[:, 0:wt], op=ALU.add),
                   r=["s5p0", "s5p1"], w=[nm + "T"])
                yield
            if not meta:
                DV(lambda e: e.tensor_tensor(out=qxT[:, 0:wt].rearrange("p (t i) -> p t i", t=nt), in0=qT[:, 0:wt].rearrange("p (t i) -> p t i", t=nt),
                                             in1=xibc[:, h, :].unsqueeze(1).broadcast_to([128, nt, 128]), op=ALU.mult),
                   r=["qT", "xibc"], w=["qxT"])

                def p1(e):
                    ins = None
                    for t in range(nt):
                        ins = e.matmul(X[0][:, 128 * t:128 * t + 128], lhsT=kT[:, 128 * t:128 * t + 128], rhs=qT[:, 128 * t:128 * t + 128], start=True, stop=True)
                    return ins
                PE(p1, r=["kT", "qT"], w=["X0"])
                DV(lambda e: e.tensor_tensor(out=scT_sb[:, 0:nt, :], in0=X[0][:, 0:wt].rearrange("p (t i) -> p t i", t=nt),
                                             in1=dmatT[:, h, :].unsqueeze(1).broadcast_to([128, nt, 128]), op=ALU.mult),
                   r=["dmatT"], w=["scT_sb"], x=["X0"])

            def p1b(e):
                ins = None
                for t in range(nt):
                    ins = e.transpose(bT_all[:, 128 * t:128 * t + 128], kT[:, 128 * t:128 * t + 128], ident_b[:])
                return ins
            PE(p1b, r=["kT", "ident_b"], w=["bT"])
            AC(lambda e: e.activation(out=kz_sb[:, 0:nt, :].rearrange("p t d -> p (t d)"), in_=bT_all[:, 0:wt], func=AF.Identity, scale=zeta[:, h:h + 1]),
               r=["zeta"], w=["kz_sb"], x=["bT"])
            for _ in range(LAG):
                yield

            def p2(e):
                ins = None
                for t in range(nt):
                    ins = e.matmul(X[1][:, 128 * t:128 * t + 128], lhsT=kz_sb[:, t, :], rhs=v_sb[:, t, 128 * h:128 * h + 128], start=True, stop=True)
                return ins
            PE(p2, r=["kz_sb", "v_sb"], w=["X1"])
            if not meta:
                AC(lambda e: e.activation(out=Sbf_all[:, 0, :], in_=S32[:, h, :], func=AF.Copy), r=[f"S32{h}"], w=["Sbf_all"])
            prev, prevn = S32[:, h, :], f"S32{h}"
            for t in range(nt):
                last = (t == nt - 1)
                dst_, dstn = (S32[:, h, :], f"S32{h}") if last else (Sall[:, t, :], "Sall")
                DV(lambda e, prev=prev, dst_=dst_, t=t: e.scalar_tensor_tensor(out=dst_, in0=prev, scalar=CONST["gammaC"][h],
                                                                                in1=X[1][:, 128 * t:128 * t + 128], op0=ALU.mult, op1=ALU.add),
                   r=[prevn], w=[dstn], x=["X1"])
                prev, prevn = dst_, dstn
            if meta:
                return
            AC(lambda e: e.activation(out=Sbf_all[:, 1:nt, :], in_=Sall[:, 0:nt - 1, :], func=AF.Copy), r=["Sall"], w=["Sbf_all"])
            for _ in range(LAG):
                yield

            def p3(e):
                ins = None
                for t in range(nt):
                    e.matmul(X[2][:, 128 * t:128 * t + 128], lhsT=scT_sb[:, t, :], rhs=v_sb[:, t, 128 * h:128 * h + 128], start=True, stop=False)
                    ins = e.matmul(X[2][:, 128 * t:128 * t + 128], lhsT=qxT[:, 128 * t:128 * t + 128], rhs=Sbf_all[:, t, :], start=False, stop=True)
                return ins
            PE(p3, r=["scT_sb", "v_sb", "qxT", "Sbf_all"], w=["X2"])
            for t in range(nt):
                DV(lambda e, t=t: e.bn_stats(out=gst[:, t, :], in_=X[2][:, 128 * t:128 * t + 128]), w=["gst"], x=["X2"])
            for t in range(nt):
                DV(lambda e, t=t: e.bn_aggr(out=gmv[:, t, :], in_=gst[:, t, :]), r=["gst"], w=["gmv"])
            AC(lambda e: e.activation(out=gsc[:, 0, 0:nt], in_=gmv[:, 0:nt, 1], func=AF.Sqrt, bias=epsb[:], scale=1.0), r=["gmv", "epsb"], w=["gsc"])
            DV(lambda e: e.reciprocal(out=gsc[:, 1, 0:nt], in_=gsc[:, 0, 0:nt]), r=["gsc"], w=["gsc"])
            DV(lambda e: e.scalar_tensor_tensor(out=gsc[:, 2, 0:nt], in0=gmv[:, 0:nt, 0], scalar=-1.0, in1=gsc[:, 1, 0:nt], op0=ALU.mult, op1=ALU.mult),
               r=["gmv", "gsc"], w=["gsc"])
            for t in range(nt):
                AC(lambda e, t=t: e.activation(out=on_sb[:, t, :], in_=X[2][:, 128 * t:128 * t + 128], func=AF.Identity,
                                               scale=gsc[:, 1, t:t + 1], bias=gsc[:, 2, t:t + 1]),
                   r=["gsc"], w=["on_sb"], x=["X2"])
            for _ in range(LAG):
                yield

            def p4(e):
                ins = None
                for t in range(nt):
                    ins = e.transpose(bT_all[:, 512 + 128 * t:512 + 128 * t + 128], on_sb[:, t, :], ident_b[:])
                return ins
            PE(p4, r=["on_sb", "ident_b"], w=["bT"])
            AC(lambda e: e.activation(out=gaff[:, 0:wt], in_=bT_all[:, 512:512 + wt], func=AF.Identity, scale=gn_gb[:, h:h + 1], bias=gn_gb[:, 6 + h:7 + h]),
               r=["gn_gb"], w=["gaff"], x=["bT"])
            DV(lambda e: e.tensor_tensor(out=yT[:, 2 + h, 0:wt], in0=gaff[:, 0:wt], in1=sgT[:, hl, 0:wt], op=ALU.mult), r=["gaff", "sgT"], w=["yT"])
            yield

        def gen_Apre(st_idx):
            meta = st_idx < 0
            nt = 1 if meta else NT
            hT, hTn = hTs[(st_idx + 1) % 2], f"hT{(st_idx + 1) % 2}"
            for t in range(nt):
                xb_ = xs[t % 2]
                xn_ = f"xs{t % 2}"
                if meta:
                    DV(lambda e, xb_=xb_: e.memset(xb_[:], 0.0), w=[xn_])
                    P.dma("sp", lambda e, xb_=xb_: e.dma_start(out=xb_[112:128, :], in_=meta_d), writes=[xn_])
                else:
                    r0 = st_idx * W + t * 128
                    P.dma("sp", lambda e, r0=r0, xb_=xb_: e.dma_start(out=xb_[:], in_=x_d[r0:r0 + 128, :]), writes=[xn_])
                yield from ln_tile([xb_[:, 0:512], xb_[:, 512:1024]], [xn_], None, t, "T", gb=lnin_gb, gbn="lnin_gb", destT=hT, dname=hTn, t=t,
                                   tb=bT_all, tbn="bT")
                yield
            if meta:
                DV(lambda e: e.memset(hT[:, :, 0:112], 0.0), r=[hTn], w=[hTn])

        def gen_Amain(st_idx):
            meta = st_idx < 0
            nt = 1 if meta else NT
            wt = nt * 128
            slot0 = 0 if meta else 128 + st_idx * W
            hT, hTn = hTs[(st_idx + 1) % 2], f"hT{(st_idx + 1) % 2}"
            P.dma("sp", lambda e: e.dma_start(out=cos_sb[:, 0:wt], in_=cosT_d[:, slot0:slot0 + wt]), writes=["cos_sb"])
            P.dma("sp", lambda e: e.dma_start(out=sin_sb[:, 0:wt], in_=sinT_d[:, slot0:slot0 + wt]), writes=["sin_sb"])
            (wb_, wn_), = WA.take()
            for U in range(2):
                xi = next_X()
                mm8(X[xi], f"X{xi}", v8(wb_), wn_, 128 * U, hT, hTn, wt)
                AC(lambda e, U=U, xi=xi: e.activation(out=uT[:, U, 0:wt], in_=X[xi][:, 0:wt], func=AF.Copy), x=[f"X{xi}"], w=["uT"])
            yield
            def gen_s5():
                cnt = 0
                for U in range(2):
                    for sbi in range(wt // SB):
                        yield from s5_block(U, sbi * SB, meta, cnt % 2)
                        cnt += 1
                    if not meta:
                        AC(lambda e, U=U: e.activation(out=ygb[:, U, 0:wt], in_=ypre[:, 0:wt], func=AF.Gelu_apprx_tanh), r=["ypre"], w=["ygb"])
                    yield
                if not meta:
                    for U2 in range(2):
                        xi = next_X()

                        def mmg(e, U2=U2, xi=xi):
                            e.matmul(X[xi][:, 0:wt], lhsT=wglu[:, 0, 128 * U2:128 * U2 + 128], rhs=ygb[:, 0, 0:wt], start=True, stop=False)
                            return e.matmul(X[xi][:, 0:wt], lhsT=wglu[:, 1, 128 * U2:128 * U2 + 128], rhs=ygb[:, 1, 0:wt], start=False, stop=True)
                        PE(mmg, r=["wglu", "ygb"], w=[f"X{xi}"])
                        AC(lambda e, U2=U2, xi=xi: e.activation(out=sgl[:, 0:wt], in_=X[xi][:, 0:wt], func=AF.Sigmoid, bias=s5_fm[:, 2 + U2:3 + U2], scale=1.0),
                           r=["s5_fm"], w=["sgl"], x=[f"X{xi}"])
                        DV(lambda e, U2=U2: e.tensor_tensor(out=yT[:, U2, 0:wt], in0=ygb[:, U2, 0:wt], in1=sgl[:, 0:wt], op=ALU.mult), r=["ygb", "sgl"], w=["yT"])
                    yield

            def gen_ret():
                for hp in range(3):
                    (wb_, wn_), = WA.take()
                    for tp in range((nt + 1) // 2):
                        xi = next_X()
                        tl = [t for t in (2 * tp, 2 * tp + 1) if t < nt]

                        def mmv(e, tl=tl, xi=xi, wb_=wb_):
                            ins = None
                            for ii, t in enumerate(tl):
                                for k in range(8):
                                    ins = e.matmul(X[xi][:, 256 * ii:256 * ii + 256], lhsT=hT[:, k, 128 * t:128 * t + 128], rhs=v8(wb_)[:, k, 0:256],
                                                   start=(k == 0), stop=(k == 7))
                            return ins
                        PE(mmv, r=[wn_, hTn], w=[f"X{xi}"])
                        AC(lambda e, tl=tl, xi=xi, hp=hp: e.activation(
                            out=v_sb[:, tl[0]:tl[0] + len(tl), 256 * hp:256 * hp + 256],
                            in_=X[xi][:, 0:256 * len(tl)].rearrange("p (t c) -> p t c", t=len(tl)), func=AF.Copy),
                           x=[f"X{xi}"], w=["v_sb"])
                    yield
                    if not meta:
                        (wb_, wn_), = WA.take()
                        for hl in range(2):
                            xi = next_X()
                            mm8(X[xi], f"X{xi}", v8(wb_), wn_, 128 * hl, hT, hTn, wt)
                            AC(lambda e, hl=hl, xi=xi: e.activation(out=sgT[:, hl, 0:wt], in_=X[xi][:, 0:wt], func=AF.Silu), x=[f"X{xi}"], w=["sgT"])
                        yield
                    for hl in range(2):
                        yield from head(hp, hl, meta, nt, wt, hT, hTn)

            g1, g2 = gen_ret(), gen_s5()
            live = [g1, g2]
            while live:
                for g in list(live):
                    try:
                        next(g)
                        yield
                    except StopIteration:
                        live.remove(g)

        def gen_E(st_idx):
            hT, hTn = hTs[(st_idx + 1) % 2], f"hT{(st_idx + 1) % 2}"
            for cb in range(4):
                (wb_, wn_), = WB.take()
                for hl in range(2):
                    dm = 2 * cb + hl
                    bi_ = next_bA()
                    mm8(bA[bi_], f"bA{bi_}", v8(wb_), wn_, 128 * hl, yT, "yT", W, resid=hT[:, dm, :], rname=hTn)
                    AC(lambda e, dm=dm, bi_=bi_: e.activation(out=rT[:, dm, :], in_=bA[bi_][:, 0:W], func=AF.Copy), x=[f"bA{bi_}"], w=["rT"])
                yield

        def gen_B(st_idx):
            for t in range(NT):
                rT_to_tokmajor(t)
                yield from ln_tile([bB0h[:, 0:512], bB0h[:, 512:1024]], [], ["bB0", "bB0"], 4 + t, "T", gb=ln1_gb, gbn="ln1_gb", destT=h1T, dname="h1T", t=t,
                                   tb=bB1h, tbn="bB1", xbh=[(relu_t[0][:], "relu_t0"), (relu_t[1][:], "relu_t1")])
                yield
            for ub in range(16):
                (wb_, wn_), = WB.take()
                for hl in range(2):
                    ff = 2 * ub + hl
                    bi_ = next_bA()
                    mm8(bA[bi_], f"bA{bi_}", v8(wb_), wn_, 128 * hl, h1T, "h1T", W)
                    rt = relu_t[ff % 2]
                    AC(lambda e, rt=rt, bi_=bi_: e.activation(out=rt[:], in_=bA[bi_][:, 0:W], func=AF.Relu), x=[f"bA{bi_}"], w=[f"relu_t{ff % 2}"])
                    AC(lambda e, rt=rt, ff=ff: e.activation(out=aT[:, ff, :], in_=rt[:], func=AF.Square), r=[f"relu_t{ff % 2}"], w=["aT"])
                    yield
            for dm in range(8):
                (w0, n0), (w1, n1) = WB.take(2)
                bi_ = next_bA()

                for q4 in range(4):
                    def fnd(e, w0=w0, w1=w1, dm=dm, bi_=bi_, q4=q4):
                        ins = None
                        for k in range(8 * q4, 8 * q4 + 8):
                            wv_ = v16(w0) if k < 16 else v16(w1)
                            ins = e.matmul(bA[bi_][:, 0:W], lhsT=wv_[:, k % 16, :], rhs=aT[:, k, :], start=(k == 0), stop=False)
                        if q4 == 3:
                            ins = e.matmul(bA[bi_][:, 0:W], lhsT=aI[:], rhs=h1T[:, dm, :], start=False, stop=True)
                        return ins
                    PE(fnd, r=[n0, n1, "aT", "h1T", "aI"], w=[f"bA{bi_}"])
                    if q4 < 3:
                        yield
                AC(lambda e, dm=dm, bi_=bi_: e.activation(out=rT[:, dm, :], in_=bA[bi_][:, 0:W], func=AF.Copy), x=[f"bA{bi_}"], w=["rT"])
                yield
            for t in range(NT):
                rT_to_tokmajor(t)
                r0 = st_idx * W + t * 128
                yield from ln_tile([bB0h[:, 0:512], bB0h[:, 512:1024]], [], ["bB0", "bB0"], 4 + t, "O", out_rows=(r0, r0 + 128))
                yield

        def run(g):
            for _ in g:
                pass

        def interleave(gb, ga, a_per_b):
            done_a = done_b = False
            acc = 0.0
            na = nb = 0
            while not (done_a and done_b):
                if not done_b:
                    try:
                        next(gb)
                        nb += 1
                    except StopIteration:
                        done_b = True
                acc += a_per_b
                while (acc >= 1.0 or done_b) and not done_a:
                    acc -= 1.0
                    try:
                        next(ga)
                        na += 1
                    except StopIteration:
                        done_a = True
                if done_a:
                    acc = 0.0
            return na, nb

        def chain(*gs):
            for g in gs:
                if g is not None:
                    yield from g

        def spread(gmain, gextra, every):
            n = 0
            extra_live = gextra is not None
            for _ in gmain:
                yield
                n += 1
                if extra_live and n % every == 0:
                    try:
                        next(gextra)
                        yield
                    except StopIteration:
                        extra_live = False
            if extra_live:
                for _ in gextra:
                    yield

        ratio = [A_PER_B]
        run(gen_Apre(-1))
        run(spread(gen_Amain(-1), gen_Apre(0), 3))
        run(spread(gen_Amain(0), gen_Apre(1) if NST > 1 else None, PRE_EVERY))
        for s in range(NST):
            run(gen_E(s))
            if s + 1 < NST:
                a_stream = spread(gen_Amain(s + 1), gen_Apre(s + 2) if s + 2 < NST else None, PRE_EVERY)
                if INTERLEAVE:
                    na_, nb_ = interleave(gen_B(s), a_stream, ratio[0])
                    ratio[0] = na_ / max(nb_, 1)
                else:
                    run(gen_B(s))
                    run(a_stream)
            else:
                run(gen_B(s))
        P.emit(final_wait_ops=out_dma_ops)
    return nc


_NC_CACHE = {}


def _prep_inputs(inp):
    f = lambda a: np.ascontiguousarray(np.asarray(a, dtype=np.float32))
    pk = lambda v: np.ascontiguousarray(f(v).reshape(-1, 128).T)
    shared = {}
    shared["meta"] = f(inp["meta_tokens"])
    shared["w_in"] = f(inp["w_in"][0])
    shared["w_out"] = f(inp["w_out"][0])
    shared["w_up"] = f(inp["w_up"][0])
    shared["w_down"] = f(inp["w_down"][0])
    shared["w_glu"] = f(inp["s5_w_glu"][0])
    shared["lnin_gb"] = np.ascontiguousarray(np.concatenate([pk(inp["ln_in_g"]), pk(inp["ln_in_b"])], axis=1))
    shared["ln1_gb"] = np.ascontiguousarray(np.concatenate([pk(inp["ln1_g"][0]), pk(inp["ln1_b"][0])], axis=1))
    shared["ln2_g"] = f(inp["ln2_g"][0]).reshape(1, D)
    shared["ln2_b"] = f(inp["ln2_b"][0]).reshape(1, D)
    st = lambda a: np.ascontiguousarray(f(a).reshape(8, 128).T)
    ldt = np.repeat(f(inp["s5_log_dt"][0])[:, None], 64, axis=1)
    shared["s5_sc"] = np.ascontiguousarray(np.concatenate([st(inp["s5_lambda_re"][0]), st(inp["s5_lambda_im"][0]), st(ldt)], axis=1))

    def bl(a):
        return f(a).reshape(8, 2, 64, 16).transpose(1, 2, 0, 3).reshape(128, 8, 16)
    shared["s5_b"] = np.ascontiguousarray(np.stack([bl(inp["s5_b_re"][0]), bl(inp["s5_b_im"][0])], axis=1))
    cl = lambda a: bl(f(a).transpose(0, 2, 1))
    shared["s5_c"] = np.ascontiguousarray(np.stack([cl(inp["s5_c_re"][0]), cl(inp["s5_c_im"][0])], axis=1))
    shared["s5_fm"] = np.ascontiguousarray(np.concatenate([pk(inp["s5_d"][0]), pk(inp["s5_b_glu"][0])], axis=1))
    shared["gn_gb"] = np.ascontiguousarray(np.concatenate([pk(inp["ret_gn_g"][0]), pk(inp["ret_gn_b"][0])], axis=1))
    for k in ("ident_f", "pswap", "cosT", "sinT", "dmatT", "xi_bc", "zeta", "maskB"):
        shared[k] = CONST[k]
    x = f(inp["x"])
    maps = []
    for c in range(8):
        m = dict(shared)
        m["x"] = x[c]
        maps.append(m)
    return maps


def kernel(**inputs):
    if "nc" not in _NC_CACHE:
        _NC_CACHE["nc"] = build()
    nc = _NC_CACHE["nc"]
    maps = _prep_inputs(inputs)
    res = run_bass_kernel_spmd(nc, maps, core_ids=list(range(8)))
    out = np.stack([np.asarray(r["out"], dtype=np.float32) for r in res.results], axis=0)
    return out
```

```python
import math
import numpy as np
import concourse.bass as bass
import concourse.mybir as mybir
from concourse.bass_utils import run_bass_kernel_spmd
from contextlib import ExitStack

F32 = mybir.dt.float32
BF16 = mybir.dt.bfloat16
ALU = mybir.AluOpType
AF = mybir.ActivationFunctionType

D = 1024
SEQ = 4096
NMETA = 16
NH = 6
NT = 4
W = NT * 128
NST = SEQ // W
SB = 128
ALPHA = 2.0 ** 0.25
LN_EPS = 1e-5
NSLOT = 33 * 128
ENG_NAMES = ("pe", "act", "dve", "pool", "sp")
STRICT_SYNC = False
INTERLEAVE = True
A_PER_B = 2.6
LAG = 2
PRE_EVERY = 12


class Prog:
    def __init__(self, nc, n_dma_sems=12):
        self.nc = nc
        self.ops = []
        self.res_w = {}
        self.res_r = {}
        self.cnt = {e: 0 for e in ENG_NAMES}
        self.n_dma_sems = n_dma_sems
        self.dma_cnt = {}
        self.dma_rr = {"sp": 0, "pool": 0, "act": 0}
        self.dma_last = {}

    def _deps(self, reads, writes, excl):
        deps = {}
        for r in list(reads) + list(excl):
            if r in self.res_w:
                deps[self.res_w[r]] = True
        for w in list(writes):
            if w in self.res_w:
                deps.setdefault(self.res_w[w], False)
            for rd in self.res_r.get(w, ()):
                deps.setdefault(rd, False)
        for w in excl:
            for rd in self.res_r.get(w, ()):
                deps.setdefault(rd, False)
        return deps

    def _commit(self, oid, reads, writes, excl):
        for r in list(reads) + list(excl):
            self.res_r.setdefault(r, []).append(oid)
        for w in list(writes):
            self.res_w[w] = oid
            self.res_r[w] = []

    def op(self, eng, fn, reads=(), writes=(), excl=()):
        oid = len(self.ops)
        deps = self._deps(reads, writes, excl)
        self.cnt[eng] += 1
        self.ops.append(dict(eng=eng, fn=fn, deps=deps, tok=("E", eng, self.cnt[eng]), dma=False))
        self._commit(oid, reads, writes, excl)
        return oid

    def dma(self, queue, fn, reads=(), writes=()):
        oid = len(self.ops)
        deps = self._deps(reads, writes, ())
        slot = self.dma_rr[queue]
        self.dma_rr[queue] = (slot + 1) % self.n_dma_sems
        key = (queue, slot)
        if key in self.dma_last:
            deps[self.dma_last[key]] = True
        self.dma_cnt[key] = self.dma_cnt.get(key, 0) + 1
        self.dma_last[key] = oid
        self.ops.append(dict(eng=queue, fn=fn, deps=deps, tok=("D", key, 16 * self.dma_cnt[key]), dma=True))
        self._commit(oid, reads, writes, ())
        return oid

    def emit(self, final_wait_ops=()):
        nc = self.nc
        with ExitStack() as es:
            esem = {e: es.enter_context(nc.semaphore(f"s_{e}")) for e in ENG_NAMES}
            dsem = {}
            for q in ("sp", "pool", "act"):
                for s in range(self.n_dma_sems):
                    if (q, s) in self.dma_cnt:
                        dsem[(q, s)] = es.enter_context(nc.semaphore(f"d_{q}{s}"))
            block = es.enter_context(nc.Block())

            def semval(tok):
                if tok[0] == "E":
                    return esem[tok[1]], tok[2]
                return dsem[tok[1]], tok[2]

            def run_engine(ename, eobj):
                known = {}
                for oid, o in enumerate(self.ops):
                    if o["eng"] != ename:
                        continue
                    for d in sorted(o["deps"]):
                        do = self.ops[d]
                        if (not o["dma"]) and (not do["dma"]) and do["eng"] == ename:
                            if ename == "pe" or not (o["deps"][d] or STRICT_SYNC):
                                continue
                        sem, val = semval(do["tok"])
                        k = id(sem)
                        if known.get(k, 0) >= val:
                            continue
                        eobj.wait_ge(sem, val)
                        known[k] = val
                    ins = o["fn"](eobj)
                    sem, val = semval(o["tok"])
                    ins.then_inc(sem, 16 if o["dma"] else 1)
                if ename == "sp":
                    for oid in final_wait_ops:
                        sem, val = semval(self.ops[oid]["tok"])
                        eobj.wait_ge(sem, val)

            @block.tensor
            def _(e):
                run_engine("pe", e)

            @block.scalar
            def _(e):
                run_engine("act", e)

            @block.vector
            def _(e):
                run_engine("dve", e)

            @block.gpsimd
            def _(e):
                run_engine("pool", e)

            @block.sync
            def _(e):
                run_engine("sp", e)


def _host_consts():
    c = {}
    c["ident_f"] = np.eye(128, dtype=np.float32)
    pm = np.zeros((128, 128), np.float32)
    for dp in range(128):
        pm[(dp + 64) % 128, dp] = 1.0
    c["pswap"] = pm
    pos = (np.arange(NSLOT, dtype=np.float32) - 112.0).astype(np.float32)
    inv_freq = (1.0 / (10000.0 ** (np.arange(0, 128, 2, dtype=np.float32) / 128.0))).astype(np.float32)
    ang = (pos[:, None] * inv_freq[None, :]).astype(np.float32)
    cs, sn = np.cos(ang).astype(np.float32), np.sin(ang).astype(np.float32)
    c["cosT"] = np.ascontiguousarray(np.concatenate([cs, cs], axis=1).T)
    c["sinT"] = np.ascontiguousarray(np.concatenate([-sn, sn], axis=1).T)
    lg = np.log1p(-np.exp2(-5.0 - np.arange(NH, dtype=np.float32))).astype(np.float32)
    idx = np.arange(128, dtype=np.float32)
    scale = 128.0 ** -0.5
    diff = idx[None, :] - idx[:, None]
    dm = np.where(diff[None] >= 0, np.exp(np.maximum(diff, 0.0)[None] * lg[:, None, None]), 0.0) * scale
    c["dmatT"] = np.ascontiguousarray(dm.transpose(1, 0, 2)).astype(np.float32)
    xi = np.exp((idx + 1.0)[None] * lg[:, None]).astype(np.float32)
    c["xi_bc"] = np.ascontiguousarray(np.broadcast_to(xi[None], (128, NH, 128))).astype(np.float32)
    zeta = (np.exp((127.0 - idx)[None] * lg[:, None]) * scale).astype(np.float32)
    c["zeta"] = np.ascontiguousarray(zeta.T)
    c["gammaC"] = [float(np.exp(128.0 * lg[h])) for h in range(NH)]
    mb = np.zeros((128, 4, 8), np.float32)
    for gl in range(2):
        for j4 in range(4):
            mb[gl * 64:(gl + 1) * 64, j4, 2 * j4 + gl] = 1.0
    c["maskB"] = mb
    return c


CONST = _host_consts()


def build():
    nc = bass.Bass("TRN2", target_bir_lowering=False)

    def din(name, shape):
        return nc.dram_tensor(name, list(shape), F32, kind="ExternalInput").ap()

    x_d = din("x", [SEQ, D])
    meta_d = din("meta", [NMETA, D])
    w_in_d = din("w_in", [D, 3328])
    w_out_d = din("w_out", [D, D])
    w_up_d = din("w_up", [D, 4 * D])
    w_down_d = din("w_down", [4 * D, D])
    w_glu_d = din("w_glu", [256, 256])
    lnin_gb_d = din("lnin_gb", [128, 16])
    ln1_gb_d = din("ln1_gb", [128, 16])
    ln2_g_d = din("ln2_g", [1, D])
    ln2_b_d = din("ln2_b", [1, D])
    s5_sc_d = din("s5_sc", [128, 24])
    s5_b_d = din("s5_b", [128, 2, 8, 16])
    s5_c_d = din("s5_c", [128, 2, 8, 16])
    s5_fm_d = din("s5_fm", [128, 4])
    gn_gb_d = din("gn_gb", [128, 12])
    ident_d = din("ident_f", [128, 128])
    pswap_d = din("pswap", [128, 128])
    cosT_d = din("cosT", [128, NSLOT])
    sinT_d = din("sinT", [128, NSLOT])
    dmatT_d = din("dmatT", [128, NH, 128])
    xibc_d = din("xi_bc", [128, NH, 128])
    zeta_d = din("zeta", [128, NH])
    maskB_d = din("maskB", [128, 4, 8])
    out_d = nc.dram_tensor("out", [SEQ, D], F32, kind="ExternalOutput").ap()

    with ExitStack() as es:
        def sb(name, shape, dt=F32):
            return es.enter_context(nc.sbuf_tensor(name, list(shape), dt))

        def psum(name, shape, dt=F32):
            return es.enter_context(nc.psum_tensor(name, list(shape), dt))

        xs = [sb(f"xs{i}", [128, D]) for i in range(2)]
        xnb = [sb(f"xnb{i}", [128, D], BF16) for i in range(2)]
        tmpT = sb("tmpT", [128, 8, 128], BF16)
        hTs = [sb(f"hT{i}", [128, 8, W], BF16) for i in range(2)]
        h1T = sb("h1T", [128, 8, W], BF16)
        wA = [sb(f"wA{i}", [128, 2048], BF16) for i in range(3)]
        wB = [sb(f"wB{i}", [128, 2048], BF16) for i in range(4)]
        uT = sb("uT", [128, 2, W], BF16)
        q_pre = sb("q_pre", [128, W], BF16)
        k_pre = sb("k_pre", [128, W], BF16)
        qT = sb("qT", [128, W], BF16)
        kT = sb("kT", [128, W], BF16)
        qxT = sb("qxT", [128, W], BF16)
        sgT = sb("sgT", [128, 2, W], BF16)
        v_sb = sb("v_sb", [128, NT, 768], BF16)
        yT = sb("yT", [128, 8, W], BF16)
        aT = sb("aT", [128, 32, W], BF16)
        relu_t = [sb(f"relu_t{i}", [128, W], BF16) for i in range(2)]
        rT = sb("rT", [128, 8, W], BF16)
        cos_sb = sb("cos_sb", [128, W])
        sin_sb = sb("sin_sb", [128, W])
        ln2g = sb("ln2g", [128, D])
        ln2b = sb("ln2b", [128, D])
        ident_f = sb("ident_fs", [128, 128])
        ident_b = sb("ident_b", [128, 128], BF16)
        aI = sb("aI", [128, 128], BF16)
        pswap = sb("pswap_s", [128, 128], BF16)
        dmatT = sb("dmatT_s", [128, NH, 128])
        xibc = sb("xibc_s", [128, NH, 128])
        zeta = sb("zeta_s", [128, NH])
        maskB = sb("maskB_s", [128, 4, 8])
        lnin_gb = sb("lnin_gb_s", [128, 16])
        ln1_gb = sb("ln1_gb_s", [128, 16])
        gn_gb = sb("gn_gb_s", [128, 12])
        s5_fm = sb("s5_fm_s", [128, 4])
        epsb = sb("epsb", [128, 1])
        stt = sb("stt", [128, 8, 12])
        mv = sb("mv", [128, 8, 8])
        scT_sb = sb("scT_sb", [128, NT, 128], BF16)
        kz_sb = sb("kz_sb", [128, NT, 128], BF16)
        on_sb = sb("on_sb", [128, NT, 128], BF16)
        gaff = sb("gaff", [128, W], BF16)
        S32 = sb("S32", [128, NH, 128])
        Sall = sb("Sall", [128, 3, 128])
        Sbf_all = sb("Sbf_all", [128, NT, 128], BF16)
        gst = sb("gst", [128, NT, 6])
        gmv = sb("gmv", [128, NT, 2])
        gsc = sb("gsc", [128, 3, NT])
        s5_sc = sb("s5_sc_s", [128, 24])
        s5w = sb("s5w", [128, 40, 8])
        WBT = sb("WBT", [128, 16, 128], BF16)
        CT = sb("CT", [128, 16, 128], BF16)
        Etab = sb("Etab", [128, 2, 8, SB])
        s5m = sb("s5m", [128, 4, 4 * SB])
        s5p = sb("s5p", [128, 4, 4 * SB])
        s5x = [sb(f"s5x{i}", [128, 2, 4 * SB], BF16) for i in range(2)]
        Xc = sb("Xc", [128, 2, 8])
        ypre = sb("ypre", [128, W])
        ygb = sb("ygb", [128, 2, W], BF16)
        sgl = sb("sgl", [128, W], BF16)
        wglu = sb("wglu", [128, 2, 256], BF16)
        rope_t1 = s5p[:, 0, :]
        rope_t2 = s5p[:, 1, :]
        PALL = ["s5p0", "s5p1", "s5p2", "s5p3"]
        pfl = s5p[:].rearrange("p a w -> p (a w)")
        s5_b = pfl[:, 0:256].rearrange("p (r j h) -> p r j h", r=2, j=8)
        s5_c = pfl[:, 256:512].rearrange("p (r j h) -> p r j h", r=2, j=8)
        s5bb = pfl[:, 512:768].rearrange("p (r j h) -> p r j h", r=2, j=8)
        s5ex = pfl[:, 768:896].rearrange("p (g h) -> p g h", g=8)
        s5tmp = pfl[:, 1024:1536].rearrange("p (x j h) -> p x j h", x=4, j=8)

        bA = [psum(f"bA{i}", [128, 512]) for i in range(2)]
        bB = [psum(f"bB{i}", [128, 512]) for i in range(2)]
        bT = psum("bT", [128, 1024], BF16)
        X = [psum(f"X{i}", [128, 512]) for i in range(3)]
        bB0h = bB[0][:].bitcast(BF16)
        bB1h = bB[1][:].bitcast(BF16)

        P = Prog(nc)
        Etmp = aT[:, 0:16, :].bitcast(F32).rearrange("p a b -> p (a b)").rearrange("p (x j n) -> p x j n", x=4, j=8)
        DV = lambda fn, r=(), w=(), x=(): P.op("dve", fn, r, w, x)
        AC = lambda fn, r=(), w=(), x=(): P.op("act", fn, r, w, x)
        PE = lambda fn, r=(), w=(): P.op("pe", fn, r, w)

        def ld(dst, src, name, q="sp"):
            P.dma(q, lambda e: e.dma_start(out=dst, in_=src), writes=name if isinstance(name, list) else [name])

        ld(ident_f[:], ident_d, "ident_f")
        ld(dmatT[:], dmatT_d, "dmatT")
        ld(zeta[:], zeta_d, "zeta")
        ld(maskB[:], maskB_d, "maskB")
        ld(lnin_gb[:], lnin_gb_d, "lnin_gb")
        ld(ln1_gb[:], ln1_gb_d, "ln1_gb")
        ld(gn_gb[:], gn_gb_d, "gn_gb")
        ld(s5_fm[:], s5_fm_d, "s5_fm")
        ld(s5_sc[:], s5_sc_d, "s5_sc")
        ld(s5_b, s5_b_d, PALL)
        ld(s5_c, s5_c_d, PALL)
        ld(ln2g[:], ln2_g_d[0].partition_broadcast(128), "ln2g")
        ld(ln2b[:], ln2_b_d[0].partition_broadcast(128), "ln2b")
        ld(pswap[:], pswap_d, "pswap", q="pool")
        ld(xibc[:], xibc_d, "xibc")
        ld(wglu[:], w_glu_d.rearrange("(k p) c -> p k c", p=128), "wglu", q="pool")
        DV(lambda e: e.memset(epsb[:], LN_EPS), w=["epsb"])
        AC(lambda e: e.activation(out=ident_b[:], in_=ident_f[:], func=AF.Copy), r=["ident_f"], w=["ident_b"])
        AC(lambda e: e.activation(out=aI[:], in_=ident_f[:], func=AF.Identity, scale=ALPHA), r=["ident_f"], w=["aI"])
        DV(lambda e: e.memset(S32[:], 0.0), w=[f"S32{h}" for h in range(NH)])
        DV(lambda e: e.memset(Xc[:], 0.0), w=["Xc"])

        SL = lambda i: s5w[:, i, :]
        lam_re, lam_im, log_dt = s5_sc[:, 0:8], s5_sc[:, 8:16], s5_sc[:, 16:24]
        (I_DT, I_AR, I_TH, I_RHO, I_PHI, I_U, I_S, I_C, I_TS, I_TC, I_CC, I_SS, I_CS, I_N, I_RN,
         I_LBR, I_LBI, I_DEN, I_NR, I_QR, I_QI, I_T1, I_T2, I_NS) = range(24)

        def dv2(out, a, b, op):
            DV(lambda e: e.tensor_tensor(out=out, in0=a, in1=b, op=op), r=["s5w", "s5_sc"], w=["s5w"])

        AC(lambda e: e.activation(out=SL(I_DT), in_=log_dt, func=AF.Exp), r=["s5_sc"], w=["s5w"])
        dv2(SL(I_AR), lam_re, SL(I_DT), ALU.mult)
        dv2(SL(I_TH), lam_im, SL(I_DT), ALU.mult)
        AC(lambda e: e.activation(out=SL(I_RHO), in_=SL(I_AR), func=AF.Exp), r=["s5w"], w=["s5w"])
        DV(lambda e: e.tensor_scalar(out=SL(I_PHI), in0=SL(I_TH), scalar1=1.0 / 32.0, scalar2=None, op0=ALU.mult), r=["s5w"], w=["s5w"])
        dv2(SL(I_U), SL(I_PHI), SL(I_PHI), ALU.mult)
        DV(lambda e: e.tensor_copy(out=SL(I_S), in_=SL(I_PHI)), r=["s5w"], w=["s5w"])
        DV(lambda e: e.tensor_copy(out=SL(I_TS), in_=SL(I_PHI)), r=["s5w"], w=["s5w"])
        DV(lambda e: e.memset(SL(I_C), 1.0), r=["s5w"], w=["s5w"])
        DV(lambda e: e.memset(SL(I_TC), 1.0), r=["s5w"], w=["s5w"])
        for kk in range(1, 7):
            cs_ = -1.0 / ((2 * kk) * (2 * kk + 1))
            cc_ = -1.0 / ((2 * kk - 1) * (2 * kk))
            DV(lambda e, c_=cs_: e.scalar_tensor_tensor(out=SL(I_TS), in0=SL(I_TS), scalar=c_, in1=SL(I_U), op0=ALU.mult, op1=ALU.mult), r=["s5w"], w=["s5w"])
            dv2(SL(I_S), SL(I_S), SL(I_TS), ALU.add)
            DV(lambda e, c_=cc_: e.scalar_tensor_tensor(out=SL(I_TC), in0=SL(I_TC), scalar=c_, in1=SL(I_U), op0=ALU.mult, op1=ALU.mult), r=["s5w"], w=["s5w"])
            dv2(SL(I_C), SL(I_C), SL(I_TC), ALU.add)
        for _ in range(5):
            dv2(SL(I_CC), SL(I_C), SL(I_C), ALU.mult)
            dv2(SL(I_SS), SL(I_S), SL(I_S), ALU.mult)
            dv2(SL(I_CS), SL(I_C), SL(I_S), ALU.mult)
            dv2(SL(I_C), SL(I_CC), SL(I_SS), ALU.subtract)
            DV(lambda e: e.tensor_scalar(out=SL(I_S), in0=SL(I_CS), scalar1=2.0, scalar2=None, op0=ALU.mult), r=["s5w"], w=["s5w"])
        dv2(SL(I_CC), SL(I_C), SL(I_C), ALU.mult)
        dv2(SL(I_SS), SL(I_S), SL(I_S), ALU.mult)
        dv2(SL(I_N), SL(I_CC), SL(I_SS), ALU.add)
        AC(lambda e: e.activation(out=SL(I_NS), in_=SL(I_N), func=AF.Sqrt), r=["s5w"], w=["s5w"])
        DV(lambda e: e.reciprocal(out=SL(I_RN), in_=SL(I_NS)), r=["s5w"], w=["s5w"])
        dv2(SL(I_C), SL(I_C), SL(I_RN), ALU.mult)
        dv2(SL(I_S), SL(I_S), SL(I_RN), ALU.mult)
        dv2(SL(I_LBR), SL(I_RHO), SL(I_C), ALU.mult)
        dv2(SL(I_LBI), SL(I_RHO), SL(I_S), ALU.mult)
        dv2(SL(I_T1), lam_re, lam_re, ALU.mult)
        dv2(SL(I_T2), lam_im, lam_im, ALU.mult)
        dv2(SL(I_DEN), SL(I_T1), SL(I_T2), ALU.add)
        DV(lambda e: e.reciprocal(out=SL(I_DEN), in_=SL(I_DEN)), r=["s5w"], w=["s5w"])
        DV(lambda e: e.tensor_scalar(out=SL(I_NR), in0=SL(I_LBR), scalar1=-1.0, scalar2=None, op0=ALU.add), r=["s5w"], w=["s5w"])
        dv2(SL(I_T1), SL(I_NR), lam_re, ALU.mult)
        dv2(SL(I_T2), SL(I_LBI), lam_im, ALU.mult)
        dv2(SL(I_QR), SL(I_T1), SL(I_T2), ALU.add)
        dv2(SL(I_QR), SL(I_QR), SL(I_DEN), ALU.mult)
        dv2(SL(I_T1), SL(I_LBI), lam_re, ALU.mult)
        dv2(SL(I_T2), SL(I_NR), lam_im, ALU.mult)
        dv2(SL(I_QI), SL(I_T1), SL(I_T2), ALU.subtract)
        dv2(SL(I_QI), SL(I_QI), SL(I_DEN), ALU.mult)
        qr_bc = SL(I_QR).unsqueeze(2).broadcast_to([128, 8, 16])
        qi_bc = SL(I_QI).unsqueeze(2).broadcast_to([128, 8, 16])

        def bb(out, a, b_, op):
            DV(lambda e: e.tensor_tensor(out=out, in0=a, in1=b_, op=op), r=["s5w"] + PALL, w=PALL)

        bb(s5tmp[:, 0], s5_b[:, 0], qr_bc, ALU.mult)
        bb(s5tmp[:, 1], s5_b[:, 1], qi_bc, ALU.mult)
        bb(s5tmp[:, 2], s5_b[:, 1], qr_bc, ALU.mult)
        bb(s5tmp[:, 3], s5_b[:, 0], qi_bc, ALU.mult)
        bb(s5bb[:, 0], s5tmp[:, 0], s5tmp[:, 1], ALU.subtract)
        bb(s5bb[:, 1], s5tmp[:, 2], s5tmp[:, 3], ALU.add)
        for j in range(8):
            mk = maskB[:, j % 4, :].unsqueeze(2).broadcast_to([128, 8, 16])
            for ri in range(2):
                src = s5bb[:, ri, j, :].unsqueeze(1).broadcast_to([128, 8, 16])
                DV(lambda e, src=src, mk=mk: e.tensor_tensor(out=s5ex, in0=src, in1=mk, op=ALU.mult), r=PALL + ["maskB"], w=PALL)
                PE(lambda e: e.transpose(X[0][:, 0:128], s5ex.rearrange("p g h -> p (g h)"), ident_f[:]), r=PALL + ["ident_f"], w=["X0"])
                AC(lambda e, j=j, ri=ri: e.activation(out=WBT[:, 2 * j + ri, :], in_=X[0][:, 0:128], func=AF.Copy), x=["X0"], w=["WBT"])
                csrc = s5_c[:, ri, j, :].unsqueeze(1).broadcast_to([128, 8, 16])
                sgn = 1.0 if ri == 0 else -1.0
                DV(lambda e, csrc=csrc, mk=mk, j=j, ri=ri, sgn=sgn: e.scalar_tensor_tensor(
                    out=CT[:, 2 * j + ri, :].rearrange("p (g h) -> p g h", g=8), in0=csrc, scalar=sgn, in1=mk,
                    op0=ALU.mult, op1=ALU.mult), r=PALL + ["maskB"], w=["CT"])
        DV(lambda e: e.tensor_copy(out=Etab[:, 0, :, 0], in_=SL(I_C)), r=["s5w"], w=["Etab"])
        DV(lambda e: e.tensor_scalar(out=Etab[:, 1, :, 0], in0=SL(I_S), scalar1=-1.0, scalar2=None, op0=ALU.mult), r=["s5w"], w=["Etab"])
        n = 1
        while n < SB:
            ar = Etab[:, 0, :, 0:n]
            ai = Etab[:, 1, :, 0:n]
            br = Etab[:, 0, :, n - 1:n].broadcast_to([128, 8, n])
            bi = Etab[:, 1, :, n - 1:n].broadcast_to([128, 8, n])
            t_ = [Etmp[:, i, :, 0:n] for i in range(4)]
            for (o_, a_, b_) in ((t_[0], ar, br), (t_[1], ai, bi), (t_[2], ar, bi), (t_[3], ai, br)):
                DV(lambda e, o_=o_, a_=a_, b_=b_: e.tensor_tensor(out=o_, in0=a_, in1=b_, op=ALU.mult), r=["Etab", "aT"], w=["aT"])
            DV(lambda e, n=n, t_=t_: e.tensor_tensor(out=Etab[:, 0, :, n:2 * n], in0=t_[0], in1=t_[1], op=ALU.subtract), r=["aT"], w=["Etab"])
            DV(lambda e, n=n, t_=t_: e.tensor_tensor(out=Etab[:, 1, :, n:2 * n], in0=t_[2], in1=t_[3], op=ALU.add), r=["aT"], w=["Etab"])
            n *= 2

        class WStream:
            def __init__(self, bufs, names, plan):
                self.bufs, self.names, self.plan = bufs, names, plan
                self.issued, self.n, self.i = 0, len(bufs), 0

            def take(self, span=1):
                i = self.i
                while self.issued < min(len(self.plan), i + self.n):
                    b = self.issued % self.n
                    for (view_fn, src) in self.plan[self.issued]:
                        P.dma("pool", lambda e, d=view_fn(self.bufs[b]), s=src: e.dma_start(out=d, in_=s), writes=[self.names[b]])
                    self.issued += 1
                self.i += span
                return [(self.bufs[(i + s) % self.n], self.names[(i + s) % self.n]) for s in range(span)]

        v8 = lambda buf: buf[:].rearrange("p (k c) -> p k c", k=8)
        v16 = lambda buf: buf[:].rearrange("p (k c) -> p k c", k=16)

        def blk256(w_d, c0):
            return [(lambda buf: v8(buf), w_d[:, c0:c0 + 256].rearrange("(k p) c -> p k c", p=128))]

        planA = []
        for st_i in range(-1, NST):
            m_ = st_i < 0
            planA.append(blk256(w_in_d, 0))
            for hp in range(3):
                planA.append(blk256(w_in_d, 256 + 1536 + 256 * hp))
                if not m_:
                    planA.append(blk256(w_in_d, 256 + 2304 + 256 * hp))
                for hl in range(2):
                    h = 2 * hp + hl
                    planA.append([
                        (lambda buf: v8(buf)[:, :, 0:128], w_in_d[:, 256 + 128 * h:256 + 128 * h + 128].rearrange("(k p) c -> p k c", p=128)),
                        (lambda buf: v8(buf)[:, :, 128:256], w_in_d[:, 1024 + 128 * h:1024 + 128 * h + 128].rearrange("(k p) c -> p k c", p=128)),
                    ])
        planB = []
        for st_i in range(NST):
            for cb in range(4):
                planB.append(blk256(w_out_d, 256 * cb))
            for ub in range(16):
                planB.append(blk256(w_up_d, 256 * ub))
            for dm in range(8):
                for half in range(2):
                    planB.append([(lambda buf: v16(buf),
                                   w_down_d[2048 * half:2048 * half + 2048, 128 * dm:128 * dm + 128].rearrange("(k p) c -> p k c", p=128))])
        WA = WStream(wA, [f"wA{i}" for i in range(3)], planA)
        WB = WStream(wB, [f"wB{i}" for i in range(4)], planB)

        bA_i = [0]

        def next_bA():
            i = bA_i[0] % 2
            bA_i[0] += 1
            return i

        X_i = [0]

        def next_X():
            i = X_i[0] % 3
            X_i[0] += 1
            return i

        def mm8(bank, bname, wv, wn, c0, actT, aname, wt, resid=None, rname=None):
            def fn(e):
                ins = None
                for k in range(8):
                    ins = e.matmul(bank[:, 0:wt], lhsT=wv[:, k, c0:c0 + 128], rhs=actT[:, k, 0:wt],
                                   start=(k == 0), stop=(k == 7 and resid is None))
                if resid is not None:
                    ins = e.matmul(bank[:, 0:wt], lhsT=aI[:], rhs=resid[:, 0:wt], start=False, stop=True)
                return ins
            PE(fn, r=[wn, aname] + (["aI", rname] if resid is not None else []), w=[bname])

        def ln_tile(src_halves, src_reads, src_excl, slot, out_kind, gb=None, gbn=None, destT=None, dname=None, t=0,
                    out_rows=None, tb=None, tbn=None, xbh=None):
            st = stt[:, slot, :]
            m = mv[:, slot, :]
            for hh in range(2):
                DV(lambda e, hh=hh: e.bn_stats(out=st[:, 6 * hh:6 * hh + 6], in_=src_halves[hh]),
                   r=src_reads, w=[f"stt{slot}"], x=[src_excl[hh]] if src_excl else [])
            DV(lambda e: e.bn_aggr(out=m[:, 0:2], in_=st), r=[f"stt{slot}"], w=[f"mv{slot}"])
            AC(lambda e: e.activation(out=m[:, 2:3], in_=m[:, 1:2], func=AF.Sqrt, bias=epsb[:], scale=1.0), r=[f"mv{slot}", "epsb"], w=[f"mv{slot}"])
            DV(lambda e: e.reciprocal(out=m[:, 3:4], in_=m[:, 2:3]), r=[f"mv{slot}"], w=[f"mv{slot}"])
            DV(lambda e: e.scalar_tensor_tensor(out=m[:, 4:5], in0=m[:, 0:1], scalar=-1.0, in1=m[:, 3:4], op0=ALU.mult, op1=ALU.mult), r=[f"mv{slot}"], w=[f"mv{slot}"])
            if out_kind == "T":
                if xbh is None:
                    xbh = [(xnb[slot % 2][:, 0:512], f"xnb{slot % 2}"), (xnb[slot % 2][:, 512:1024], f"xnb{slot % 2}")]
                for hh in range(2):
                    AC(lambda e, hh=hh: e.activation(out=xbh[hh][0], in_=src_halves[hh], func=AF.Identity,
                                                     scale=m[:, 3:4], bias=m[:, 4:5]),
                       r=src_reads + [f"mv{slot}"], w=[xbh[hh][1]], x=[src_excl[hh]] if src_excl else [])
                for _ in range(LAG):
                    yield

                def trf(e):
                    ins = None
                    for k in range(8):
                        ins = e.transpose(tb[:, 128 * k:128 * k + 128], xbh[k // 4][0][:, 128 * (k % 4):128 * (k % 4) + 128], ident_b[:])
                    return ins
                PE(trf, r=[xbh[0][1], xbh[1][1], "ident_b"], w=[tbn])
                g_bc = gb[:, 0:8].unsqueeze(2).broadcast_to([128, 8, 128])
                b_bc = gb[:, 8:16].unsqueeze(2).broadcast_to([128, 8, 128])
                DV(lambda e: e.tensor_tensor(out=tmpT[:], in0=tb.rearrange("p (k c) -> p k c", k=8), in1=g_bc, op=ALU.mult),
                   r=[gbn], w=["tmpT"], x=[tbn])
                DV(lambda e: e.tensor_tensor(out=destT[:, :, 128 * t:128 * t + 128], in0=tmpT[:], in1=b_bc, op=ALU.add),
                   r=["tmpT", gbn], w=[dname])
            else:
                xo = xs[slot % 2]
                xon = f"xs{slot % 2}"
                for hh in range(2):
                    AC(lambda e, hh=hh: e.activation(out=xo[:, 512 * hh:512 * hh + 512], in_=src_halves[hh], func=AF.Identity,
                                                     scale=m[:, 3:4], bias=m[:, 4:5]),
                       r=src_reads + [f"mv{slot}"], w=[xon], x=[src_excl[hh]] if src_excl else [])
                DV(lambda e: e.tensor_tensor(out=xo[:], in0=xo[:], in1=ln2g[:], op=ALU.mult), r=[xon, "ln2g"], w=[xon])
                DV(lambda e: e.tensor_tensor(out=xo[:], in0=xo[:], in1=ln2b[:], op=ALU.add), r=[xon, "ln2b"], w=[xon])
                out_dma_ops.append(P.dma("sp", lambda e: e.dma_start(out=out_d[out_rows[0]:out_rows[1], :], in_=xo[:]), reads=[xon]))

        bBh = [(bB0h, "bB0"), (bB1h, "bB1")]
        bAh = [(bA[0][:].bitcast(BF16), "bA0"), (bA[1][:].bitcast(BF16), "bA1")]

        def rT_to_tokmajor(t):
            tk, tkn = bBh[t % 2]

            def fn(e):
                ins = None
                for dm in range(8):
                    ins = e.transpose(tk[:, 128 * dm:128 * dm + 128], rT[:, dm, 128 * t:128 * t + 128], ident_b[:])
                return ins
            PE(fn, r=["rT", "ident_b"], w=[tkn])
            return tk, tkn

        out_dma_ops = []
        bT_all = bT[:]

        def s5_block(U, c0, meta, xi_):
            Er = Etab[:, 0, 4 * U:4 * U + 4, :].rearrange("p j t -> p (j t)")
            Ei = Etab[:, 1, 4 * U:4 * U + 4, :].rearrange("p j t -> p (j t)")
            Xre, Xim, XY = X[0], X[1], X[2]

            def mmbu(e):
                ins = None
                for j4 in range(4):
                    j = 4 * U + j4
                    e.matmul(Xre[:, 128 * j4:128 * j4 + 128], lhsT=WBT[:, 2 * j, :], rhs=uT[:, U, c0:c0 + 128], start=True, stop=True)
                    ins = e.matmul(Xim[:, 128 * j4:128 * j4 + 128], lhsT=WBT[:, 2 * j + 1, :], rhs=uT[:, U, c0:c0 + 128], start=True, stop=True)
                return ins
            PE(mmbu, r=["WBT", "uT"], w=["X0", "X1"])
            for (o_, a_, b_, xb_) in ((0, Xre, Er, "X0"), (1, Xim, Ei, "X1"), (2, Xim, Er, "X1"), (3, Xre, Ei, "X0")):
                DV(lambda e, o_=o_, a_=a_, b_=b_: e.tensor_tensor(out=s5m[:, o_, :], in0=a_[:], in1=b_, op=ALU.mult),
                   r=["Etab"], w=[f"s5m{o_}"], x=[xb_])
            DV(lambda e: e.tensor_tensor(out=s5m[:, 0, :], in0=s5m[:, 0, :], in1=s5m[:, 1, :], op=ALU.subtract), r=["s5m0", "s5m1"], w=["s5m0"])
            DV(lambda e: e.tensor_tensor(out=s5m[:, 2, :], in0=s5m[:, 2, :], in1=s5m[:, 3, :], op=ALU.add), r=["s5m2", "s5m3"], w=["s5m2"])
            yield
            for j4 in range(4):
                j = 4 * U + j4
                bs = slice(128 * j4, 128 * j4 + 128)
                for (ri, si, so) in ((0, 0, 1), (1, 2, 3)):
                    DV(lambda e, ri=ri, si=si, so=so, j=j, bs=bs: e.tensor_tensor_scan(
                        out=s5m[:, so, bs], data0=s5w[:, I_RHO, j:j + 1].broadcast_to([128, 128]), data1=s5m[:, si, bs],
                        initial=Xc[:, ri, j:j + 1], op0=ALU.mult, op1=ALU.add),
                       r=["s5w", f"s5m{si}", "Xc"], w=[f"s5m{so}"])
            for (o_, zi_, e_) in ((0, 1, Er), (1, 3, Ei), (2, 3, Er), (3, 1, Ei)):
                DV(lambda e, o_=o_, zi_=zi_, e_=e_: e.tensor_tensor(out=s5p[:, o_, :], in0=s5m[:, zi_, :], in1=e_, op=ALU.mult),
                   r=["Etab", f"s5m{zi_}"], w=[f"s5p{o_}"])
            pv = lambda o_: s5p[:, o_, :].rearrange("p (j t) -> p j t", j=4)[:, :, 127]
            DV(lambda e: e.tensor_tensor(out=Xc[:, 0, 4 * U:4 * U + 4], in0=pv(0), in1=pv(1), op=ALU.add), r=["s5p0", "s5p1"], w=["Xc"])
            DV(lambda e: e.tensor_tensor(out=Xc[:, 1, 4 * U:4 * U + 4], in0=pv(2), in1=pv(3), op=ALU.subtract), r=["s5p2", "s5p3"], w=["Xc"])
            if meta:
                return
            xx = s5x[xi_]
            xn = f"s5x{xi_}"
            DV(lambda e: e.tensor_tensor(out=xx[:, 0, :], in0=s5p[:, 0, :], in1=s5p[:, 1, :], op=ALU.add), r=["s5p0", "s5p1"], w=[xn])
            DV(lambda e: e.tensor_tensor(out=xx[:, 1, :], in0=s5p[:, 2, :], in1=s5p[:, 3, :], op=ALU.subtract), r=["s5p2", "s5p3"], w=[xn])
            for _ in range(LAG):
                yield

            def mmy(e):
                ins = None
                for j4 in range(4):
                    j = 4 * U + j4
                    e.matmul(XY[:, 0:128], lhsT=CT[:, 2 * j, :], rhs=xx[:, 0, 128 * j4:128 * j4 + 128], start=(j4 == 0), stop=False)
                    ins = e.matmul(XY[:, 0:128], lhsT=CT[:, 2 * j + 1, :], rhs=xx[:, 1, 128 * j4:128 * j4 + 128], start=False, stop=(j4 == 3))
                return ins
            PE(mmy, r=["CT", xn], w=["X2"])
            DV(lambda e: e.scalar_tensor_tensor(out=ypre[:, c0:c0 + 128], in0=uT[:, U, c0:c0 + 128], scalar=s5_fm[:, U:U + 1],
                                                in1=XY[:, 0:128], op0=ALU.mult, op1=ALU.add),
               r=["uT", "s5_fm"], w=["ypre"], x=["X2"])
            yield

        def head(hp, hl, meta, nt, wt, hT, hTn):
            h = 2 * hp + hl
            (wb_, wn_), = WA.take()
            wv_ = v8(wb_)
            todo = [("k", 128, k_pre, kT)] + ([] if meta else [("q", 0, q_pre, qT)])
            for (nm, coff, pre, dst) in todo:
                xi = next_X()
                mm8(X[xi], f"X{xi}", wv_, wn_, coff, hT, hTn, wt)
                AC(lambda e, pre=pre, xi=xi: e.activation(out=pre[:, 0:wt], in_=X[xi][:, 0:wt], func=AF.Copy), x=[f"X{xi}"], w=[nm + "_pre"])
                xj = next_X()
                PE(lambda e, pre=pre, xj=xj: e.matmul(X[xj][:, 0:wt], lhsT=pswap[:], rhs=pre[:, 0:wt], start=True, stop=True),
                   r=[nm + "_pre", "pswap"], w=[f"X{xj}"])
                DV(lambda e, pre=pre: e.tensor_tensor(out=rope_t1[:, 0:wt], in0=pre[:, 0:wt], in1=cos_sb[:, 0:wt], op=ALU.mult),
                   r=[nm + "_pre", "cos_sb"], w=["s5p0"])
                DV(lambda e, xj=xj: e.tensor_tensor(out=rope_t2[:, 0:wt], in0=X[xj][:, 0:wt], in1=sin_sb[:, 0:wt], op=ALU.mult),
                   r=["sin_sb"], w=["s5p1"], x=[f"X{xj}"])
                DV(lambda e, dst=dst: e.tensor_tensor(out=dst[:, 0:wt], in0=rope_t1[:, 0:wt], in1=rope_t2[:, 0:wt], op=ALU.add),
                   r=["s5p0", "s5p1"], w=[nm + "T"])
                yield
            if not meta:
                DV(lambda e: e.tensor_tensor(out=qxT[:, 0:wt].rearrange("p (t i) -> p t i", t=nt), in0=qT[:, 0:wt].rearrange("p (t i) -> p t i", t=nt),
                                             in1=xibc[:, h, :].unsqueeze(1).broadcast_to([128, nt, 128]), op=ALU.mult),
                   r=["qT", "xibc"], w=["qxT"])

                def p1(e):
                    ins = None
                    for t in range(nt):
                        ins = e.matmul(X[0][:, 128 * t:128 * t + 128], lhsT=kT[:, 128 * t:128 * t + 128], rhs=qT[:, 128 * t:128 * t + 128], start=True, stop=True)
                    return ins
                PE(p1, r=["kT", "qT"], w=["X0"])
                DV(lambda e: e.tensor_tensor(out=scT_sb[:, 0:nt, :], in0=X[0][:, 0:wt].rearrange("p (t i) -> p t i", t=nt),
                                             in1=dmatT[:, h, :].unsqueeze(1).broadcast_to([128, nt, 128]), op=ALU.mult),
                   r=["dmatT"], w=["scT_sb"], x=["X0"])

            def p1b(e):
                ins = None
                for t in range(nt):
                    ins = e.transpose(bT_all[:, 128 * t:128 * t + 128], kT[:, 128 * t:128 * t + 128], ident_b[:])
                return ins
            PE(p1b, r=["kT", "ident_b"], w=["bT"])
            AC(lambda e: e.activation(out=kz_sb[:, 0:nt, :].rearrange("p t d -> p (t d)"), in_=bT_all[:, 0:wt], func=AF.Identity, scale=zeta[:, h:h + 1]),
               r=["zeta"], w=["kz_sb"], x=["bT"])
            for _ in range(LAG):
                yield

            def p2(e):
                ins = None
                for t in range(nt):
                    ins = e.matmul(X[1][:, 128 * t:128 * t + 128], lhsT=kz_sb[:, t, :], rhs=v_sb[:, t, 128 * h:128 * h + 128], start=True, stop=True)
                return ins
            PE(p2, r=["kz_sb", "v_sb"], w=["X1"])
            if not meta:
                AC(lambda e: e.activation(out=Sbf_all[:, 0, :], in_=S32[:, h, :], func=AF.Copy), r=[f"S32{h}"], w=["Sbf_all"])
            prev, prevn = S32[:, h, :], f"S32{h}"
            for t in range(nt):
                last = (t == nt - 1)
                dst_, dstn = (S32[:, h, :], f"S32{h}") if last else (Sall[:, t, :], "Sall")
                DV(lambda e, prev=prev, dst_=dst_, t=t: e.scalar_tensor_tensor(out=dst_, in0=prev, scalar=CONST["gammaC"][h],
                                                                                in1=X[1][:, 128 * t:128 * t + 128], op0=ALU.mult, op1=ALU.add),
                   r=[prevn], w=[dstn], x=["X1"])
                prev, prevn = dst_, dstn
            if meta:
                return
            AC(lambda e: e.activation(out=Sbf_all[:, 1:nt, :], in_=Sall[:, 0:nt - 1, :], func=AF.Copy), r=["Sall"], w=["Sbf_all"])
            for _ in range(LAG):
                yield

            def p3(e):
                ins = None
                for t in range(nt):
                    e.matmul(X[2][:, 128 * t:128 * t + 128], lhsT=scT_sb[:, t, :], rhs=v_sb[:, t, 128 * h:128 * h + 128], start=True, stop=False)
                    ins = e.matmul(X[2][:, 128 * t:128 * t + 128], lhsT=qxT[:, 128 * t:128 * t + 128], rhs=Sbf_all[:, t, :], start=False, stop=True)
                return ins
            PE(p3, r=["scT_sb", "v_sb", "qxT", "Sbf_all"], w=["X2"])
            for t in range(nt):
                DV(lambda e, t=t: e.bn_stats(out=gst[:, t, :], in_=X[2][:, 128 * t:128 * t + 128]), w=["gst"], x=["X2"])
            for t in range(nt):
                DV(lambda e, t=t: e.bn_aggr(out=gmv[:, t, :], in_=gst[:, t, :]), r=["gst"], w=["gmv"])
            AC(lambda e: e.activation(out=gsc[:, 0, 0:nt], in_=gmv[:, 0:nt, 1], func=AF.Sqrt, bias=epsb[:], scale=1.0), r=["gmv", "epsb"], w=["gsc"])
            DV(lambda e: e.reciprocal(out=gsc[:, 1, 0:nt], in_=gsc[:, 0, 0:nt]), r=["gsc"], w=["gsc"])
            DV(lambda e: e.scalar_tensor_tensor(out=gsc[:, 2, 0:nt], in0=gmv[:, 0:nt, 0], scalar=-1.0, in1=gsc[:, 1, 0:nt], op0=ALU.mult, op1=ALU.mult),
               r=["gmv", "gsc"], w=["gsc"])
            for t in range(nt):
                AC(lambda e, t=t: e.activation(out=on_sb[:, t, :], in_=X[2][:, 128 * t:128 * t + 128], func=AF.Identity,
                                               scale=gsc[:, 1, t:t + 1], bias=gsc[:, 2, t:t + 1]),
                   r=["gsc"], w=["on_sb"], x=["X2"])
            for _ in range(LAG):
                yield

            def p4(e):
                ins = None
                for t in range(nt):
                    ins = e.transpose(bT_all[:, 512 + 128 * t:512 + 128 * t + 128], on_sb[:, t, :], ident_b[:])
                return ins
            PE(p4, r=["on_sb", "ident_b"], w=["bT"])
            AC(lambda e: e.activation(out=gaff[:, 0:wt], in_=bT_all[:, 512:512 + wt], func=AF.Identity, scale=gn_gb[:, h:h + 1], bias=gn_gb[:, 6 + h:7 + h]),
               r=["gn_gb"], w=["gaff"], x=["bT"])
            DV(lambda e: e.tensor_tensor(out=yT[:, 2 + h, 0:wt], in0=gaff[:, 0:wt], in1=sgT[:, hl, 0:wt], op=ALU.mult), r=["gaff", "sgT"], w=["yT"])
            yield

        def gen_Apre(st_idx):
            meta = st_idx < 0
            nt = 1 if meta else NT
            hT, hTn = hTs[(st_idx + 1) % 2], f"hT{(st_idx + 1) % 2}"
            for t in range(nt):
                xb_ = xs[t % 2]
                xn_ = f"xs{t % 2}"
                if meta:
                    DV(lambda e, xb_=xb_: e.memset(xb_[:], 0.0), w=[xn_])
                    P.dma("sp", lambda e, xb_=xb_: e.dma_start(out=xb_[112:128, :], in_=meta_d), writes=[xn_])
                else:
                    r0 = st_idx * W + t * 128
                    P.dma("sp", lambda e, r0=r0, xb_=xb_: e.dma_start(out=xb_[:], in_=x_d[r0:r0 + 128, :]), writes=[xn_])
                yield from ln_tile([xb_[:, 0:512], xb_[:, 512:1024]], [xn_], None, t, "T", gb=lnin_gb, gbn="lnin_gb", destT=hT, dname=hTn, t=t,
                                   tb=bT_all, tbn="bT")
                yield
            if meta:
                DV(lambda e: e.memset(hT[:, :, 0:112], 0.0), r=[hTn], w=[hTn])

        def gen_Amain(st_idx):
            meta = st_idx < 0
            nt = 1 if meta else NT
            wt = nt * 128
            slot0 = 0 if meta else 128 + st_idx * W
            hT, hTn = hTs[(st_idx + 1) % 2], f"hT{(st_idx + 1) % 2}"
            P.dma("sp", lambda e: e.dma_start(out=cos_sb[:, 0:wt], in_=cosT_d[:, slot0:slot0 + wt]), writes=["cos_sb"])
            P.dma("sp", lambda e: e.dma_start(out=sin_sb[:, 0:wt], in_=sinT_d[:, slot0:slot0 + wt]), writes=["sin_sb"])
            (wb_, wn_), = WA.take()
            for U in range(2):
                xi = next_X()
                mm8(X[xi], f"X{xi}", v8(wb_), wn_, 128 * U, hT, hTn, wt)
                AC(lambda e, U=U, xi=xi: e.activation(out=uT[:, U, 0:wt], in_=X[xi][:, 0:wt], func=AF.Copy), x=[f"X{xi}"], w=["uT"])
            yield
            def gen_s5():
                cnt = 0
                for U in range(2):
                    for sbi in range(wt // SB):
                        yield from s5_block(U, sbi * SB, meta, cnt % 2)
                        cnt += 1
                    if not meta:
                        AC(lambda e, U=U: e.activation(out=ygb[:, U, 0:wt], in_=ypre[:, 0:wt], func=AF.Gelu_apprx_tanh), r=["ypre"], w=["ygb"])
                    yield
                if not meta:
                    for U2 in range(2):
                        xi = next_X()

                        def mmg(e, U2=U2, xi=xi):
                            e.matmul(X[xi][:, 0:wt], lhsT=wglu[:, 0, 128 * U2:128 * U2 + 128], rhs=ygb[:, 0, 0:wt], start=True, stop=False)
                            return e.matmul(X[xi][:, 0:wt], lhsT=wglu[:, 1, 128 * U2:128 * U2 + 128], rhs=ygb[:, 1, 0:wt], start=False, stop=True)
                        PE(mmg, r=["wglu", "ygb"], w=[f"X{xi}"])
                        AC(lambda e, U2=U2, xi=xi: e.activation(out=sgl[:, 0:wt], in_=X[xi][:, 0:wt], func=AF.Sigmoid, bias=s5_fm[:, 2 + U2:3 + U2], scale=1.0),
                           r=["s5_fm"], w=["sgl"], x=[f"X{xi}"])
                        DV(lambda e, U2=U2: e.tensor_tensor(out=yT[:, U2, 0:wt], in0=ygb[:, U2, 0:wt], in1=sgl[:, 0:wt], op=ALU.mult), r=["ygb", "sgl"], w=["yT"])
                    yield

            def gen_ret():
                for hp in range(3):
                    (wb_, wn_), = WA.take()
                    for tp in range((nt + 1) // 2):
                        xi = next_X()
                        tl = [t for t in (2 * tp, 2 * tp + 1) if t < nt]

                        def mmv(e, tl=tl, xi=xi, wb_=wb_):
                            ins = None
                            for ii, t in enumerate(tl):
                                for k in range(8):
                                    ins = e.matmul(X[xi][:, 256 * ii:256 * ii + 256], lhsT=hT[:, k, 128 * t:128 * t + 128], rhs=v8(wb_)[:, k, 0:256],
                                                   start=(k == 0), stop=(k == 7))
                            return ins
                        PE(mmv, r=[wn_, hTn], w=[f"X{xi}"])
                        AC(lambda e, tl=tl, xi=xi, hp=hp: e.activation(
                            out=v_sb[:, tl[0]:tl[0] + len(tl), 256 * hp:256 * hp + 256],
                            in_=X[xi][:, 0:256 * len(tl)].rearrange("p (t c) -> p t c", t=len(tl)), func=AF.Copy),
                           x=[f"X{xi}"], w=["v_sb"])
                    yield
                    if not meta:
                        (wb_, wn_), = WA.take()
                        for hl in range(2):
                            xi = next_X()
                            mm8(X[xi], f"X{xi}", v8(wb_), wn_, 128 * hl, hT, hTn, wt)
                            AC(lambda e, hl=hl, xi=xi: e.activation(out=sgT[:, hl, 0:wt], in_=X[xi][:, 0:wt], func=AF.Silu), x=[f"X{xi}"], w=["sgT"])
                        yield
                    for hl in range(2):
                        yield from head(hp, hl, meta, nt, wt, hT, hTn)

            g1, g2 = gen_ret(), gen_s5()
            live = [g1, g2]
            while live:
                for g in list(live):
                    try:
                        next(g)
                        yield
                    except StopIteration:
                        live.remove(g)

        def gen_E(st_idx):
            hT, hTn = hTs[(st_idx + 1) % 2], f"hT{(st_idx + 1) % 2}"
            for cb in range(4):
                (wb_, wn_), = WB.take()
                for hl in range(2):
                    dm = 2 * cb + hl
                    bi_ = next_bA()
                    mm8(bA[bi_], f"bA{bi_}", v8(wb_), wn_, 128 * hl, yT, "yT", W, resid=hT[:, dm, :], rname=hTn)
                    AC(lambda e, dm=dm, bi_=bi_: e.activation(out=rT[:, dm, :], in_=bA[bi_][:, 0:W], func=AF.Copy), x=[f"bA{bi_}"], w=["rT"])
                yield

        def gen_B(st_idx):
            for t in range(NT):
                tk, tkn = rT_to_tokmajor(t)
                yield from ln_tile([tk[:, 0:512], tk[:, 512:1024]], [], [tkn, tkn], 4 + t, "T", gb=ln1_gb, gbn="ln1_gb", destT=h1T, dname="h1T", t=t,
                                   tb=bAh[t % 2][0], tbn=bAh[t % 2][1], xbh=[(relu_t[0][:], "relu_t0"), (relu_t[1][:], "relu_t1")])
                yield
            for ub in range(16):
                (wb_, wn_), = WB.take()
                for hl in range(2):
                    ff = 2 * ub + hl
                    bi_ = next_bA()
                    mm8(bA[bi_], f"bA{bi_}", v8(wb_), wn_, 128 * hl, h1T, "h1T", W)
                    rt = relu_t[ff % 2]
                    AC(lambda e, rt=rt, bi_=bi_: e.activation(out=rt[:], in_=bA[bi_][:, 0:W], func=AF.Relu), x=[f"bA{bi_}"], w=[f"relu_t{ff % 2}"])
                    AC(lambda e, rt=rt, ff=ff: e.activation(out=aT[:, ff, :], in_=rt[:], func=AF.Square), r=[f"relu_t{ff % 2}"], w=["aT"])
                    yield
            for dm in range(8):
                (w0, n0), (w1, n1) = WB.take(2)
                bi_ = next_bA()

                for q4 in range(4):
                    def fnd(e, w0=w0, w1=w1, dm=dm, bi_=bi_, q4=q4):
                        ins = None
                        for k in range(8 * q4, 8 * q4 + 8):
                            wv_ = v16(w0) if k < 16 else v16(w1)
                            ins = e.matmul(bA[bi_][:, 0:W], lhsT=wv_[:, k % 16, :], rhs=aT[:, k, :], start=(k == 0), stop=False)
                        if q4 == 3:
                            ins = e.matmul(bA[bi_][:, 0:W], lhsT=aI[:], rhs=h1T[:, dm, :], start=False, stop=True)
                        return ins
                    PE(fnd, r=[n0, n1, "aT", "h1T", "aI"], w=[f"bA{bi_}"])
                    if q4 < 3:
                        yield
                AC(lambda e, dm=dm, bi_=bi_: e.activation(out=rT[:, dm, :], in_=bA[bi_][:, 0:W], func=AF.Copy), x=[f"bA{bi_}"], w=["rT"])
                yield
            for t in range(NT):
                tk, tkn = rT_to_tokmajor(t)
                r0 = st_idx * W + t * 128
                yield from ln_tile([tk[:, 0:512], tk[:, 512:1024]], [], [tkn, tkn], 4 + t, "O", out_rows=(r0, r0 + 128))
                yield

        def run(g):
            for _ in g:
                pass

        def interleave(gb, ga, a_per_b):
            done_a = done_b = False
            acc = 0.0
            na = nb = 0
            while not (done_a and done_b):
                if not done_b:
                    try:
                        next(gb)
                        nb += 1
                    except StopIteration:
                        done_b = True
                acc += a_per_b
                while (acc >= 1.0 or done_b) and not done_a:
                    acc -= 1.0
                    try:
                        next(ga)
                        na += 1
                    except StopIteration:
                        done_a = True
                if done_a:
                    acc = 0.0
            return na, nb

        def chain(*gs):
            for g in gs:
                if g is not None:
                    yield from g

        def spread(gmain, gextra, every):
            n = 0
            extra_live = gextra is not None
            for _ in gmain:
                yield
                n += 1
                if extra_live and n % every == 0:
                    try:
                        next(gextra)
                        yield
                    except StopIteration:
                        extra_live = False
            if extra_live:
                for _ in gextra:
                    yield

        ratio = [A_PER_B]
        run(gen_Apre(-1))
        run(spread(gen_Amain(-1), gen_Apre(0), 3))
        run(spread(gen_Amain(0), gen_Apre(1) if NST > 1 else None, PRE_EVERY))
        for s in range(NST):
            run(gen_E(s))
            if s + 1 < NST:
                a_stream = spread(gen_Amain(s + 1), gen_Apre(s + 2) if s + 2 < NST else None, PRE_EVERY)
                if INTERLEAVE:
                    na_, nb_ = interleave(gen_B(s), a_stream, ratio[0])
                    ratio[0] = na_ / max(nb_, 1)
                else:
                    run(gen_B(s))
                    run(a_stream)
            else:
                run(gen_B(s))
        P.emit(final_wait_ops=out_dma_ops)
    return nc


_NC_CACHE = {}


def _prep_inputs(inp):
    f = lambda a: np.ascontiguousarray(np.asarray(a, dtype=np.float32))
    pk = lambda v: np.ascontiguousarray(f(v).reshape(-1, 128).T)
    shared = {}
    shared["meta"] = f(inp["meta_tokens"])
    shared["w_in"] = f(inp["w_in"][0])
    shared["w_out"] = f(inp["w_out"][0])
    shared["w_up"] = f(inp["w_up"][0])
    shared["w_down"] = f(inp["w_down"][0])
    shared["w_glu"] = f(inp["s5_w_glu"][0])
    shared["lnin_gb"] = np.ascontiguousarray(np.concatenate([pk(inp["ln_in_g"]), pk(inp["ln_in_b"])], axis=1))
    shared["ln1_gb"] = np.ascontiguousarray(np.concatenate([pk(inp["ln1_g"][0]), pk(inp["ln1_b"][0])], axis=1))
    shared["ln2_g"] = f(inp["ln2_g"][0]).reshape(1, D)
    shared["ln2_b"] = f(inp["ln2_b"][0]).reshape(1, D)
    st = lambda a: np.ascontiguousarray(f(a).reshape(8, 128).T)
    ldt = np.repeat(f(inp["s5_log_dt"][0])[:, None], 64, axis=1)
    shared["s5_sc"] = np.ascontiguousarray(np.concatenate([st(inp["s5_lambda_re"][0]), st(inp["s5_lambda_im"][0]), st(ldt)], axis=1))

    def bl(a):
        return f(a).reshape(8, 2, 64, 16).transpose(1, 2, 0, 3).reshape(128, 8, 16)
    shared["s5_b"] = np.ascontiguousarray(np.stack([bl(inp["s5_b_re"][0]), bl(inp["s5_b_im"][0])], axis=1))
    cl = lambda a: bl(f(a).transpose(0, 2, 1))
    shared["s5_c"] = np.ascontiguousarray(np.stack([cl(inp["s5_c_re"][0]), cl(inp["s5_c_im"][0])], axis=1))
    shared["s5_fm"] = np.ascontiguousarray(np.concatenate([pk(inp["s5_d"][0]), pk(inp["s5_b_glu"][0])], axis=1))
    shared["gn_gb"] = np.ascontiguousarray(np.concatenate([pk(inp["ret_gn_g"][0]), pk(inp["ret_gn_b"][0])], axis=1))
    for k in ("ident_f", "pswap", "cosT", "sinT", "dmatT", "xi_bc", "zeta", "maskB"):
        shared[k] = CONST[k]
    x = f(inp["x"])
    maps = []
    for c in range(8):
        m = dict(shared)
        m["x"] = x[c]
        maps.append(m)
    return maps


def kernel(**inputs):
    if "nc" not in _NC_CACHE:
        _NC_CACHE["nc"] = build()
    nc = _NC_CACHE["nc"]
    maps = _prep_inputs(inputs)
    res = run_bass_kernel_spmd(nc, maps, core_ids=list(range(8)))
    out = np.stack([np.asarray(r["out"], dtype=np.float32) for r in res.results], axis=0)
    return out
```

```python
import math
import numpy as np
import concourse.bass as bass
import concourse.mybir as mybir
from concourse.bass_utils import run_bass_kernel_spmd
from contextlib import ExitStack

F32 = mybir.dt.float32
BF16 = mybir.dt.bfloat16
ALU = mybir.AluOpType
AF = mybir.ActivationFunctionType

D = 1024
SEQ = 4096
NMETA = 16
NH = 6
NT = 4
W = NT * 128
NST = SEQ // W
SB = 128
ALPHA = 2.0 ** 0.25
LN_EPS = 1e-5
NSLOT = 33 * 128
ENG_NAMES = ("pe", "act", "dve", "pool", "sp")
STRICT_SYNC = False
INTERLEAVE = True
A_PER_B = 2.6
LAG = 2
PRE_EVERY = 12


class Prog:
    def __init__(self, nc, n_dma_sems=12):
        self.nc = nc
        self.ops = []
        self.res_w = {}
        self.res_r = {}
        self.cnt = {e: 0 for e in ENG_NAMES}
        self.n_dma_sems = n_dma_sems
        self.dma_cnt = {}
        self.dma_rr = {"sp": 0, "pool": 0, "act": 0}
        self.dma_last = {}

    def _deps(self, reads, writes, excl):
        deps = {}
        for r in list(reads) + list(excl):
            if r in self.res_w:
                deps[self.res_w[r]] = True
        for w in list(writes):
            if w in self.res_w:
                deps.setdefault(self.res_w[w], False)
            for rd in self.res_r.get(w, ()):
                deps.setdefault(rd, False)
        for w in excl:
            for rd in self.res_r.get(w, ()):
                deps.setdefault(rd, False)
        return deps

    def _commit(self, oid, reads, writes, excl):
        for r in list(reads) + list(excl):
            self.res_r.setdefault(r, []).append(oid)
        for w in list(writes):
            self.res_w[w] = oid
            self.res_r[w] = []

    def op(self, eng, fn, reads=(), writes=(), excl=()):
        oid = len(self.ops)
        deps = self._deps(reads, writes, excl)
        self.cnt[eng] += 1
        self.ops.append(dict(eng=eng, fn=fn, deps=deps, tok=("E", eng, self.cnt[eng]), dma=False))
        self._commit(oid, reads, writes, excl)
        return oid

    def dma(self, queue, fn, reads=(), writes=()):
        oid = len(self.ops)
        deps = self._deps(reads, writes, ())
        slot = self.dma_rr[queue]
        self.dma_rr[queue] = (slot + 1) % self.n_dma_sems
        key = (queue, slot)
        if key in self.dma_last:
            deps[self.dma_last[key]] = True
        self.dma_cnt[key] = self.dma_cnt.get(key, 0) + 1
        self.dma_last[key] = oid
        self.ops.append(dict(eng=queue, fn=fn, deps=deps, tok=("D", key, 16 * self.dma_cnt[key]), dma=True))
        self._commit(oid, reads, writes, ())
        return oid

    def emit(self, final_wait_ops=()):
        nc = self.nc
        with ExitStack() as es:
            esem = {e: es.enter_context(nc.semaphore(f"s_{e}")) for e in ENG_NAMES}
            dsem = {}
            for q in ("sp", "pool", "act"):
                for s in range(self.n_dma_sems):
                    if (q, s) in self.dma_cnt:
                        dsem[(q, s)] = es.enter_context(nc.semaphore(f"d_{q}{s}"))
            block = es.enter_context(nc.Block())

            def semval(tok):
                if tok[0] == "E":
                    return esem[tok[1]], tok[2]
                return dsem[tok[1]], tok[2]

            def run_engine(ename, eobj):
                known = {}
                for oid, o in enumerate(self.ops):
                    if o["eng"] != ename:
                        continue
                    for d in sorted(o["deps"]):
                        do = self.ops[d]
                        if (not o["dma"]) and (not do["dma"]) and do["eng"] == ename:
                            if ename == "pe" or not (o["deps"][d] or STRICT_SYNC):
                                continue
                        sem, val = semval(do["tok"])
                        k = id(sem)
                        if known.get(k, 0) >= val:
                            continue
                        eobj.wait_ge(sem, val)
                        known[k] = val
                    ins = o["fn"](eobj)
                    sem, val = semval(o["tok"])
                    ins.then_inc(sem, 16 if o["dma"] else 1)
                if ename == "sp":
                    for oid in final_wait_ops:
                        sem, val = semval(self.ops[oid]["tok"])
                        eobj.wait_ge(sem, val)

            @block.tensor
            def _(e):
                run_engine("pe", e)

            @block.scalar
            def _(e):
                run_engine("act", e)

            @block.vector
            def _(e):
                run_engine("dve", e)

            @block.gpsimd
            def _(e):
                run_engine("pool", e)

            @block.sync
            def _(e):
                run_engine("sp", e)


def _host_consts():
    c = {}
    c["ident_f"] = np.eye(128, dtype=np.float32)
    pm = np.zeros((128, 128), np.float32)
    for dp in range(128):
        pm[(dp + 64) % 128, dp] = 1.0
    c["pswap"] = pm
    pos = (np.arange(NSLOT, dtype=np.float32) - 112.0).astype(np.float32)
    inv_freq = (1.0 / (10000.0 ** (np.arange(0, 128, 2, dtype=np.float32) / 128.0))).astype(np.float32)
    ang = (pos[:, None] * inv_freq[None, :]).astype(np.float32)
    cs, sn = np.cos(ang).astype(np.float32), np.sin(ang).astype(np.float32)
    c["cosT"] = np.ascontiguousarray(np.concatenate([cs, cs], axis=1).T)
    c["sinT"] = np.ascontiguousarray(np.concatenate([-sn, sn], axis=1).T)
    lg = np.log1p(-np.exp2(-5.0 - np.arange(NH, dtype=np.float32))).astype(np.float32)
    idx = np.arange(128, dtype=np.float32)
    scale = 128.0 ** -0.5
    diff = idx[None, :] - idx[:, None]
    dm = np.where(diff[None] >= 0, np.exp(np.maximum(diff, 0.0)[None] * lg[:, None, None]), 0.0) * scale
    c["dmatT"] = np.ascontiguousarray(dm.transpose(1, 0, 2)).astype(np.float32)
    xi = np.exp((idx + 1.0)[None] * lg[:, None]).astype(np.float32)
    c["xi_bc"] = np.ascontiguousarray(np.broadcast_to(xi[None], (128, NH, 128))).astype(np.float32)
    zeta = (np.exp((127.0 - idx)[None] * lg[:, None]) * scale).astype(np.float32)
    c["zeta"] = np.ascontiguousarray(zeta.T)
    c["gammaC"] = [float(np.exp(128.0 * lg[h])) for h in range(NH)]
    mb = np.zeros((128, 4, 8), np.float32)
    for gl in range(2):
        for j4 in range(4):
            mb[gl * 64:(gl + 1) * 64, j4, 2 * j4 + gl] = 1.0
    c["maskB"] = mb
    return c


CONST = _host_consts()


def build():
    nc = bass.Bass("TRN2", target_bir_lowering=False)

    def din(name, shape):
        return nc.dram_tensor(name, list(shape), F32, kind="ExternalInput").ap()

    x_d = din("x", [SEQ, D])
    meta_d = din("meta", [NMETA, D])
    w_in_d = din("w_in", [D, 3328])
    w_out_d = din("w_out", [D, D])
    w_up_d = din("w_up", [D, 4 * D])
    w_down_d = din("w_down", [4 * D, D])
    w_glu_d = din("w_glu", [256, 256])
    lnin_gb_d = din("lnin_gb", [128, 16])
    ln1_gb_d = din("ln1_gb", [128, 16])
    ln2_g_d = din("ln2_g", [1, D])
    ln2_b_d = din("ln2_b", [1, D])
    s5_sc_d = din("s5_sc", [128, 24])
    s5_b_d = din("s5_b", [128, 2, 8, 16])
    s5_c_d = din("s5_c", [128, 2, 8, 16])
    s5_fm_d = din("s5_fm", [128, 4])
    gn_gb_d = din("gn_gb", [128, 12])
    ident_d = din("ident_f", [128, 128])
    pswap_d = din("pswap", [128, 128])
    cosT_d = din("cosT", [128, NSLOT])
    sinT_d = din("sinT", [128, NSLOT])
    dmatT_d = din("dmatT", [128, NH, 128])
    xibc_d = din("xi_bc", [128, NH, 128])
    zeta_d = din("zeta", [128, NH])
    maskB_d = din("maskB", [128, 4, 8])
    out_d = nc.dram_tensor("out", [SEQ, D], F32, kind="ExternalOutput").ap()

    with ExitStack() as es:
        def sb(name, shape, dt=F32):
            return es.enter_context(nc.sbuf_tensor(name, list(shape), dt))

        def psum(name, shape, dt=F32):
            return es.enter_context(nc.psum_tensor(name, list(shape), dt))

        xs = [sb(f"xs{i}", [128, D]) for i in range(2)]
        xnb = [sb(f"xnb{i}", [128, D], BF16) for i in range(2)]
        tmpT = sb("tmpT", [128, 8, 128], BF16)
        hTs = [sb(f"hT{i}", [128, 8, W], BF16) for i in range(2)]
        h1T = sb("h1T", [128, 8, W], BF16)
        wA = [sb(f"wA{i}", [128, 2048], BF16) for i in range(3)]
        wB = [sb(f"wB{i}", [128, 2048], BF16) for i in range(4)]
        uT = sb("uT", [128, 2, W], BF16)
        q_pre = sb("q_pre", [128, W], BF16)
        k_pre = sb("k_pre", [128, W], BF16)
        qT = sb("qT", [128, W], BF16)
        kT = sb("kT", [128, W], BF16)
        qxT = sb("qxT", [128, W], BF16)
        sgT = sb("sgT", [128, 2, W], BF16)
        v_sb = sb("v_sb", [128, NT, 768], BF16)
        yT = sb("yT", [128, 8, W], BF16)
        aT = sb("aT", [128, 32, W], BF16)
        relu_t = [sb(f"relu_t{i}", [128, W], BF16) for i in range(2)]
        rT = sb("rT", [128, 8, W], BF16)
        cos_sb = sb("cos_sb", [128, W])
        sin_sb = sb("sin_sb", [128, W])
        ln2g = sb("ln2g", [128, D])
        ln2b = sb("ln2b", [128, D])
        ident_f = sb("ident_fs", [128, 128])
        ident_b = sb("ident_b", [128, 128], BF16)
        aI = sb("aI", [128, 128], BF16)
        pswap = sb("pswap_s", [128, 128], BF16)
        dmatT = sb("dmatT_s", [128, NH, 128])
        xibc = sb("xibc_s", [128, NH, 128])
        zeta = sb("zeta_s", [128, NH])
        maskB = sb("maskB_s", [128, 4, 8])
        lnin_gb = sb("lnin_gb_s", [128, 16])
        ln1_gb = sb("ln1_gb_s", [128, 16])
        gn_gb = sb("gn_gb_s", [128, 12])
        s5_fm = sb("s5_fm_s", [128, 4])
        epsb = sb("epsb", [128, 1])
        stt = sb("stt", [128, 8, 12])
        mv = sb("mv", [128, 8, 8])
        scT_sb = sb("scT_sb", [128, NT, 128], BF16)
        kz_sb = sb("kz_sb", [128, NT, 128], BF16)
        on_sb = sb("on_sb", [128, NT, 128], BF16)
        gaff = sb("gaff", [128, W], BF16)
        S32 = sb("S32", [128, NH, 128])
        Sall = sb("Sall", [128, 3, 128])
        Sbf_all = sb("Sbf_all", [128, NT, 128], BF16)
        gst = sb("gst", [128, NT, 6])
        gmv = sb("gmv", [128, NT, 2])
        gsc = sb("gsc", [128, 3, NT])
        s5_sc = sb("s5_sc_s", [128, 24])
        s5w = sb("s5w", [128, 40, 8])
        WBT = sb("WBT", [128, 16, 128], BF16)
        CT = sb("CT", [128, 16, 128], BF16)
        Etab = sb("Etab", [128, 2, 8, SB])
        s5m = sb("s5m", [128, 4, 4 * SB])
        s5p = sb("s5p", [128, 4, 4 * SB])
        s5x = [sb(f"s5x{i}", [128, 2, 4 * SB], BF16) for i in range(2)]
        Xc = sb("Xc", [128, 2, 8])
        ypre = sb("ypre", [128, W])
        ygb = sb("ygb", [128, 2, W], BF16)
        sgl = sb("sgl", [128, W], BF16)
        wglu = sb("wglu", [128, 2, 256], BF16)
        rope_t1 = s5p[:, 0, :]
        rope_t2 = s5p[:, 1, :]
        PALL = ["s5p0", "s5p1", "s5p2", "s5p3"]
        pfl = s5p[:].rearrange("p a w -> p (a w)")
        s5_b = pfl[:, 0:256].rearrange("p (r j h) -> p r j h", r=2, j=8)
        s5_c = pfl[:, 256:512].rearrange("p (r j h) -> p r j h", r=2, j=8)
        s5bb = pfl[:, 512:768].rearrange("p (r j h) -> p r j h", r=2, j=8)
        s5ex = pfl[:, 768:896].rearrange("p (g h) -> p g h", g=8)
        s5tmp = pfl[:, 1024:1536].rearrange("p (x j h) -> p x j h", x=4, j=8)

        bA = [psum(f"bA{i}", [128, 512]) for i in range(2)]
        bB = [psum(f"bB{i}", [128, 512]) for i in range(2)]
        bT = psum("bT", [128, 1024], BF16)
        X = [psum(f"X{i}", [128, 512]) for i in range(3)]
        bB0h = bB[0][:].bitcast(BF16)
        bB1h = bB[1][:].bitcast(BF16)

        P = Prog(nc)
        Etmp = aT[:, 0:16, :].bitcast(F32).rearrange("p a b -> p (a b)").rearrange("p (x j n) -> p x j n", x=4, j=8)
        DV = lambda fn, r=(), w=(), x=(): P.op("dve", fn, r, w, x)
        AC = lambda fn, r=(), w=(), x=(): P.op("act", fn, r, w, x)
        PE = lambda fn, r=(), w=(): P.op("pe", fn, r, w)

        def ld(dst, src, name, q="sp"):
            P.dma(q, lambda e: e.dma_start(out=dst, in_=src), writes=name if isinstance(name, list) else [name])

        ld(ident_f[:], ident_d, "ident_f")
        ld(dmatT[:], dmatT_d, "dmatT")
        ld(zeta[:], zeta_d, "zeta")
        ld(maskB[:], maskB_d, "maskB")
        ld(lnin_gb[:], lnin_gb_d, "lnin_gb")
        ld(ln1_gb[:], ln1_gb_d, "ln1_gb")
        ld(gn_gb[:], gn_gb_d, "gn_gb")
        ld(s5_fm[:], s5_fm_d, "s5_fm")
        ld(s5_sc[:], s5_sc_d, "s5_sc")
        ld(s5_b, s5_b_d, PALL)
        ld(s5_c, s5_c_d, PALL)
        ld(ln2g[:], ln2_g_d[0].partition_broadcast(128), "ln2g")
        ld(ln2b[:], ln2_b_d[0].partition_broadcast(128), "ln2b")
        ld(pswap[:], pswap_d, "pswap", q="pool")
        ld(xibc[:], xibc_d, "xibc")
        ld(wglu[:], w_glu_d.rearrange("(k p) c -> p k c", p=128), "wglu", q="pool")
        DV(lambda e: e.memset(epsb[:], LN_EPS), w=["epsb"])
        AC(lambda e: e.activation(out=ident_b[:], in_=ident_f[:], func=AF.Copy), r=["ident_f"], w=["ident_b"])
        AC(lambda e: e.activation(out=aI[:], in_=ident_f[:], func=AF.Identity, scale=ALPHA), r=["ident_f"], w=["aI"])
        DV(lambda e: e.memset(S32[:], 0.0), w=[f"S32{h}" for h in range(NH)])
        DV(lambda e: e.memset(Xc[:], 0.0), w=["Xc"])

        SL = lambda i: s5w[:, i, :]
        lam_re, lam_im, log_dt = s5_sc[:, 0:8], s5_sc[:, 8:16], s5_sc[:, 16:24]
        (I_DT, I_AR, I_TH, I_RHO, I_PHI, I_U, I_S, I_C, I_TS, I_TC, I_CC, I_SS, I_CS, I_N, I_RN,
         I_LBR, I_LBI, I_DEN, I_NR, I_QR, I_QI, I_T1, I_T2, I_NS) = range(24)

        def dv2(out, a, b, op):
            DV(lambda e: e.tensor_tensor(out=out, in0=a, in1=b, op=op), r=["s5w", "s5_sc"], w=["s5w"])

        AC(lambda e: e.activation(out=SL(I_DT), in_=log_dt, func=AF.Exp), r=["s5_sc"], w=["s5w"])
        dv2(SL(I_AR), lam_re, SL(I_DT), ALU.mult)
        dv2(SL(I_TH), lam_im, SL(I_DT), ALU.mult)
        AC(lambda e: e.activation(out=SL(I_RHO), in_=SL(I_AR), func=AF.Exp), r=["s5w"], w=["s5w"])
        DV(lambda e: e.tensor_scalar(out=SL(I_PHI), in0=SL(I_TH), scalar1=1.0 / 32.0, scalar2=None, op0=ALU.mult), r=["s5w"], w=["s5w"])
        dv2(SL(I_U), SL(I_PHI), SL(I_PHI), ALU.mult)
        DV(lambda e: e.tensor_copy(out=SL(I_S), in_=SL(I_PHI)), r=["s5w"], w=["s5w"])
        DV(lambda e: e.tensor_copy(out=SL(I_TS), in_=SL(I_PHI)), r=["s5w"], w=["s5w"])
        DV(lambda e: e.memset(SL(I_C), 1.0), r=["s5w"], w=["s5w"])
        DV(lambda e: e.memset(SL(I_TC), 1.0), r=["s5w"], w=["s5w"])
        for kk in range(1, 7):
            cs_ = -1.0 / ((2 * kk) * (2 * kk + 1))
            cc_ = -1.0 / ((2 * kk - 1) * (2 * kk))
            DV(lambda e, c_=cs_: e.scalar_tensor_tensor(out=SL(I_TS), in0=SL(I_TS), scalar=c_, in1=SL(I_U), op0=ALU.mult, op1=ALU.mult), r=["s5w"], w=["s5w"])
            dv2(SL(I_S), SL(I_S), SL(I_TS), ALU.add)
            DV(lambda e, c_=cc_: e.scalar_tensor_tensor(out=SL(I_TC), in0=SL(I_TC), scalar=c_, in1=SL(I_U), op0=ALU.mult, op1=ALU.mult), r=["s5w"], w=["s5w"])
            dv2(SL(I_C), SL(I_C), SL(I_TC), ALU.add)
        for _ in range(5):
            dv2(SL(I_CC), SL(I_C), SL(I_C), ALU.mult)
            dv2(SL(I_SS), SL(I_S), SL(I_S), ALU.mult)
            dv2(SL(I_CS), SL(I_C), SL(I_S), ALU.mult)
            dv2(SL(I_C), SL(I_CC), SL(I_SS), ALU.subtract)
            DV(lambda e: e.tensor_scalar(out=SL(I_S), in0=SL(I_CS), scalar1=2.0, scalar2=None, op0=ALU.mult), r=["s5w"], w=["s5w"])
        dv2(SL(I_CC), SL(I_C), SL(I_C), ALU.mult)
        dv2(SL(I_SS), SL(I_S), SL(I_S), ALU.mult)
        dv2(SL(I_N), SL(I_CC), SL(I_SS), ALU.add)
        AC(lambda e: e.activation(out=SL(I_NS), in_=SL(I_N), func=AF.Sqrt), r=["s5w"], w=["s5w"])
        DV(lambda e: e.reciprocal(out=SL(I_RN), in_=SL(I_NS)), r=["s5w"], w=["s5w"])
        dv2(SL(I_C), SL(I_C), SL(I_RN), ALU.mult)
        dv2(SL(I_S), SL(I_S), SL(I_RN), ALU.mult)
        dv2(SL(I_LBR), SL(I_RHO), SL(I_C), ALU.mult)
        dv2(SL(I_LBI), SL(I_RHO), SL(I_S), ALU.mult)
        dv2(SL(I_T1), lam_re, lam_re, ALU.mult)
        dv2(SL(I_T2), lam_im, lam_im, ALU.mult)
        dv2(SL(I_DEN), SL(I_T1), SL(I_T2), ALU.add)
        DV(lambda e: e.reciprocal(out=SL(I_DEN), in_=SL(I_DEN)), r=["s5w"], w=["s5w"])
        DV(lambda e: e.tensor_scalar(out=SL(I_NR), in0=SL(I_LBR), scalar1=-1.0, scalar2=None, op0=ALU.add), r=["s5w"], w=["s5w"])
        dv2(SL(I_T1), SL(I_NR), lam_re, ALU.mult)
        dv2(SL(I_T2), SL(I_LBI), lam_im, ALU.mult)
        dv2(SL(I_QR), SL(I_T1), SL(I_T2), ALU.add)
        dv2(SL(I_QR), SL(I_QR), SL(I_DEN), ALU.mult)
        dv2(SL(I_T1), SL(I_LBI), lam_re, ALU.mult)
        dv2(SL(I_T2), SL(I_NR), lam_im, ALU.mult)
        dv2(SL(I_QI), SL(I_T1), SL(I_T2), ALU.subtract)
        dv2(SL(I_QI), SL(I_QI), SL(I_DEN), ALU.mult)
        qr_bc = SL(I_QR).unsqueeze(2).broadcast_to([128, 8, 16])
        qi_bc = SL(I_QI).unsqueeze(2).broadcast_to([128, 8, 16])

        def bb(out, a, b_, op):
            DV(lambda e: e.tensor_tensor(out=out, in0=a, in1=b_, op=op), r=["s5w"] + PALL, w=PALL)

        bb(s5tmp[:, 0], s5_b[:, 0], qr_bc, ALU.mult)
        bb(s5tmp[:, 1], s5_b[:, 1], qi_bc, ALU.mult)
        bb(s5tmp[:, 2], s5_b[:, 1], qr_bc, ALU.mult)
        bb(s5tmp[:, 3], s5_b[:, 0], qi_bc, ALU.mult)
        bb(s5bb[:, 0], s5tmp[:, 0], s5tmp[:, 1], ALU.subtract)
        bb(s5bb[:, 1], s5tmp[:, 2], s5tmp[:, 3], ALU.add)
        for j in range(8):
            mk = maskB[:, j % 4, :].unsqueeze(2).broadcast_to([128, 8, 16])
            for ri in range(2):
                src = s5bb[:, ri, j, :].unsqueeze(1).broadcast_to([128, 8, 16])
                DV(lambda e, src=src, mk=mk: e.tensor_tensor(out=s5ex, in0=src, in1=mk, op=ALU.mult), r=PALL + ["maskB"], w=PALL)
                PE(lambda e: e.transpose(X[0][:, 0:128], s5ex.rearrange("p g h -> p (g h)"), ident_f[:]), r=PALL + ["ident_f"], w=["X0"])
                AC(lambda e, j=j, ri=ri: e.activation(out=WBT[:, 2 * j + ri, :], in_=X[0][:, 0:128], func=AF.Copy), x=["X0"], w=["WBT"])
                csrc = s5_c[:, ri, j, :].unsqueeze(1).broadcast_to([128, 8, 16])
                sgn = 1.0 if ri == 0 else -1.0
                DV(lambda e, csrc=csrc, mk=mk, j=j, ri=ri, sgn=sgn: e.scalar_tensor_tensor(
                    out=CT[:, 2 * j + ri, :].rearrange("p (g h) -> p g h", g=8), in0=csrc, scalar=sgn, in1=mk,
                    op0=ALU.mult, op1=ALU.mult), r=PALL + ["maskB"], w=["CT"])
        DV(lambda e: e.tensor_copy(out=Etab[:, 0, :, 0], in_=SL(I_C)), r=["s5w"], w=["Etab"])
        DV(lambda e: e.tensor_scalar(out=Etab[:, 1, :, 0], in0=SL(I_S), scalar1=-1.0, scalar2=None, op0=ALU.mult), r=["s5w"], w=["Etab"])
        n = 1
        while n < SB:
            ar = Etab[:, 0, :, 0:n]
            ai = Etab[:, 1, :, 0:n]
            br = Etab[:, 0, :, n - 1:n].broadcast_to([128, 8, n])
            bi = Etab[:, 1, :, n - 1:n].broadcast_to([128, 8, n])
            t_ = [Etmp[:, i, :, 0:n] for i in range(4)]
            for (o_, a_, b_) in ((t_[0], ar, br), (t_[1], ai, bi), (t_[2], ar, bi), (t_[3], ai, br)):
                DV(lambda e, o_=o_, a_=a_, b_=b_: e.tensor_tensor(out=o_, in0=a_, in1=b_, op=ALU.mult), r=["Etab", "aT"], w=["aT"])
            DV(lambda e, n=n, t_=t_: e.tensor_tensor(out=Etab[:, 0, :, n:2 * n], in0=t_[0], in1=t_[1], op=ALU.subtract), r=["aT"], w=["Etab"])
            DV(lambda e, n=n, t_=t_: e.tensor_tensor(out=Etab[:, 1, :, n:2 * n], in0=t_[2], in1=t_[3], op=ALU.add), r=["aT"], w=["Etab"])
            n *= 2

        class WStream:
            def __init__(self, bufs, names, plan):
                self.bufs, self.names, self.plan = bufs, names, plan
                self.issued, self.n, self.i = 0, len(bufs), 0

            def take(self, span=1):
                i = self.i
                while self.issued < min(len(self.plan), i + self.n):
                    b = self.issued % self.n
                    for (view_fn, src) in self.plan[self.issued]:
                        P.dma("pool", lambda e, d=view_fn(self.bufs[b]), s=src: e.dma_start(out=d, in_=s), writes=[self.names[b]])
                    self.issued += 1
                self.i += span
                return [(self.bufs[(i + s) % self.n], self.names[(i + s) % self.n]) for s in range(span)]

        v8 = lambda buf: buf[:].rearrange("p (k c) -> p k c", k=8)
        v16 = lambda buf: buf[:].rearrange("p (k c) -> p k c", k=16)

        def blk256(w_d, c0):
            return [(lambda buf: v8(buf), w_d[:, c0:c0 + 256].rearrange("(k p) c -> p k c", p=128))]

        planA = []
        for st_i in range(-1, NST):
            m_ = st_i < 0
            planA.append(blk256(w_in_d, 0))
            for hp in range(3):
                planA.append(blk256(w_in_d, 256 + 1536 + 256 * hp))
                if not m_:
                    planA.append(blk256(w_in_d, 256 + 2304 + 256 * hp))
                for hl in range(2):
                    h = 2 * hp + hl
                    planA.append([
                        (lambda buf: v8(buf)[:, :, 0:128], w_in_d[:, 256 + 128 * h:256 + 128 * h + 128].rearrange("(k p) c -> p k c", p=128)),
                        (lambda buf: v8(buf)[:, :, 128:256], w_in_d[:, 1024 + 128 * h:1024 + 128 * h + 128].rearrange("(k p) c -> p k c", p=128)),
                    ])
        planB = []
        for st_i in range(NST):
            for cb in range(4):
                planB.append(blk256(w_out_d, 256 * cb))
            for ub in range(16):
                planB.append(blk256(w_up_d, 256 * ub))
            for dm in range(8):
                for half in range(2):
                    planB.append([(lambda buf: v16(buf),
                                   w_down_d[2048 * half:2048 * half + 2048, 128 * dm:128 * dm + 128].rearrange("(k p) c -> p k c", p=128))])
        WA = WStream(wA, [f"wA{i}" for i in range(3)], planA)
        WB = WStream(wB, [f"wB{i}" for i in range(4)], planB)

        bA_i = [0]

        def next_bA():
            i = bA_i[0] % 2
            bA_i[0] += 1
            return i

        X_i = [0]

        def next_X():
            i = X_i[0] % 3
            X_i[0] += 1
            return i

        def mm8(bank, bname, wv, wn, c0, actT, aname, wt, resid=None, rname=None):
            def fn(e):
                ins = None
                for k in range(8):
                    ins = e.matmul(bank[:, 0:wt], lhsT=wv[:, k, c0:c0 + 128], rhs=actT[:, k, 0:wt],
                                   start=(k == 0), stop=(k == 7 and resid is None))
                if resid is not None:
                    ins = e.matmul(bank[:, 0:wt], lhsT=aI[:], rhs=resid[:, 0:wt], start=False, stop=True)
                return ins
            PE(fn, r=[wn, aname] + (["aI", rname] if resid is not None else []), w=[bname])

        def ln_tile(src_halves, src_reads, src_excl, slot, out_kind, gb=None, gbn=None, destT=None, dname=None, t=0,
                    out_rows=None, tb=None, tbn=None, xbh=None):
            st = stt[:, slot, :]
            m = mv[:, slot, :]
            for hh in range(2):
                DV(lambda e, hh=hh: e.bn_stats(out=st[:, 6 * hh:6 * hh + 6], in_=src_halves[hh]),
                   r=src_reads, w=[f"stt{slot}"], x=[src_excl[hh]] if src_excl else [])
            DV(lambda e: e.bn_aggr(out=m[:, 0:2], in_=st), r=[f"stt{slot}"], w=[f"mv{slot}"])
            AC(lambda e: e.activation(out=m[:, 2:3], in_=m[:, 1:2], func=AF.Sqrt, bias=epsb[:], scale=1.0), r=[f"mv{slot}", "epsb"], w=[f"mv{slot}"])
            DV(lambda e: e.reciprocal(out=m[:, 3:4], in_=m[:, 2:3]), r=[f"mv{slot}"], w=[f"mv{slot}"])
            DV(lambda e: e.scalar_tensor_tensor(out=m[:, 4:5], in0=m[:, 0:1], scalar=-1.0, in1=m[:, 3:4], op0=ALU.mult, op1=ALU.mult), r=[f"mv{slot}"], w=[f"mv{slot}"])
            if out_kind == "T":
                if xbh is None:
                    xbh = [(xnb[slot % 2][:, 0:512], f"xnb{slot % 2}"), (xnb[slot % 2][:, 512:1024], f"xnb{slot % 2}")]
                for hh in range(2):
                    AC(lambda e, hh=hh: e.activation(out=xbh[hh][0], in_=src_halves[hh], func=AF.Identity,
                                                     scale=m[:, 3:4], bias=m[:, 4:5]),
                       r=src_reads + [f"mv{slot}"], w=[xbh[hh][1]], x=[src_excl[hh]] if src_excl else [])
                for _ in range(LAG):
                    yield

                def trf(e):
                    ins = None
                    for k in range(8):
                        ins = e.transpose(tb[:, 128 * k:128 * k + 128], xbh[k // 4][0][:, 128 * (k % 4):128 * (k % 4) + 128], ident_b[:])
                    return ins
                PE(trf, r=[xbh[0][1], xbh[1][1], "ident_b"], w=[tbn])
                for k in range(8):
                    AC(lambda e, k=k: e.activation(out=destT[:, k, 128 * t:128 * t + 128], in_=tb[:, 128 * k:128 * k + 128], func=AF.Identity,
                                                   scale=gb[:, k:k + 1], bias=gb[:, 8 + k:9 + k]),
                       r=[gbn], w=[dname], x=[tbn])
            else:
                xo = xs[slot % 2]
                xon = f"xs{slot % 2}"
                for hh in range(2):
                    AC(lambda e, hh=hh: e.activation(out=xo[:, 512 * hh:512 * hh + 512], in_=src_halves[hh], func=AF.Identity,
                                                     scale=m[:, 3:4], bias=m[:, 4:5]),
                       r=src_reads + [f"mv{slot}"], w=[xon], x=[src_excl[hh]] if src_excl else [])
                DV(lambda e: e.tensor_tensor(out=xo[:], in0=xo[:], in1=ln2g[:], op=ALU.mult), r=[xon, "ln2g"], w=[xon])
                DV(lambda e: e.tensor_tensor(out=xo[:], in0=xo[:], in1=ln2b[:], op=ALU.add), r=[xon, "ln2b"], w=[xon])
                out_dma_ops.append(P.dma("sp", lambda e: e.dma_start(out=out_d[out_rows[0]:out_rows[1], :], in_=xo[:]), reads=[xon]))

        def rT_to_tokmajor(t):
            def fn(e):
                ins = None
                for dm in range(8):
                    ins = e.transpose(bB0h[:, 128 * dm:128 * dm + 128], rT[:, dm, 128 * t:128 * t + 128], ident_b[:])
                return ins
            PE(fn, r=["rT", "ident_b"], w=["bB0"])

        out_dma_ops = []
        bT_all = bT[:]

        def s5_block(U, c0, meta, xi_):
            Er = Etab[:, 0, 4 * U:4 * U + 4, :].rearrange("p j t -> p (j t)")
            Ei = Etab[:, 1, 4 * U:4 * U + 4, :].rearrange("p j t -> p (j t)")
            Xre, Xim, XY = X[0], X[1], X[2]

            def mmbu(e):
                ins = None
                for j4 in range(4):
                    j = 4 * U + j4
                    e.matmul(Xre[:, 128 * j4:128 * j4 + 128], lhsT=WBT[:, 2 * j, :], rhs=uT[:, U, c0:c0 + 128], start=True, stop=True)
                    ins = e.matmul(Xim[:, 128 * j4:128 * j4 + 128], lhsT=WBT[:, 2 * j + 1, :], rhs=uT[:, U, c0:c0 + 128], start=True, stop=True)
                return ins
            PE(mmbu, r=["WBT", "uT"], w=["X0", "X1"])
            for (o_, a_, b_, xb_) in ((0, Xre, Er, "X0"), (1, Xim, Ei, "X1"), (2, Xim, Er, "X1"), (3, Xre, Ei, "X0")):
                DV(lambda e, o_=o_, a_=a_, b_=b_: e.tensor_tensor(out=s5m[:, o_, :], in0=a_[:], in1=b_, op=ALU.mult),
                   r=["Etab"], w=[f"s5m{o_}"], x=[xb_])
            DV(lambda e: e.tensor_tensor(out=s5m[:, 0, :], in0=s5m[:, 0, :], in1=s5m[:, 1, :], op=ALU.subtract), r=["s5m0", "s5m1"], w=["s5m0"])
            DV(lambda e: e.tensor_tensor(out=s5m[:, 2, :], in0=s5m[:, 2, :], in1=s5m[:, 3, :], op=ALU.add), r=["s5m2", "s5m3"], w=["s5m2"])
            yield
            for j4 in range(4):
                j = 4 * U + j4
                bs = slice(128 * j4, 128 * j4 + 128)
                for (ri, si, so) in ((0, 0, 1), (1, 2, 3)):
                    DV(lambda e, ri=ri, si=si, so=so, j=j, bs=bs: e.tensor_tensor_scan(
                        out=s5m[:, so, bs], data0=s5w[:, I_RHO, j:j + 1].broadcast_to([128, 128]), data1=s5m[:, si, bs],
                        initial=Xc[:, ri, j:j + 1], op0=ALU.mult, op1=ALU.add),
                       r=["s5w", f"s5m{si}", "Xc"], w=[f"s5m{so}"])
            for (o_, zi_, e_) in ((0, 1, Er), (1, 3, Ei), (2, 3, Er), (3, 1, Ei)):
                DV(lambda e, o_=o_, zi_=zi_, e_=e_: e.tensor_tensor(out=s5p[:, o_, :], in0=s5m[:, zi_, :], in1=e_, op=ALU.mult),
                   r=["Etab", f"s5m{zi_}"], w=[f"s5p{o_}"])
            pv = lambda o_: s5p[:, o_, :].rearrange("p (j t) -> p j t", j=4)[:, :, 127]
            DV(lambda e: e.tensor_tensor(out=Xc[:, 0, 4 * U:4 * U + 4], in0=pv(0), in1=pv(1), op=ALU.add), r=["s5p0", "s5p1"], w=["Xc"])
            DV(lambda e: e.tensor_tensor(out=Xc[:, 1, 4 * U:4 * U + 4], in0=pv(2), in1=pv(3), op=ALU.subtract), r=["s5p2", "s5p3"], w=["Xc"])
            if meta:
                return
            xx = s5x[xi_]
            xn = f"s5x{xi_}"
            DV(lambda e: e.tensor_tensor(out=xx[:, 0, :], in0=s5p[:, 0, :], in1=s5p[:, 1, :], op=ALU.add), r=["s5p0", "s5p1"], w=[xn])
            DV(lambda e: e.tensor_tensor(out=xx[:, 1, :], in0=s5p[:, 2, :], in1=s5p[:, 3, :], op=ALU.subtract), r=["s5p2", "s5p3"], w=[xn])
            for _ in range(LAG):
                yield

            def mmy(e):
                ins = None
                for j4 in range(4):
                    j = 4 * U + j4
                    e.matmul(XY[:, 0:128], lhsT=CT[:, 2 * j, :], rhs=xx[:, 0, 128 * j4:128 * j4 + 128], start=(j4 == 0), stop=False)
                    ins = e.matmul(XY[:, 0:128], lhsT=CT[:, 2 * j + 1, :], rhs=xx[:, 1, 128 * j4:128 * j4 + 128], start=False, stop=(j4 == 3))
                return ins
            PE(mmy, r=["CT", xn], w=["X2"])
            DV(lambda e: e.scalar_tensor_tensor(out=ypre[:, c0:c0 + 128], in0=uT[:, U, c0:c0 + 128], scalar=s5_fm[:, U:U + 1],
                                                in1=XY[:, 0:128], op0=ALU.mult, op1=ALU.add),
               r=["uT", "s5_fm"], w=["ypre"], x=["X2"])
            yield

        def head(hp, hl, meta, nt, wt, hT, hTn):
            h = 2 * hp + hl
            (wb_, wn_), = WA.take()
            wv_ = v8(wb_)
            todo = [("k", 128, k_pre, kT)] + ([] if meta else [("q", 0, q_pre, qT)])
            for (nm, coff, pre, dst) in todo:
                xi = next_X()
                mm8(X[xi], f"X{xi}", wv_, wn_, coff, hT, hTn, wt)
                AC(lambda e, pre=pre, xi=xi: e.activation(out=pre[:, 0:wt], in_=X[xi][:, 0:wt], func=AF.Copy), x=[f"X{xi}"], w=[nm + "_pre"])
                xj = next_X()
                PE(lambda e, pre=pre, xj=xj: e.matmul(X[xj][:, 0:wt], lhsT=pswap[:], rhs=pre[:, 0:wt], start=True, stop=True),
                   r=[nm + "_pre", "pswap"], w=[f"X{xj}"])
                DV(lambda e, pre=pre: e.tensor_tensor(out=rope_t1[:, 0:wt], in0=pre[:, 0:wt], in1=cos_sb[:, 0:wt], op=ALU.mult),
                   r=[nm + "_pre", "cos_sb"], w=["s5p0"])
                DV(lambda e, xj=xj: e.tensor_tensor(out=rope_t2[:, 0:wt], in0=X[xj][:, 0:wt], in1=sin_sb[:, 0:wt], op=ALU.mult),
                   r=["sin_sb"], w=["s5p1"], x=[f"X{xj}"])
                DV(lambda e, dst=dst: e.tensor_tensor(out=dst[:, 0:wt], in0=rope_t1[:, 0:wt], in1=rope_t2[:, 0:wt], op=ALU.add),
                   r=["s5p0", "s5p1"], w=[nm + "T"])
                yield
            if not meta:
                DV(lambda e: e.tensor_tensor(out=qxT[:, 0:wt].rearrange("p (t i) -> p t i", t=nt), in0=qT[:, 0:wt].rearrange("p (t i) -> p t i", t=nt),
                                             in1=xibc[:, h, :].unsqueeze(1).broadcast_to([128, nt, 128]), op=ALU.mult),
                   r=["qT", "xibc"], w=["qxT"])

                def p1(e):
                    ins = None
                    for t in range(nt):
                        ins = e.matmul(X[0][:, 128 * t:128 * t + 128], lhsT=kT[:, 128 * t:128 * t + 128], rhs=qT[:, 128 * t:128 * t + 128], start=True, stop=True)
                    return ins
                PE(p1, r=["kT", "qT"], w=["X0"])
                DV(lambda e: e.tensor_tensor(out=scT_sb[:, 0:nt, :], in0=X[0][:, 0:wt].rearrange("p (t i) -> p t i", t=nt),
                                             in1=dmatT[:, h, :].unsqueeze(1).broadcast_to([128, nt, 128]), op=ALU.mult),
                   r=["dmatT"], w=["scT_sb"], x=["X0"])

            def p1b(e):
                ins = None
                for t in range(nt):
                    ins = e.transpose(bT_all[:, 128 * t:128 * t + 128], kT[:, 128 * t:128 * t + 128], ident_b[:])
                return ins
            PE(p1b, r=["kT", "ident_b"], w=["bT"])
            AC(lambda e: e.activation(out=kz_sb[:, 0:nt, :].rearrange("p t d -> p (t d)"), in_=bT_all[:, 0:wt], func=AF.Identity, scale=zeta[:, h:h + 1]),
               r=["zeta"], w=["kz_sb"], x=["bT"])
            for _ in range(LAG):
                yield

            def p2(e):
                ins = None
                for t in range(nt):
                    ins = e.matmul(X[1][:, 128 * t:128 * t + 128], lhsT=kz_sb[:, t, :], rhs=v_sb[:, t, 128 * h:128 * h + 128], start=True, stop=True)
                return ins
            PE(p2, r=["kz_sb", "v_sb"], w=["X1"])
            if not meta:
                AC(lambda e: e.activation(out=Sbf_all[:, 0, :], in_=S32[:, h, :], func=AF.Copy), r=[f"S32{h}"], w=["Sbf_all"])
            prev, prevn = S32[:, h, :], f"S32{h}"
            for t in range(nt):
                last = (t == nt - 1)
                dst_, dstn = (S32[:, h, :], f"S32{h}") if last else (Sall[:, t, :], "Sall")
                DV(lambda e, prev=prev, dst_=dst_, t=t: e.scalar_tensor_tensor(out=dst_, in0=prev, scalar=CONST["gammaC"][h],
                                                                                in1=X[1][:, 128 * t:128 * t + 128], op0=ALU.mult, op1=ALU.add),
                   r=[prevn], w=[dstn], x=["X1"])
                prev, prevn = dst_, dstn
            if meta:
                return
            AC(lambda e: e.activation(out=Sbf_all[:, 1:nt, :], in_=Sall[:, 0:nt - 1, :], func=AF.Copy), r=["Sall"], w=["Sbf_all"])
            for _ in range(LAG):
                yield

            def p3(e):
                ins = None
                for t in range(nt):
                    e.matmul(X[2][:, 128 * t:128 * t + 128], lhsT=scT_sb[:, t, :], rhs=v_sb[:, t, 128 * h:128 * h + 128], start=True, stop=False)
                    ins = e.matmul(X[2][:, 128 * t:128 * t + 128], lhsT=qxT[:, 128 * t:128 * t + 128], rhs=Sbf_all[:, t, :], start=False, stop=True)
                return ins
            PE(p3, r=["scT_sb", "v_sb", "qxT", "Sbf_all"], w=["X2"])
            for t in range(nt):
                DV(lambda e, t=t: e.bn_stats(out=gst[:, t, :], in_=X[2][:, 128 * t:128 * t + 128]), w=["gst"], x=["X2"])
            for t in range(nt):
                DV(lambda e, t=t: e.bn_aggr(out=gmv[:, t, :], in_=gst[:, t, :]), r=["gst"], w=["gmv"])
            AC(lambda e: e.activation(out=gsc[:, 0, 0:nt], in_=gmv[:, 0:nt, 1], func=AF.Sqrt, bias=epsb[:], scale=1.0), r=["gmv", "epsb"], w=["gsc"])
            DV(lambda e: e.reciprocal(out=gsc[:, 1, 0:nt], in_=gsc[:, 0, 0:nt]), r=["gsc"], w=["gsc"])
            DV(lambda e: e.scalar_tensor_tensor(out=gsc[:, 2, 0:nt], in0=gmv[:, 0:nt, 0], scalar=-1.0, in1=gsc[:, 1, 0:nt], op0=ALU.mult, op1=ALU.mult),
               r=["gmv", "gsc"], w=["gsc"])
            for t in range(nt):
                AC(lambda e, t=t: e.activation(out=on_sb[:, t, :], in_=X[2][:, 128 * t:128 * t + 128], func=AF.Identity,
                                               scale=gsc[:, 1, t:t + 1], bias=gsc[:, 2, t:t + 1]),
                   r=["gsc"], w=["on_sb"], x=["X2"])
            for _ in range(LAG):
                yield

            def p4(e):
                ins = None
                for t in range(nt):
                    ins = e.transpose(bT_all[:, 512 + 128 * t:512 + 128 * t + 128], on_sb[:, t, :], ident_b[:])
                return ins
            PE(p4, r=["on_sb", "ident_b"], w=["bT"])
            AC(lambda e: e.activation(out=gaff[:, 0:wt], in_=bT_all[:, 512:512 + wt], func=AF.Identity, scale=gn_gb[:, h:h + 1], bias=gn_gb[:, 6 + h:7 + h]),
               r=["gn_gb"], w=["gaff"], x=["bT"])
            DV(lambda e: e.tensor_tensor(out=yT[:, 2 + h, 0:wt], in0=gaff[:, 0:wt], in1=sgT[:, hl, 0:wt], op=ALU.mult), r=["gaff", "sgT"], w=["yT"])
            yield

        def gen_Apre(st_idx):
            meta = st_idx < 0
            nt = 1 if meta else NT
            hT, hTn = hTs[(st_idx + 1) % 2], f"hT{(st_idx + 1) % 2}"
            for t in range(nt):
                xb_ = xs[t % 2]
                xn_ = f"xs{t % 2}"
                if meta:
                    DV(lambda e, xb_=xb_: e.memset(xb_[:], 0.0), w=[xn_])
                    P.dma("sp", lambda e, xb_=xb_: e.dma_start(out=xb_[112:128, :], in_=meta_d), writes=[xn_])
                else:
                    r0 = st_idx * W + t * 128
                    P.dma("sp", lambda e, r0=r0, xb_=xb_: e.dma_start(out=xb_[:], in_=x_d[r0:r0 + 128, :]), writes=[xn_])
                yield from ln_tile([xb_[:, 0:512], xb_[:, 512:1024]], [xn_], None, t, "T", gb=lnin_gb, gbn="lnin_gb", destT=hT, dname=hTn, t=t,
                                   tb=bT_all, tbn="bT")
                yield
            if meta:
                DV(lambda e: e.memset(hT[:, :, 0:112], 0.0), r=[hTn], w=[hTn])

        def gen_Amain(st_idx):
            meta = st_idx < 0
            nt = 1 if meta else NT
            wt = nt * 128
            slot0 = 0 if meta else 128 + st_idx * W
            hT, hTn = hTs[(st_idx + 1) % 2], f"hT{(st_idx + 1) % 2}"
            P.dma("sp", lambda e: e.dma_start(out=cos_sb[:, 0:wt], in_=cosT_d[:, slot0:slot0 + wt]), writes=["cos_sb"])
            P.dma("sp", lambda e: e.dma_start(out=sin_sb[:, 0:wt], in_=sinT_d[:, slot0:slot0 + wt]), writes=["sin_sb"])
            (wb_, wn_), = WA.take()
            for U in range(2):
                xi = next_X()
                mm8(X[xi], f"X{xi}", v8(wb_), wn_, 128 * U, hT, hTn, wt)
                AC(lambda e, U=U, xi=xi: e.activation(out=uT[:, U, 0:wt], in_=X[xi][:, 0:wt], func=AF.Copy), x=[f"X{xi}"], w=["uT"])
            yield
            def gen_s5():
                cnt = 0
                for U in range(2):
                    for sbi in range(wt // SB):
                        yield from s5_block(U, sbi * SB, meta, cnt % 2)
                        cnt += 1
                    if not meta:
                        AC(lambda e, U=U: e.activation(out=ygb[:, U, 0:wt], in_=ypre[:, 0:wt], func=AF.Gelu_apprx_tanh), r=["ypre"], w=["ygb"])
                    yield
                if not meta:
                    for U2 in range(2):
                        xi = next_X()

                        def mmg(e, U2=U2, xi=xi):
                            e.matmul(X[xi][:, 0:wt], lhsT=wglu[:, 0, 128 * U2:128 * U2 + 128], rhs=ygb[:, 0, 0:wt], start=True, stop=False)
                            return e.matmul(X[xi][:, 0:wt], lhsT=wglu[:, 1, 128 * U2:128 * U2 + 128], rhs=ygb[:, 1, 0:wt], start=False, stop=True)
                        PE(mmg, r=["wglu", "ygb"], w=[f"X{xi}"])
                        AC(lambda e, U2=U2, xi=xi: e.activation(out=sgl[:, 0:wt], in_=X[xi][:, 0:wt], func=AF.Sigmoid, bias=s5_fm[:, 2 + U2:3 + U2], scale=1.0),
                           r=["s5_fm"], w=["sgl"], x=[f"X{xi}"])
                        DV(lambda e, U2=U2: e.tensor_tensor(out=yT[:, U2, 0:wt], in0=ygb[:, U2, 0:wt], in1=sgl[:, 0:wt], op=ALU.mult), r=["ygb", "sgl"], w=["yT"])
                    yield

            def gen_ret():
                for hp in range(3):
                    (wb_, wn_), = WA.take()
                    for tp in range((nt + 1) // 2):
                        xi = next_X()
                        tl = [t for t in (2 * tp, 2 * tp + 1) if t < nt]

                        def mmv(e, tl=tl, xi=xi, wb_=wb_):
                            ins = None
                            for ii, t in enumerate(tl):
                                for k in range(8):
                                    ins = e.matmul(X[xi][:, 256 * ii:256 * ii + 256], lhsT=hT[:, k, 128 * t:128 * t + 128], rhs=v8(wb_)[:, k, 0:256],
                                                   start=(k == 0), stop=(k == 7))
                            return ins
                        PE(mmv, r=[wn_, hTn], w=[f"X{xi}"])
                        AC(lambda e, tl=tl, xi=xi, hp=hp: e.activation(
                            out=v_sb[:, tl[0]:tl[0] + len(tl), 256 * hp:256 * hp + 256],
                            in_=X[xi][:, 0:256 * len(tl)].rearrange("p (t c) -> p t c", t=len(tl)), func=AF.Copy),
                           x=[f"X{xi}"], w=["v_sb"])
                    yield
                    if not meta:
                        (wb_, wn_), = WA.take()
                        for hl in range(2):
                            xi = next_X()
                            mm8(X[xi], f"X{xi}", v8(wb_), wn_, 128 * hl, hT, hTn, wt)
                            AC(lambda e, hl=hl, xi=xi: e.activation(out=sgT[:, hl, 0:wt], in_=X[xi][:, 0:wt], func=AF.Silu), x=[f"X{xi}"], w=["sgT"])
                        yield
                    for hl in range(2):
                        yield from head(hp, hl, meta, nt, wt, hT, hTn)

            g1, g2 = gen_ret(), gen_s5()
            live = [g1, g2]
            while live:
                for g in list(live):
                    try:
                        next(g)
                        yield
                    except StopIteration:
                        live.remove(g)

        def gen_E(st_idx):
            hT, hTn = hTs[(st_idx + 1) % 2], f"hT{(st_idx + 1) % 2}"
            for cb in range(4):
                (wb_, wn_), = WB.take()
                for hl in range(2):
                    dm = 2 * cb + hl
                    bi_ = next_bA()
                    mm8(bA[bi_], f"bA{bi_}", v8(wb_), wn_, 128 * hl, yT, "yT", W, resid=hT[:, dm, :], rname=hTn)
                    AC(lambda e, dm=dm, bi_=bi_: e.activation(out=rT[:, dm, :], in_=bA[bi_][:, 0:W], func=AF.Copy), x=[f"bA{bi_}"], w=["rT"])
                yield

        def gen_B(st_idx):
            for t in range(NT):
                rT_to_tokmajor(t)
                yield from ln_tile([bB0h[:, 0:512], bB0h[:, 512:1024]], [], ["bB0", "bB0"], 4 + t, "T", gb=ln1_gb, gbn="ln1_gb", destT=h1T, dname="h1T", t=t,
                                   tb=bB1h, tbn="bB1", xbh=[(relu_t[0][:], "relu_t0"), (relu_t[1][:], "relu_t1")])
                yield
            for ub in range(16):
                (wb_, wn_), = WB.take()
                for hl in range(2):
                    ff = 2 * ub + hl
                    bi_ = next_bA()
                    mm8(bA[bi_], f"bA{bi_}", v8(wb_), wn_, 128 * hl, h1T, "h1T", W)
                    rt = relu_t[ff % 2]
                    AC(lambda e, rt=rt, bi_=bi_: e.activation(out=rt[:], in_=bA[bi_][:, 0:W], func=AF.Relu), x=[f"bA{bi_}"], w=[f"relu_t{ff % 2}"])
                    AC(lambda e, rt=rt, ff=ff: e.activation(out=aT[:, ff, :], in_=rt[:], func=AF.Square), r=[f"relu_t{ff % 2}"], w=["aT"])
                    yield
            for dm in range(8):
                (w0, n0), (w1, n1) = WB.take(2)
                bi_ = next_bA()

                for q4 in range(4):
                    def fnd(e, w0=w0, w1=w1, dm=dm, bi_=bi_, q4=q4):
                        ins = None
                        for k in range(8 * q4, 8 * q4 + 8):
                            wv_ = v16(w0) if k < 16 else v16(w1)
                            ins = e.matmul(bA[bi_][:, 0:W], lhsT=wv_[:, k % 16, :], rhs=aT[:, k, :], start=(k == 0), stop=False)
                        if q4 == 3:
                            ins = e.matmul(bA[bi_][:, 0:W], lhsT=aI[:], rhs=h1T[:, dm, :], start=False, stop=True)
                        return ins
                    PE(fnd, r=[n0, n1, "aT", "h1T", "aI"], w=[f"bA{bi_}"])
                    if q4 < 3:
                        yield
                AC(lambda e, dm=dm, bi_=bi_: e.activation(out=rT[:, dm, :], in_=bA[bi_][:, 0:W], func=AF.Copy), x=[f"bA{bi_}"], w=["rT"])
                yield
            for t in range(NT):
                rT_to_tokmajor(t)
                r0 = st_idx * W + t * 128
                yield from ln_tile([bB0h[:, 0:512], bB0h[:, 512:1024]], [], ["bB0", "bB0"], 4 + t, "O", out_rows=(r0, r0 + 128))
                yield

        def run(g):
            for _ in g:
                pass

        def interleave(gb, ga, a_per_b):
            done_a = done_b = False
            acc = 0.0
            na = nb = 0
            while not (done_a and done_b):
                if not done_b:
                    try:
                        next(gb)
                        nb += 1
                    except StopIteration:
                        done_b = True
                acc += a_per_b
                while (acc >= 1.0 or done_b) and not done_a:
                    acc -= 1.0
                    try:
                        next(ga)
                        na += 1
                    except StopIteration:
                        done_a = True
                if done_a:
                    acc = 0.0
            return na, nb

        def chain(*gs):
            for g in gs:
                if g is not None:
                    yield from g

        def spread(gmain, gextra, every):
            n = 0
            extra_live = gextra is not None
            for _ in gmain:
                yield
                n += 1
                if extra_live and n % every == 0:
                    try:
                        next(gextra)
                        yield
                    except StopIteration:
                        extra_live = False
            if extra_live:
                for _ in gextra:
                    yield

        ratio = [A_PER_B]
        run(gen_Apre(-1))
        run(spread(gen_Amain(-1), gen_Apre(0), 3))
        run(spread(gen_Amain(0), gen_Apre(1) if NST > 1 else None, PRE_EVERY))
        for s in range(NST):
            run(gen_E(s))
            if s + 1 < NST:
                a_stream = spread(gen_Amain(s + 1), gen_Apre(s + 2) if s + 2 < NST else None, PRE_EVERY)
                if INTERLEAVE:
                    na_, nb_ = interleave(gen_B(s), a_stream, ratio[0])
                    ratio[0] = na_ / max(nb_, 1)
                else:
                    run(gen_B(s))
                    run(a_stream)
            else:
                run(gen_B(s))
        P.emit(final_wait_ops=out_dma_ops)
    return nc


_NC_CACHE = {}


def _prep_inputs(inp):
    f = lambda a: np.ascontiguousarray(np.asarray(a, dtype=np.float32))
    pk = lambda v: np.ascontiguousarray(f(v).reshape(-1, 128).T)
    shared = {}
    shared["meta"] = f(inp["meta_tokens"])
    shared["w_in"] = f(inp["w_in"][0])
    shared["w_out"] = f(inp["w_out"][0])
    shared["w_up"] = f(inp["w_up"][0])
    shared["w_down"] = f(inp["w_down"][0])
    shared["w_glu"] = f(inp["s5_w_glu"][0])
    shared["lnin_gb"] = np.ascontiguousarray(np.concatenate([pk(inp["ln_in_g"]), pk(inp["ln_in_b"])], axis=1))
    shared["ln1_gb"] = np.ascontiguousarray(np.concatenate([pk(inp["ln1_g"][0]), pk(inp["ln1_b"][0])], axis=1))
    shared["ln2_g"] = f(inp["ln2_g"][0]).reshape(1, D)
    shared["ln2_b"] = f(inp["ln2_b"][0]).reshape(1, D)
    st = lambda a: np.ascontiguousarray(f(a).reshape(8, 128).T)
    ldt = np.repeat(f(inp["s5_log_dt"][0])[:, None], 64, axis=1)
    shared["s5_sc"] = np.ascontiguousarray(np.concatenate([st(inp["s5_lambda_re"][0]), st(inp["s5_lambda_im"][0]), st(ldt)], axis=1))

    def bl(a):
        return f(a).reshape(8, 2, 64, 16).transpose(1, 2, 0, 3).reshape(128, 8, 16)
    shared["s5_b"] = np.ascontiguousarray(np.stack([bl(inp["s5_b_re"][0]), bl(inp["s5_b_im"][0])], axis=1))
    cl = lambda a: bl(f(a).transpose(0, 2, 1))
    shared["s5_c"] = np.ascontiguousarray(np.stack([cl(inp["s5_c_re"][0]), cl(inp["s5_c_im"][0])], axis=1))
    shared["s5_fm"] = np.ascontiguousarray(np.concatenate([pk(inp["s5_d"][0]), pk(inp["s5_b_glu"][0])], axis=1))
    shared["gn_gb"] = np.ascontiguousarray(np.concatenate([pk(inp["ret_gn_g"][0]), pk(inp["ret_gn_b"][0])], axis=1))
    for k in ("ident_f", "pswap", "cosT", "sinT", "dmatT", "xi_bc", "zeta", "maskB"):
        shared[k] = CONST[k]
    x = f(inp["x"])
    maps = []
    for c in range(8):
        m = dict(shared)
        m["x"] = x[c]
        maps.append(m)
    return maps


def kernel(**inputs):
    if "nc" not in _NC_CACHE:
        _NC_CACHE["nc"] = build()
    nc = _NC_CACHE["nc"]
    maps = _prep_inputs(inputs)
    res = run_bass_kernel_spmd(nc, maps, core_ids=list(range(8)))
    out = np.stack([np.asarray(r["out"], dtype=np.float32) for r in res.results], axis=0)
    return out
```
